# Optimizing a Trainium2 kernel written in Bass

```python
import math
import jax
import jax.numpy as jnp
from jax import lax
import numpy as np

D_MODEL = 1024
BATCH = 1
SEQ = 16384
DEPTH = 2
DEC_BATCH = 128
DEC_SEQ = 4
PAST_LEN = 16384
PAGE_SIZE = 128

HEAD_DIM = 64
ROT_DIM = HEAD_DIM // 4
ROPE_THETA = 500000.0
NORM_EPS = 1e-6
A_Q_HEADS = 4
A_KV_HEADS = 2
A_WINDOW = 128
C_Q_HEADS = 4
C_KV_HEADS = 2
C_PATTERNS = ((128, 1), (512, 4), (2048, 16))
C_MAX_SPAN = 2048
B_HEADS = 8
B_HEADDIM = 64
B_INNER = B_HEADS * B_HEADDIM
B_GROUPS = 2
B_STATE = 128
CONV_K = 4
CONV_DIM = B_INNER + 2 * B_GROUPS * B_STATE
SSD_CHUNK = 128
A_WIDTH = A_Q_HEADS * HEAD_DIM
A_KV_WIDTH = A_KV_HEADS * HEAD_DIM
C_WIDTH = C_Q_HEADS * HEAD_DIM
C_KV_WIDTH = C_KV_HEADS * HEAD_DIM
MIX_WIDTH = A_WIDTH + B_INNER + C_WIDTH
IN_SIZES = (A_WIDTH, A_KV_WIDTH, A_KV_WIDTH, C_WIDTH, C_KV_WIDTH, C_KV_WIDTH, B_INNER, CONV_DIM, B_HEADS)
N_IN = sum(IN_SIZES)
D_FF = -(-(8 * D_MODEL) // (3 * 256)) * 256
BAND_BLOCK = 128

kernel_name = 'hymba_swa_ssd_dilated_decoder'


def _rmsnorm(x, g):
    xf = x.astype(jnp.float32)
    y = xf * lax.rsqrt(jnp.mean(xf * xf, axis=-1, keepdims=True) + NORM_EPS)
    return (y * g.astype(jnp.float32)).astype(x.dtype)


def _rope(x, pos):
    half = ROT_DIM // 2
    inv = ROPE_THETA ** (-(jnp.arange(half, dtype=jnp.float32) * 2.0 / ROT_DIM))
    ang = pos.astype(jnp.float32)[:, None] * inv[None, :]
    cos = jnp.cos(ang)[None, :, None, :]
    sin = jnp.sin(ang)[None, :, None, :]
    xf = x.astype(jnp.float32)
    x1, x2 = xf[..., :half], xf[..., half:ROT_DIM]
    return jnp.concatenate([x1 * cos - x2 * sin, x2 * cos + x1 * sin, xf[..., ROT_DIM:]], axis=-1).astype(x.dtype)


def _masked_softmax(s, valid, sinks=None):
    s = jnp.where(valid, s, -jnp.inf)
    m = jnp.max(s, axis=-1, keepdims=True)
    if sinks is not None:
        m = jnp.maximum(m, sinks)
    e = jnp.exp(s - m)
    den = jnp.sum(e, axis=-1, keepdims=True)
    if sinks is not None:
        den = den + jnp.exp(sinks - m)
    return e / den, (m + jnp.log(den))[..., 0]


def _banded_attention(q, k, v, max_dist, sinks=None):
    b, L, hq, hd = q.shape
    hkv = k.shape[2]
    rep = hq // hkv
    blk = BAND_BLOCK
    nb = -(-L // blk)
    pad = nb * blk - L
    qp = jnp.pad(q, ((0, 0), (0, pad), (0, 0), (0, 0))).reshape(b, nb, blk, hkv, rep, hd)
    kp = jnp.pad(k, ((0, 0), (blk, pad), (0, 0), (0, 0))).reshape(b, nb + 1, blk, hkv, hd)
    vp = jnp.pad(v, ((0, 0), (blk, pad), (0, 0), (0, 0))).reshape(b, nb + 1, blk, hkv, hd)
    kw = jnp.concatenate([kp[:, :-1], kp[:, 1:]], axis=2)
    vw = jnp.concatenate([vp[:, :-1], vp[:, 1:]], axis=2)
    s = jnp.einsum('bnqgrd,bnkgd->bngrqk', qp, kw, preferred_element_type=jnp.float32) * (hd ** -0.5)
    qi = jnp.arange(blk)[:, None] + blk
    ki = jnp.arange(2 * blk)[None, :]
    dist = qi - ki
    kpos = jnp.arange(nb)[:, None, None] * blk + ki[None] - blk
    valid = (dist >= 0) & (dist <= max_dist) & (kpos >= 0)
    sk = None if sinks is None else sinks.astype(jnp.float32).reshape(hkv, rep, 1, 1)
    p, lse = _masked_softmax(s, valid[:, None, None], sk)
    o = jnp.einsum('bngrqk,bnkgd->bnqgrd', p.astype(v.dtype), vw)
    o = o.reshape(b, nb * blk, hq, hd)[:, :L]
    lse = lse.transpose(0, 1, 4, 2, 3).reshape(b, nb * blk, hq)[:, :L]
    return o, lse


def _merge_by_denominator(outs, lses):
    w = jax.nn.softmax(jnp.stack(lses, axis=0), axis=0)
    return jnp.einsum('pblh,pblhd->blhd', w.astype(outs[0].dtype), jnp.stack(outs, axis=0))


def _dilated_prompt(q, k, v):
    b, L, hq, hd = q.shape
    outs, lses = [], []
    for w, d in C_PATTERNS:
        def fold(t):
            h = t.shape[2]
            return t.reshape(b, L // d, d, h, hd).transpose(0, 2, 1, 3, 4).reshape(b * d, L // d, h, hd)
        o, lse = _banded_attention(fold(q), fold(k), fold(v), w // d)
        outs.append(o.reshape(b, d, L // d, hq, hd).transpose(0, 2, 1, 3, 4).reshape(b, L, hq, hd))
        lses.append(lse.reshape(b, d, L // d, hq).transpose(0, 2, 1, 3).reshape(b, L, hq))
    return _merge_by_denominator(outs, lses)


def _dilated_sample(q, kcat, vcat):
    b, T, hq, hd = q.shape
    hkv = kcat.shape[2]
    rep = hq // hkv
    buf = kcat.shape[1] - T
    qg = q.reshape(b, T, hkv, rep, hd)
    t = jnp.arange(T)
    outs, lses = [], []
    for w, d in C_PATTERNS:
        idx = buf + t[:, None] - d * jnp.arange(w // d + 1)[None, :]
        valid = idx >= 0
        idx = jnp.maximum(idx, 0)
        kg = kcat[:, idx]
        vg = vcat[:, idx]
        s = jnp.einsum('btgrd,btkgd->bgrtk', qg, kg, preferred_element_type=jnp.float32) * (hd ** -0.5)
        p, lse = _masked_softmax(s, valid, None)
        o = jnp.einsum('bgrtk,btkgd->btgrd', p.astype(vg.dtype), vg)
        outs.append(o.reshape(b, T, hq, hd))
        lses.append(lse.transpose(0, 3, 1, 2).reshape(b, T, hq))
    return _merge_by_denominator(outs, lses)


def _window_sample(q, kcat, vcat, sinks):
    b, T, hq, hd = q.shape
    hkv = kcat.shape[2]
    rep = hq // hkv
    buf = kcat.shape[1] - T
    qg = q.reshape(b, T, hkv, rep, hd)
    s = jnp.einsum('btgrd,bkgd->bgrtk', qg, kcat, preferred_element_type=jnp.float32) * (hd ** -0.5)
    dist = (buf + jnp.arange(T))[:, None] - jnp.arange(buf + T)[None, :]
    valid = (dist >= 0) & (dist < A_WINDOW)
    p, _ = _masked_softmax(s, valid, sinks.astype(jnp.float32).reshape(hkv, rep, 1, 1))
    o = jnp.einsum('bgrtk,bkgd->btgrd', p.astype(vcat.dtype), vcat)
    return o.reshape(b, T, hq, hd)


def _causal_conv(xbc, prefix, w, bias):
    L = xbc.shape[1]
    xp = jnp.concatenate([prefix.astype(xbc.dtype), xbc], axis=1)
    out = bias + sum(xp[:, j:j + L] * w[j] for j in range(CONV_K))
    return jax.nn.silu(out), xp[:, -(CONV_K - 1):]


def _ssd(x, a, bm, cm, h0):
    b, L, H, P = x.shape
    G, N = bm.shape[2], bm.shape[3]
    rep = H // G
    T = math.gcd(L, SSD_CHUNK)
    nc = L // T
    f32 = jnp.float32
    xc = x.astype(f32).reshape(b, nc, T, G, rep, P)
    ac = a.astype(f32).reshape(b, nc, T, G, rep)
    bc = bm.astype(f32).reshape(b, nc, T, G, N)
    cc = cm.astype(f32).reshape(b, nc, T, G, N)
    acum = jnp.cumsum(ac, axis=2)
    seg = acum[:, :, :, None] - acum[:, :, None, :]
    causal = jnp.tril(jnp.ones((T, T), dtype=bool))[:, :, None, None]
    lmat = jnp.exp(jnp.where(causal, seg, -jnp.inf))
    cb = jnp.einsum('bclgn,bcsgn->bclsg', cc, bc)
    y_diag = jnp.einsum('bclsgr,bcsgrp->bclgrp', cb[..., None] * lmat, xc)
    decay_to_end = jnp.exp(acum[:, :, -1:] - acum)
    chunk_states = jnp.einsum('bcsgn,bcsgrp->bcgrpn', bc, xc * decay_to_end[..., None])
    chunk_decay = jnp.exp(acum[:, :, -1])

    def step(h, inp):
        s_c, dec = inp
        return h * dec[..., None, None] + s_c, h

    h_final, h_in = lax.scan(step, h0.astype(f32).reshape(b, G, rep, P, N),
                             (jnp.moveaxis(chunk_states, 1, 0), jnp.moveaxis(chunk_decay, 1, 0)))
    h_in = jnp.moveaxis(h_in, 0, 1)
    y_off = jnp.einsum('bclgn,bcgrpn->bclgrp', cc, h_in) * jnp.exp(acum)[..., None]
    return (y_diag + y_off).reshape(b, L, H, P), h_final.reshape(b, H, P, N)


def _mamba(xbc_act, z, dt_raw, h0, lp):
    b, L, _ = xbc_act.shape
    gn = B_GROUPS * B_STATE
    xs = xbc_act[..., :B_INNER].reshape(b, L, B_HEADS, B_HEADDIM)
    bm = xbc_act[..., B_INNER:B_INNER + gn].reshape(b, L, B_GROUPS, B_STATE)
    cm = xbc_act[..., B_INNER + gn:].reshape(b, L, B_GROUPS, B_STATE)
    dt = jax.nn.softplus(dt_raw.astype(jnp.float32) + lp['dt_bias'].astype(jnp.float32))
    a = -jnp.exp(lp['a_log'].astype(jnp.float32)) * dt
    y, h = _ssd(xs.astype(jnp.float32) * dt[..., None], a, bm, cm, h0)
    y = y + xs.astype(jnp.float32) * lp['d_skip'].astype(jnp.float32)[:, None]
    y = y.reshape(b, L, B_INNER) * jax.nn.silu(z.astype(jnp.float32))
    return _rmsnorm(y, lp['ssm_norm']), h


def _in_proj(h, pos, lp):
    b, L, _ = h.shape
    proj = _rmsnorm(h, lp['norm1']) @ lp['w_in']
    parts = []
    off = 0
    for sz in IN_SIZES:
        parts.append(proj[..., off:off + sz])
        off += sz
    aq, ak, av, cq, ck, cv, z, xbc, dt = parts

    def heads(t):
        return t.reshape(b, L, -1, HEAD_DIM)

    aq = _rope(_rmsnorm(heads(aq), lp['a_qn']), pos)
    ak = _rope(_rmsnorm(heads(ak), lp['a_kn']), pos)
    cq = _rope(_rmsnorm(heads(cq), lp['c_qn']), pos)
    ck = _rope(_rmsnorm(heads(ck), lp['c_kn']), pos)
    return aq, ak, heads(av), cq, ck, heads(cv), z, xbc, dt


def _out_ffn(h, a_o, m_o, c_o, lp):
    b, L, _ = h.shape
    mix = jnp.concatenate([a_o.reshape(b, L, A_WIDTH).astype(h.dtype), m_o.astype(h.dtype),
                           c_o.reshape(b, L, C_WIDTH).astype(h.dtype)], axis=-1)
    h = h + mix @ lp['w_out']
    u = _rmsnorm(h, lp['norm2'])
    return h + (jax.nn.silu(u @ lp['w_gate']) * (u @ lp['w_up'])) @ lp['w_down']


def _layer_prompt(h, lp):
    b, L, _ = h.shape
    pos = jnp.arange(L)
    aq, ak, av, cq, ck, cv, z, xbc, dt = _in_proj(h, pos, lp)
    a_o, _ = _banded_attention(aq, ak, av, A_WINDOW - 1, lp['a_sinks'])
    c_o = _dilated_prompt(cq, ck, cv)
    xbc_act, conv_state = _causal_conv(xbc, jnp.zeros((b, CONV_K - 1, CONV_DIM), xbc.dtype), lp['conv_w'], lp['conv_b'])
    m_o, ssm_state = _mamba(xbc_act, z, dt, jnp.zeros((b, B_HEADS, B_HEADDIM, B_STATE), jnp.float32), lp)
    h = _out_ffn(h, a_o, m_o, c_o, lp)
    wa = min(A_WINDOW, L)
    wc = min(C_MAX_SPAN, L)
    return h, (ak[:, L - wa:], av[:, L - wa:], ck[:, L - wc:], cv[:, L - wc:], ssm_state, conv_state)


def _layer_sample(h, c_ak, c_av, c_ck, c_cv, s_ssm, s_conv, lp):
    b, T, _ = h.shape
    pos = PAST_LEN + jnp.arange(T)
    aq, ak, av, cq, ck, cv, z, xbc, dt = _in_proj(h, pos, lp)
    buf_a = c_ak.shape[1]
    buf_c = c_ck.shape[1]
    ka = jnp.concatenate([c_ak.astype(ak.dtype), ak], axis=1)
    va = jnp.concatenate([c_av.astype(av.dtype), av], axis=1)
    kc = jnp.concatenate([c_ck.astype(ck.dtype), ck], axis=1)
    vc = jnp.concatenate([c_cv.astype(cv.dtype), cv], axis=1)
    a_o = _window_sample(aq, ka, va, lp['a_sinks'])
    c_o = _dilated_sample(cq, kc, vc)
    xbc_act, conv_state = _causal_conv(xbc, s_conv, lp['conv_w'], lp['conv_b'])
    m_o, ssm_state = _mamba(xbc_act, z, dt, s_ssm, lp)
    h = _out_ffn(h, a_o, m_o, c_o, lp)
    return h, (ka[:, -buf_a:], va[:, -buf_a:], kc[:, -buf_c:], vc[:, -buf_c:], ssm_state, conv_state)


def setup_inputs(seed: int = 0) -> dict:
    key = jax.random.key(seed)
    ks = iter(jax.random.split(key, 40))
    f32 = jnp.float32

    def nrm(shape, scale):
        return jax.random.normal(next(ks), shape, f32) * scale

    def gain(shape):
        return 1.0 + nrm(shape, 0.02)

    a_buf = min(A_WINDOW, PAST_LEN)
    c_buf = min(C_MAX_SPAN, PAST_LEN)
    x_prompt = nrm((BATCH, SEQ, D_MODEL), 1.0)
    x_sample = nrm((DEC_BATCH, DEC_SEQ, D_MODEL), 1.0)
    cache_a_k = nrm((DEPTH, DEC_BATCH, a_buf, A_KV_HEADS, HEAD_DIM), 1.0)
    cache_a_v = nrm((DEPTH, DEC_BATCH, a_buf, A_KV_HEADS, HEAD_DIM), 1.0)
    cache_c_k = nrm((DEPTH, DEC_BATCH, c_buf, C_KV_HEADS, HEAD_DIM), 1.0)
    cache_c_v = nrm((DEPTH, DEC_BATCH, c_buf, C_KV_HEADS, HEAD_DIM), 1.0)
    state_ssm = nrm((DEPTH, DEC_BATCH, B_HEADS, B_HEADDIM, B_STATE), 0.1)
    state_conv = nrm((DEPTH, DEC_BATCH, CONV_K - 1, CONV_DIM), 1.0)
    norm1 = gain((DEPTH, D_MODEL))
    w_in = nrm((DEPTH, D_MODEL, N_IN), D_MODEL ** -0.5)
    a_qn = gain((DEPTH, HEAD_DIM))
    a_kn = gain((DEPTH, HEAD_DIM))
    a_sinks = nrm((DEPTH, A_Q_HEADS), 0.5)
    c_qn = gain((DEPTH, HEAD_DIM))
    c_kn = gain((DEPTH, HEAD_DIM))
    conv_w = nrm((DEPTH, CONV_K, CONV_DIM), CONV_K ** -0.5)
    conv_b = nrm((DEPTH, CONV_DIM), 0.02)
    u = jax.random.uniform(next(ks), (DEPTH, B_HEADS), f32)
    dt0 = jnp.exp(u * (math.log(0.1) - math.log(0.001)) + math.log(0.001))
    dt_bias = dt0 + jnp.log(-jnp.expm1(-dt0))
    a_log = jnp.log(jax.random.uniform(next(ks), (DEPTH, B_HEADS), f32, 1.0, 16.0))
    d_skip = 1.0 + nrm((DEPTH, B_HEADS), 0.1)
    ssm_norm = gain((DEPTH, B_INNER))
    w_out = nrm((DEPTH, MIX_WIDTH, D_MODEL), MIX_WIDTH ** -0.5)
    norm2 = gain((DEPTH, D_MODEL))
    w_gate = nrm((DEPTH, D_MODEL, D_FF), D_MODEL ** -0.5)
    w_up = nrm((DEPTH, D_MODEL, D_FF), D_MODEL ** -0.5)
    w_down = nrm((DEPTH, D_FF, D_MODEL), D_FF ** -0.5)
    return {'x_prompt': x_prompt, 'x_sample': x_sample,
            'cache_a_k': cache_a_k, 'cache_a_v': cache_a_v, 'cache_c_k': cache_c_k, 'cache_c_v': cache_c_v,
            'state_ssm': state_ssm, 'state_conv': state_conv,
            'norm1': norm1, 'w_in': w_in, 'a_qn': a_qn, 'a_kn': a_kn, 'a_sinks': a_sinks,
            'c_qn': c_qn, 'c_kn': c_kn, 'conv_w': conv_w, 'conv_b': conv_b, 'dt_bias': dt_bias,
            'a_log': a_log, 'd_skip': d_skip, 'ssm_norm': ssm_norm, 'w_out': w_out, 'norm2': norm2,
            'w_gate': w_gate, 'w_up': w_up, 'w_down': w_down}


def reference(x_prompt, x_sample, cache_a_k, cache_a_v, cache_c_k, cache_c_v, state_ssm, state_conv,
              norm1, w_in, a_qn, a_kn, a_sinks, c_qn, c_kn, conv_w, conv_b, dt_bias, a_log, d_skip,
              ssm_norm, w_out, norm2, w_gate, w_up, w_down):
    hp = x_prompt
    hs = x_sample
    p_out = [[] for _ in range(6)]
    s_out = [[] for _ in range(6)]
    for l in range(DEPTH):
        lp = {'norm1': norm1[l], 'w_in': w_in[l], 'a_qn': a_qn[l], 'a_kn': a_kn[l], 'a_sinks': a_sinks[l],
              'c_qn': c_qn[l], 'c_kn': c_kn[l], 'conv_w': conv_w[l], 'conv_b': conv_b[l],
              'dt_bias': dt_bias[l], 'a_log': a_log[l], 'd_skip': d_skip[l], 'ssm_norm': ssm_norm[l],
              'w_out': w_out[l], 'norm2': norm2[l], 'w_gate': w_gate[l], 'w_up': w_up[l], 'w_down': w_down[l]}
        hp, pst = _layer_prompt(hp, lp)
        hs, sst = _layer_sample(hs, cache_a_k[l], cache_a_v[l], cache_c_k[l], cache_c_v[l],
                                state_ssm[l], state_conv[l], lp)
        for i in range(6):
            p_out[i].append(pst[i])
            s_out[i].append(sst[i])
    p_a_k, p_a_v, p_c_k, p_c_v, p_ssm, p_conv = [jnp.stack(t, axis=0) for t in p_out]
    s_a_k, s_a_v, s_c_k, s_c_v, s_ssm, s_conv = [jnp.stack(t, axis=0) for t in s_out]
    return (hp, hs, p_a_k, p_a_v, p_c_k, p_c_v, p_ssm, p_conv, s_a_k, s_a_v, s_c_k, s_c_v, s_ssm, s_conv)
```

```python
import numpy as np
import ml_dtypes
from contextlib import ExitStack
import concourse.bass as bass
import concourse.mybir as mybir
from concourse.bass_utils import run_bass_kernel_spmd

F32 = mybir.dt.float32
BF16 = mybir.dt.bfloat16
AF = mybir.ActivationFunctionType
ALU = mybir.AluOpType
AX = mybir.AxisListType

NCORES = 8
D = 1024
DEPTH = 2
TOK = 2048
NT = 16
SB = 16
NIN = 2568
DFF = 2816
EPS = 1e-6
C_QA, C_QC, C_KA, C_KC, C_VA, C_VC, C_Z, C_DT, C_X = 0, 256, 512, 640, 768, 896, 1024, 1536, 1544
NR1 = 2048 + 1024 + 64 + 12
STAGE = 2


class Buf:
    __slots__ = ("w", "r")

    def __init__(self):
        self.w = None
        self.r = []


class Prog:
    ENG = ("pe", "act", "dve", "pool", "sp")
    NDSEM = {"sp": 12, "pool": 6, "act": 44}

    def __init__(self, nc):
        self.nc = nc
        self.ops = []
        self.last = {e: None for e in self.ENG}
        self.pend = {e: {} for e in self.ENG}
        self.dmas = []

    def _emit(self, eng, fn, r, w, kind):
        deps = dict(self.pend[eng])
        self.pend[eng] = {}
        for b in r:
            if b.w is not None:
                deps[b.w] = "raw"
        for b in w:
            if b.w is not None:
                deps[b.w] = "raw"
            for j in b.r:
                deps.setdefault(j, "war")
        i = len(self.ops)
        self.ops.append(dict(eng=eng, fn=fn, deps=deps, kind=kind))
        for b in r:
            if kind == "c":
                b.r = [j for j in b.r if not (self.ops[j]["kind"] == "c" and self.ops[j]["eng"] == eng)]
            b.r.append(i)
        for b in w:
            b.w = i
            b.r = []
        if kind in ("dma", "cc"):
            self.dmas.append(i)
        else:
            self.last[eng] = i
        return i

    def op(self, eng, fn, r=(), w=()):
        return self._emit(eng, fn, r, w, "c")

    def dma(self, q, out, in_, r=(), w=(), **kw):
        return self._emit(q, lambda e: e.dma_start(out=out, in_=in_, **kw), r, w, "dma")

    def cc(self, fn, r=(), w=()):
        return self._emit("pool", fn, r, w, "cc")

    def barrier(self):
        allp = {}
        for e in self.ENG:
            if self.last[e] is not None:
                allp[self.last[e]] = "raw"
        for j in self.dmas:
            allp[j] = "raw"
        self.dmas = []
        for e in self.ENG:
            self.pend[e].update(allp)

    def finish(self, es):
        nc = self.nc
        ops = self.ops
        self.barrier()
        fin = dict(self.pend["sp"])
        needed = set()

        def real_dep(i, j, kind):
            oi, oj = ops[i], ops[j]
            if oj["kind"] in ("dma", "cc"):
                return True
            if oi["eng"] == oj["eng"] and oi["kind"] != "dma":
                if oi["eng"] == "pe":
                    return False
                return kind == "raw"
            return True

        for i, o in enumerate(ops):
            for j, k in o["deps"].items():
                if ops[j]["kind"] == "c" and real_dep(i, j, k):
                    needed.add(j)
        for j in fin:
            if ops[j]["kind"] == "c":
                needed.add(j)
        seq = {}
        cnt = {e: 0 for e in self.ENG}
        dcount = {q: 0 for q in self.NDSEM}
        dtok = {}
        for i, o in enumerate(ops):
            if o["kind"] == "dma":
                q = o["eng"]
                k = dcount[q]
                dcount[q] += 1
                R = self.NDSEM[q]
                dtok[i] = (q, k % R, 16 * (k // R + 1))
            elif o["kind"] == "cc":
                dtok[i] = ("cc", i, 1)
            elif i in needed:
                cnt[o["eng"]] += 1
                seq[i] = cnt[o["eng"]]
        ccsem = {i: es.enter_context(nc.semaphore("cc%d" % i)) for i, o in enumerate(ops) if o["kind"] == "cc"}
        csem = {e: es.enter_context(nc.semaphore("c_" + e)) for e in ("pe", "act", "dve", "pool")}
        dsem = {q: [es.enter_context(nc.semaphore("d_%s%d" % (q, k))) for k in range(R)]
                for q, R in self.NDSEM.items()}
        streams = {e: [] for e in self.ENG}
        waited = {e: {} for e in self.ENG}

        def add_wait(e, lst, key, sem, val):
            if waited[e].get(key, 0) >= val:
                return
            waited[e][key] = val
            lst.append((sem, val))

        for i, o in enumerate(ops):
            e = o["eng"]
            waits = []
            for j, k in sorted(o["deps"].items()):
                if not real_dep(i, j, k):
                    continue
                if ops[j]["kind"] == "dma":
                    q, s, tgt = dtok[j]
                    add_wait(e, waits, ("d", q, s), dsem[q][s], tgt)
                elif ops[j]["kind"] == "cc":
                    add_wait(e, waits, ("cc", j), ccsem[j], 1)
                else:
                    f = ops[j]["eng"]
                    add_wait(e, waits, ("c", f), csem[f], seq[j])
            inc = None
            if o["kind"] == "dma":
                q, s, tgt = dtok[i]
                if tgt > 16:
                    add_wait(e, waits, ("d", q, s), dsem[q][s], tgt - 16)
                inc = (dsem[q][s], 16)
            elif o["kind"] == "cc":
                inc = (ccsem[i], 1)
            elif i in needed:
                inc = (csem[e], 1)
            streams[e].append((waits, o["fn"], inc))
        finw = []
        for j in sorted(fin):
            if ops[j]["kind"] == "dma":
                q, s, tgt = dtok[j]
                add_wait("sp", finw, ("d", q, s), dsem[q][s], tgt)
            elif ops[j]["kind"] == "cc":
                add_wait("sp", finw, ("cc", j), ccsem[j], 1)
            else:
                f = ops[j]["eng"]
                add_wait("sp", finw, ("c", f), csem[f], seq[j])
        for q, R in self.NDSEM.items():
            for s in range(R):
                k = dcount[q]
                n = (k - s + R - 1) // R if k > s else 0
                if n > 0:
                    add_wait("sp", finw, ("d", q, s), dsem[q][s], 16 * n)
        self.stats = dict(n_ops=len(ops), needed=len(needed), cnt=cnt, dcount=dcount)

        def run(eng_obj, lst, tail=()):
            for waits, fn, inc in lst:
                for sem, val in waits:
                    eng_obj.wait_ge(sem, val)
                ins = fn(eng_obj)
                if inc is not None:
                    ins.then_inc(inc[0], inc[1])
            for sem, val in tail:
                eng_obj.wait_ge(sem, val)

        with nc.Block() as block:
            @block.tensor
            def _(e):
                run(e, streams["pe"])

            @block.scalar
            def _(e):
                run(e, streams["act"])

            @block.vector
            def _(e):
                run(e, streams["dve"])

            @block.gpsimd
            def _(e):
                run(e, streams["pool"])

            @block.sync
            def _(e):
                run(e, streams["sp"], finw)


class Arena:
    def __init__(self, handle, nf32):
        self.h = handle
        self.n = nf32
        self.off = 0

    def reset(self):
        self.off = 0

    def f32(self, n):
        a = self.h[:, self.off:self.off + n]
        self.off += n
        assert self.off <= self.n, (self.off, self.n)
        return a

    def bf16(self, n):
        m = (n + 1) // 2
        a = self.h[:, self.off:self.off + m].bitcast(BF16)
        self.off += m
        assert self.off <= self.n, (self.off, self.n)
        return a[:, 0:n]


def build():
    nc = bass.Bass("TRN2", target_bir_lowering=False)
    P = Prog(nc)
    es = ExitStack()

    def din(name, shape, dt=F32):
        return nc.dram_tensor(name, list(shape), dt, kind="ExternalInput").ap()

    def dout(name, shape, dt=F32):
        return nc.dram_tensor(name, list(shape), dt, kind="ExternalOutput").ap()

    def dscr(name, shape, dt=F32):
        return nc.dram_tensor(name, list(shape), dt).ap()

    def sb(name, shape, dt=F32):
        return es.enter_context(nc.sbuf_tensor(name, list(shape), dt))

    xp = din("xp", [TOK, D])
    xs = din("xs", [64, D])
    cak = din("cak", [DEPTH, SB, 128, 128])
    cav = din("cav", [DEPTH, SB, 128, 128])
    cck = din("cck", [DEPTH, SB, 2048, 128])
    ccv = din("ccv", [DEPTH, SB, 2048, 128])
    sssm = din("sssm", [DEPTH, SB, 8, 8192])
    sconv = din("sconv", [DEPTH, SB * 3, 1024])
    w_in = din("w_in", [DEPTH, D, NIN])
    w_out = din("w_out", [DEPTH, D, D])
    w_gate = din("w_gate", [DEPTH, D, DFF])
    w_up = din("w_up", [DEPTH, D, DFF])
    w_down = din("w_down", [DEPTH, DFF, D])
    par_in = din("par", [DEPTH, 128, 3456])
    cst_f = din("cst_f", [128, 1344])
    cst_b = din("cst_b", [128, 2304], BF16)

    yp = dout("yp", [TOK, D])
    ys = dout("ys", [64, D])
    pak = dout("pak", [DEPTH, 128, 128])
    pav = dout("pav", [DEPTH, 128, 128])
    pck = dout("pck", [DEPTH, TOK, 128])
    pcv = dout("pcv", [DEPTH, TOK, 128])
    pssm = dout("pssm", [DEPTH, 512, 128])
    pconv = dout("pconv", [DEPTH, 3, 1024])
    sak = dout("sak", [DEPTH, SB, 128, 128])
    sav = dout("sav", [DEPTH, SB, 128, 128])
    sck = dout("sck", [DEPTH, SB, 2048, 128])
    scv = dout("scv", [DEPTH, SB, 2048, 128])
    sssm_o = dout("sssm_o", [DEPTH, SB, 8, 8192])
    sconv_o = dout("sconv_o", [DEPTH, SB * 3, 1024])

    bn1 = dscr("bn1", [NR1, 256], BF16)
    g1 = dscr("g1", [NCORES * NR1, 256], BF16)
    bn2 = dscr("bn2", [129, 512])
    g2 = dscr("g2", [NCORES * 129, 512])
    zs_scr = dscr("zs_scr", [17, 128, 512], BF16)
    qt_scr = dscr("qt_scr", [128, 4 * 2176], BF16)
    kta_scr = dscr("kta_scr", [128, 2048], BF16)
    sq_scr = dscr("sq_scr", [128, 512])
    sx_scr = dscr("sx_scr", [64, 1024])
    sdt_scr = dscr("sdt_scr", [64, 8])
    sy_scr = dscr("sy_scr", [64, 512])
    smix = dscr("smix", [64, 512])
    B_sq, B_sx, B_sdt, B_sy, B_smix = Buf(), Buf(), Buf(), Buf(), Buf()
    vprev = dscr("vprev", [2048, 256], BF16)
    B_vprev = Buf()
    B_kta = Buf()
    B_bn1, B_g1, B_bn2, B_g2, B_zs, B_qt = Buf(), Buf(), Buf(), Buf(), Buf(), Buf()
    RG = [list(range(NCORES))]

    Hh = sb("H", [128, NT * D])
    H = Hh[:, :].rearrange("p (t d) -> p t d", t=NT)
    BH = [Buf() for _ in range(NT + 1)]
    HSh = sb("HS", [128, D])
    HS = HSh[:, :]
    CFh = sb("CF", [128, 1344])
    CF = CFh[:, :]
    CBh = sb("CB", [128, 2304], BF16)
    CB = CBh[:, :]
    B_cst = Buf()
    identF, triU, triS, onesF = CF[:, 0:128], CF[:, 128:256], CF[:, 256:384], CF[:, 384:512]
    cosT = CF[:, 512:648].rearrange("p (t k) -> p t k", t=17)
    sinT = CF[:, 648:784].rearrange("p (t k) -> p t k", t=17)
    cs1T = CF[:, 800:1072].rearrange("p (t k) -> p t k", t=17)
    cs2T = CF[:, 1072:1344].rearrange("p (t k) -> p t k", t=17)
    pv = CF[:, 784:785]
    rmask = CF[:, 785:793]
    identB = CB[:, 0:128]
    mA0, mA, mC0, mC = CB[:, 128:640], CB[:, 640:1152], CB[:, 1152:1664], CB[:, 1664:2176]
    causB = CB[:, 2176:2304]
    PARh = sb("PAR", [128, 3456])
    PAR = PARh[:, :]
    B_par = Buf()
    g1b, g2b = PAR[:, 0:1024], PAR[:, 1024:2048]
    qkg = PAR[:, 2048:2816].rearrange("p (h d) -> p h d", h=12)
    qg = PAR[:, 2048:2560].rearrange("p (h d) -> p h d", h=8)
    kg = PAR[:, 2560:2816].rearrange("p (h d) -> p h d", h=4)
    ssmn = PAR[:, 2816:3328]
    sinks_b = PAR[:, 3328:3332]
    dtb_b = PAR[:, 3332:3340]
    aneg_b = PAR[:, 3340:3348]
    dsk_b = PAR[:, 3348:3356]
    convw = PAR[:, 3356:3388].rearrange("p (c j) -> p c j", c=8)
    convb = PAR[:, 3388:3396]
    esink_b = PAR[:, 3396:3400]
    sinkl = PAR[:, 3400:3402]
    anegl = PAR[:, 3402:3403]
    dskl = PAR[:, 3403:3404]
    dtbl = PAR[:, 3404:3405]
    ARN = 29000
    ARh = sb("ARENA", [128, ARN])
    AR = Arena(ARh, ARN)

    PS = [es.enter_context(nc.psum_tensor("ps%d" % i, [128, 512], F32)) for i in range(8)]
    BP = [Buf() for _ in range(8)]

    def psf(i):
        return PS[i][:, :]

    def psb(i):
        return PS[i][:, :].bitcast(BF16)

    def mm(out, lhsT, rhs, st, sp_, r, w):
        P.op("pe", lambda e: e.matmul(out, lhsT, rhs, start=st, stop=sp_), r, w)

    def tr(out, in_, idn, r, w):
        P.op("pe", lambda e: e.transpose(out, in_, idn), r, w)

    def act(out, in_, fn, r, w, bias=0.0, scale=1.0, accum=None):
        if accum is None:
            P.op("act", lambda e: e.activation(out, in_, fn, bias=bias, scale=scale), r, w)
        else:
            P.op("act", lambda e: e.activation(out, in_, fn, bias=bias, scale=scale, accum_out=accum), r, w)

    def tt(eng, out, in0, in1, op, r, w):
        P.op(eng, lambda e: e.tensor_tensor(out, in0, in1, op), r, w)

    def ts(eng, out, in0, s1, s2, op0, op1, r, w):
        if op1 is None:
            P.op(eng, lambda e: e.tensor_scalar(out, in0, s1, None, op0), r, w)
        else:
            P.op(eng, lambda e: e.tensor_scalar(out, in0, s1, s2, op0, op1), r, w)

    def stt(eng, out, in0, sc, in1, op0, op1, r, w):
        P.op(eng, lambda e: e.scalar_tensor_tensor(out, in0, sc, in1, op0, op1), r, w)

    def red(eng, out, in_, op, r, w):
        P.op(eng, lambda e: e.tensor_reduce(out, in_, AX.X, op), r, w)

    def cp(eng, out, in_, r, w):
        if eng == "act":
            P.op("act", lambda e: e.activation(out, in_, AF.Copy), r, w)
        else:
            P.op(eng, lambda e: e.tensor_copy(out, in_), r, w)

    def rcp(out, in_, r, w):
        P.op("dve", lambda e: e.reciprocal(out, in_), r, w)

    def bc(ap, shape):
        return ap.to_broadcast(list(shape))

    P.dma("sp", CF, cst_f[:, :], w=[B_cst])
    P.dma("sp", CB, cst_b[:, :], w=[B_cst])
    class _Fresh(list):
        def __iter__(self):
            return iter([Buf()])

    B_out = Buf()
    OUTW = _Fresh()
    B_cpy = {}
    B_new = {}
    def issue_copies(l, q):
        for (nm, src, dst, n) in (("ak", cak, sak, 128), ("av", cav, sav, 128), ("ck", cck, sck, 2048), ("cv", ccv, scv, 2048)):
            B_cpy[(l, nm)] = []
            B_new[(l, nm)] = Buf()
            step = SB if n == 128 else 4
            for b0 in range(0, SB, step):
                bb = Buf()
                B_cpy[(l, nm)].append(bb)
                P.dma(q, dst[l, b0:b0 + step, 0:n - 4, :].rearrange("b (r f) c -> b r (f c)", f=4),
                      src[l, b0:b0 + step, 4:n, :].rearrange("b (r f) c -> b r (f c)", f=4), w=[bb])

    issue_copies(0, "act")
    issue_copies(1, "act")

    def load_params(l):
        P.dma("sp", PAR, par_in[l], w=[B_par])
        act(aneg_b, aneg_b, AF.Exp, [B_par], [B_par])
        ts("dve", aneg_b, aneg_b, -1.0, None, ALU.mult, None, [B_par], [B_par])
        act(anegl, anegl, AF.Exp, [B_par], [B_par])
        ts("dve", anegl, anegl, -1.0, None, ALU.mult, None, [B_par], [B_par])
        act(esink_b, sinks_b, AF.Exp, [B_par], [B_par])

    V, A_, G = "dve", "act", "pool"
    MUL, ADD, SUB, POW, MAXOP = ALU.mult, ALU.add, ALU.subtract, ALU.pow, ALU.max
    bn1_kc = bn1[2048:3072, :].rearrange("(p j) c -> p (j c)", j=8)
    bn1_ka = bn1[3072:3136, :].rearrange("r (a c) -> (r a) c", a=2)
    bn1_tail = bn1[3136:3148, :].rearrange("r c -> (r c)").rearrange("(p k) -> p k", p=128)
    qt_v = qt_scr.rearrange("p (j t) -> p j t", j=4)

    def normrope(X, nh, gain, t, tmp, sm, bufs):
        sq = tmp[:, 0:nh * 64].rearrange("p (h d) -> p h d", h=nh)
        tt(V, sq, X, X, MUL, bufs, bufs)
        ss = sm[:, 0:nh]
        red(V, ss, sq, ADD, bufs, bufs)
        act(ss, ss, AF.Sqrt, bufs, bufs, bias=EPS, scale=1.0 / 64)
        rcp(ss, ss, bufs, bufs)
        tt(V, X, X, bc(ss.unsqueeze(2), [128, nh, 64]), MUL, bufs, bufs)
        tt(V, X, X, gain, MUL, bufs + [B_par], bufs)
        tA = tmp[:, 0:nh * 16].rearrange("p (h d) -> p h d", h=nh)
        tB = tmp[:, nh * 16:nh * 32].rearrange("p (h d) -> p h d", h=nh)
        rb = bufs + [B_cst]
        tt(V, tA, X[:, :, 0:16], bc(cs1T[:, t, :].unsqueeze(1), [128, nh, 16]), MUL, rb, bufs)
        tt(V, tB, X[:, :, 0:16], bc(cs2T[:, t, :].unsqueeze(1), [128, nh, 16]), MUL, rb, bufs)
        tt(V, X[:, :, 0:8], tA[:, :, 0:8], tA[:, :, 8:16], SUB, bufs, bufs)
        tt(V, X[:, :, 8:16], tB[:, :, 0:8], tB[:, :, 8:16], ADD, bufs, bufs)

    def rmsnorm_T(xin, bx, gb, ubf, uT, sm, junk, bufs):
        ss = sm[:, 16:17]
        P.op(V, lambda e: e.memset(ss, 0.0), [], bufs)
        act(junk, xin, AF.Square, [bx] + bufs, bufs, accum=ss)
        act(ss, ss, AF.Sqrt, bufs, bufs, bias=EPS, scale=1.0 / D)
        rcp(ss, ss, bufs, bufs)
        stt(V, ubf, xin, ss, gb, MUL, MUL, [bx, B_par] + bufs, bufs)
        for c in range(8):
            tr(psb(6)[:, c * 128:(c + 1) * 128], ubf[:, c * 128:(c + 1) * 128], identB, bufs + [B_cst], [BP[6]])
        cp(A_, uT, psb(6)[:, 0:1024], [BP[6]], bufs)

    state = {}

    def layer_P1(l):
        load_params(l)
        AR.reset()
        xbS = AR.bf16(8 * 64).rearrange("p (c t) -> p c t", c=8)
        dtall = AR.f32(17 * 8).rearrange("p (t h) -> p t h", t=17)
        mo_all = AR.bf16(17 * 512).rearrange("p (t c) -> p t c", t=17)
        markA = AR.off
        xbcT = AR.bf16(8 * 2052).rearrange("p (c t) -> p c t", c=8)
        B_xbc, B_dt, B_mo = Buf(), Buf(), Buf()
        state.update(xbcT=xbcT, xbS=xbS, dtall=dtall, B_xbc=B_xbc, B_dt=B_dt, mark=AR.off, markA=markA,
                     mo_all=mo_all, B_mo=B_mo)
        Wb = AR.bf16(8 * NIN).rearrange("p (c n) -> p c n", c=8)
        B_W = Buf()
        for c in range(8):
            P.dma("pool", Wb[:, c, :], w_in[l, c * 128:(c + 1) * 128, :], w=[B_W])
        usets = []
        for k in range(2):
            usets.append(dict(ubf=AR.bf16(1024), uT=AR.bf16(1024), sm=AR.f32(32), B=Buf()))
        qkf = AR.f32(768)
        vf = AR.f32(256)
        tmpq = AR.f32(768)
        smq = AR.f32(32)
        smd = AR.f32(8)
        qkT = AR.bf16(768)
        zst = AR.bf16(512)
        vbf = AR.bf16(256)
        Bq, Bq2, Bkv, Bk2, Bz, Bd, Bvb = Buf(), Buf(), Buf(), Buf(), Buf(), Buf(), Buf()
        smd_t = AR.f32(24)
        Btl = Buf()
        qkbs = [AR.bf16(768), AR.bf16(768)]
        Bqn = [Buf(), Buf()]

        def tile_io(t):
            smp = (t == 16)
            us = usets[t % 2]
            if not smp:
                return smp, us, H[:, t, :], BH[t]
            return smp, us, HS, BH[16]

        def stA(t):
            smp, us, xin, bx = tile_io(t)
            if l == 0:
                if not smp:
                    P.dma("sp", xin, xp[t * 128:(t + 1) * 128, :], w=[bx])
                else:
                    P.dma("sp", HS[0:64, :], xs[:, :], w=[bx])
                    P.dma("sp", HS[64:128, :], xs[:, :], w=[bx])
            rmsnorm_T(xin, bx, g1b, us["ubf"], us["uT"], us["sm"], us["ubf"], [us["B"]])

        def stB(t):
            smp, us, xin, bx = tile_io(t)
            Bu, uT = us["B"], us["uT"]
            for (bk, c0, n) in ((0, 0, 512), (1, 512, 512), (2, 1024, 512), (3, 1536, 8)):
                for c in range(8):
                    mm(psf(bk)[:, 0:n], uT[:, c * 128:(c + 1) * 128], Wb[:, c, c0:c0 + n], c == 0, c == 7,
                       [Bu, B_W], [BP[bk]])
            for cc in range(8):
                for c in range(8):
                    mm(psf(4 + cc // 4)[:, (cc % 4) * 128:(cc % 4 + 1) * 128],
                       Wb[:, c, C_X + cc * 128:C_X + (cc + 1) * 128], uT[:, c * 128:(c + 1) * 128],
                       c == 0, c == 7, [Bu, B_W], [BP[4 + cc // 4]])

        def stC1(t):
            smp, us, xin, bx = tile_io(t)
            Bu, uT = us["B"], us["uT"]
            k2 = t % 2
            cp(A_, qkf[:, 0:512], psf(0), [BP[0]], [Bq])
            cp(A_, qkf[:, 512:768], psf(1)[:, 0:256], [BP[1]], [Bq])
            cp(A_, vf, psf(1)[:, 256:512], [BP[1]], [Bkv])
            normrope(qkf.rearrange("p (h d) -> p h d", h=12), 12, qkg, t, tmpq, smq, [Bq])
            cp(V, qkbs[k2], qkf, [Bq], [Bqn[k2]])
            if smp:
                P.dma("sp", sq_scr[:, :], qkf[:, 0:512], r=[Bq], w=[B_sq])
            if not smp:
                P.dma("sp", pck[l, t * 128:(t + 1) * 128, :], qkf[:, 640:768], r=[Bq], w=OUTW)
                P.dma("sp", pcv[l, t * 128:(t + 1) * 128, :], vf[:, 128:256], r=[Bkv], w=OUTW)
                if t == 15:
                    P.dma("sp", pak[l], qkf[:, 512:640], r=[Bq], w=OUTW)
                    P.dma("sp", pav[l], vf[:, 0:128], r=[Bkv], w=OUTW)
                cp(G, vbf, vf, [Bkv], [Bvb])
                P.dma("sp", bn1[t * 128:(t + 1) * 128, :], vbf, r=[Bvb], w=[B_bn1])
            else:
                for tk in range(4):
                    rows = slice(tk * 16, (tk + 1) * 16)
                    P.dma("sp", sak[l, :, 124 + tk, :], qkf[rows, 512:640], r=[Bq], w=[B_new[(l, "ak")]])
                    P.dma("sp", sck[l, :, 2044 + tk, :], qkf[rows, 640:768], r=[Bq], w=[B_new[(l, "ck")]])
                    P.dma("sp", sav[l, :, 124 + tk, :], vf[rows, 0:128], r=[Bkv], w=[B_new[(l, "av")]])
                    P.dma("sp", scv[l, :, 2044 + tk, :], vf[rows, 128:256], r=[Bkv], w=[B_new[(l, "cv")]])
            act(zst, psf(2), AF.Silu, [BP[2]], [Bz])
            P.dma("sp", zs_scr[t], zst, r=[Bz], w=[B_zs])
            tt(V, smd, psf(3)[:, 0:8], dtb_b, ADD, [BP[3], B_par], [Bd])
            act(smd, smd, AF.Exp, [Bd], [Bd])
            act(dtall[:, t, :], smd, AF.Ln, [Bd], [B_dt], bias=1.0)
            if smp:
                P.dma("sp", sdt_scr[:, :], dtall[0:64, t, :], r=[B_dt], w=[B_sdt])
            for bk in range(2):
                src = psf(4 + bk).rearrange("p (c t) -> p c t", c=4)
                if not smp:
                    cp(A_, xbcT[:, 4 * bk:4 * bk + 4, 3 + t * 128:3 + (t + 1) * 128], src, [BP[4 + bk]], [B_xbc])
                else:
                    cp(A_, xbS[:, 4 * bk:4 * bk + 4, :], src[:, :, 0:64], [BP[4 + bk]], [B_xbc])
            if t == 15:
                P.dma("sp", bn1_tail, xbcT[:, :, 3 + 2045:3 + 2048], r=[B_xbc], w=[B_bn1])
                tl = smd_t.rearrange("p (j c) -> p j c", j=3)
                cp(V, tl, xbcT[:, :, 3 + 2045:3 + 2048].rearrange("p c j -> p j c"), [B_xbc, Btl], [Btl])
                for j in range(3):
                    P.dma("sp", pconv[l, j].rearrange("(c p) -> p c", p=128), tl[:, j, :], r=[Btl], w=OUTW,
                          allow_slow_non_contiguous=True)
            if smp:
                xtm = (tmpq[:, 0:512], qkf[:, 0:512])
                for bk in range(2):
                    for c in range(8):
                        mm(psf(4 + bk), uT[:, c * 128:(c + 1) * 128],
                           Wb[:, c, C_X + bk * 512:C_X + (bk + 1) * 512], c == 0, c == 7, [Bu, B_W], [BP[4 + bk]])
                    cp(A_, xtm[bk], psf(4 + bk), [BP[4 + bk]], [Bq])
                sco = sconv_o[l].rearrange("(b j) c -> b j c", j=3)
                for tk in range(1, 4):
                    for bk in range(2):
                        P.dma("sp", sco[:, tk - 1, bk * 512:(bk + 1) * 512], xtm[bk][tk * 16:(tk + 1) * 16, :],
                              r=[Bq], w=OUTW)

        def stC2(t):
            smp = (t == 16)
            k2 = t % 2
            for j in range(6):
                tr(psb(7)[:, j * 128:(j + 1) * 128], qkbs[k2][:, j * 128:(j + 1) * 128], identB, [Bqn[k2], B_cst], [BP[7]])
            cp(A_, qkT, psb(7)[:, 0:768], [BP[7]], [Bq2])
            P.dma("sp", qt_v[:, :, t * 128:(t + 1) * 128], qkT[:, 0:512].rearrange("p (j t) -> p j t", j=4), r=[Bq2], w=[B_qt])
            if not smp:
                P.dma("sp", bn1_kc[:, t * 128:(t + 1) * 128], qkT[:, 640:768], r=[Bq2], w=[B_bn1])
                P.dma("sp", kta_scr[:, t * 128:(t + 1) * 128], qkT[:, 512:640], r=[Bq2], w=[B_kta])
                if t == 15:
                    P.dma("sp", bn1_ka, qkT[:, 512:640], r=[Bq2], w=[B_bn1])

        stA(0)
        for t in range(17):
            stB(t)
            if t + 1 < 17:
                stA(t + 1)
            stC1(t)
            if t >= 1:
                stC2(t - 1)
        stC2(16)
        P.cc(lambda e: e.collective_compute("AllGather", ALU.bypass, replica_groups=RG,
                                            ins=[bn1[:, :]], outs=[g1[:, :]]), r=[B_bn1], w=[B_g1])


    def dma_fn(q, fn, r=(), w=()):
        return P._emit(q, fn, r, w, "dma")

    pcache = {}

    def prev_base(e):
        k = id(e)
        if k not in pcache:
            pcache[k] = ((e.partition_id() + (NCORES - 1)) % NCORES) * (NR1 * 256)
        return pcache[k]

    def prev_rows(e, off, n):
        return bass.AP(g1.tensor, prev_base(e) + off * 256, [[256, n], [1, 256]])

    def layer_P23(l):
        xbcT, dtall, B_xbc, B_dt = state["xbcT"], state["dtall"], state["B_xbc"], state["B_dt"]
        AR.off = state["mark"]
        xact = AR.bf16(8 * 2048).rearrange("p (c t) -> p c t", c=8)
        hT = AR.f32(512)
        hTb = AR.bf16(512)
        Ltot = AR.f32(8)
        mo_all, B_mo = state["mo_all"], state["B_mo"]
        B_xact, B_h = Buf(), Buf()
        mk = AR.off
        dma_fn("sp", lambda e: e.dma_start(
            out=xbcT[:, :, 0:3],
            in_=prev_rows(e, 3136, 12).rearrange("r c -> (r c)").rearrange("(p k) -> p k", p=128)),
            r=[B_g1], w=[B_xbc])
        ts(V, xbcT[:, :, 0:3], xbcT[:, :, 0:3], pv, None, MUL, None, [B_xbc, B_cst], [B_xbc])
        accs = [AR.f32(2048), AR.f32(2048)]
        Bacc = [Buf(), Buf()]
        for c in range(8):
            eng = V
            acc, ba = accs[c % 2], Bacc[c % 2]
            ts(eng, acc, xbcT[:, c, 0:2048], convw[:, c, 0:1], None, MUL, None, [B_xbc, B_par], [ba])
            for j in range(1, 4):
                stt(eng, acc, xbcT[:, c, j:j + 2048], convw[:, c, j:j + 1], acc, MUL, ADD, [B_xbc, B_par, ba], [ba])
            act(xact[:, c, :], acc, AF.Silu, [ba, B_par], [B_xact], bias=convb[:, c:c + 1])
        AR.off = mk
        xbms = [AR.bf16(768), AR.bf16(768)]
        Bxb = [Buf(), Buf()]
        xw = AR.bf16(512)
        sm = AR.f32(64)
        Bt = Buf()
        Ba, Bxw = Buf(), Buf()
        a8, dte8, dec8, w8 = sm[:, 0:8], sm[:, 8:16], sm[:, 16:24], sm[:, 24:32]
        P.op(V, lambda e: e.memset(hT, 0.0), [], [B_h])
        P.op(V, lambda e: e.memset(Ltot, 0.0), [], [B_h])

        def xs_bm_tm(t):
            xbm, bb = xbms[t % 2], Bxb[t % 2]
            for j in range(6):
                tr(psb(6)[:, j * 128:(j + 1) * 128], xact[:, j, t * 128:(t + 1) * 128], identB, [B_xact, B_cst], [BP[6]])
            cp(A_, xbm, psb(6)[:, 0:768], [BP[6]], [bb])
            return xbm, bb

        for t in range(NT):
            xbm, bb = xs_bm_tm(t)
            tt(V, a8, dtall[:, t, :], aneg_b, MUL, [B_dt, B_par], [Ba])
            mm(psf(3)[:, 0:8], triS, a8, True, True, [Ba, B_cst], [BP[3]])
            mm(psf(3)[:, 8:16], onesF, a8, True, True, [Ba, B_cst], [BP[3]])
            act(dte8, psf(3)[:, 0:8], AF.Exp, [BP[3]], [Ba])
            act(dec8, psf(3)[:, 8:16], AF.Exp, [BP[3]], [Ba])
            act(sm[:, 48:56], psf(3)[:, 8:16], AF.Copy, [BP[3]], [Ba])
            tt(V, Ltot, Ltot, sm[:, 48:56], ADD, [Ba, B_h], [B_h])
            tt(V, w8, dtall[:, t, :], dte8, MUL, [B_dt, Ba], [Ba])
            tt(V, xw.rearrange("p (h d) -> p h d", h=8), xbm[:, 0:512].rearrange("p (h d) -> p h d", h=8),
               bc(w8.unsqueeze(2), [128, 8, 64]), MUL, [Ba, bb], [Bxw])
            for g in range(2):
                mm(psf(2)[:, g * 256:(g + 1) * 256], xbm[:, 512 + g * 128:512 + (g + 1) * 128],
                   xw[:, g * 256:(g + 1) * 256], True, True, [bb, Bxw], [BP[2]])
            tt(V, hT.rearrange("p (h d) -> p h d", h=8), hT.rearrange("p (h d) -> p h d", h=8),
               bc(dec8.unsqueeze(2), [128, 8, 64]), MUL, [Ba, B_h], [B_h])
            tt(V, hT, hT, psf(2), ADD, [BP[2], B_h], [B_h])
        P.dma("sp", bn2[0:128, :], hT, r=[B_h], w=[B_bn2])
        P.dma("sp", bn2[128:129, 0:8], Ltot[0:1, :], r=[B_h], w=[B_bn2])
        P.cc(lambda e: e.collective_compute("AllGather", ALU.bypass, replica_groups=RG,
                                            ins=[bn2[:, :]], outs=[g2[:, :]]), r=[B_bn2], w=[B_g2])
        Sall = AR.f32(8 * 512).rearrange("p (r c) -> p r c", r=8)
        B_S = Buf()
        P.dma("sp", Sall, g2.rearrange("(r k) c -> k r c", k=129)[0:128], r=[B_g2], w=[B_S])
        AX2 = Arena(ARh, state["mark"])
        AX2.off = state["markA"]
        Lall = AX2.f32(64).rearrange("p (r h) -> p r h", r=8)
        Dm = AX2.f32(64).rearrange("p (r h) -> p r h", r=8)
        P.dma("sp", Lall, bass.AP(g2.tensor, 128 * 512, [[0, 128], [129 * 512, 8], [1, 8]]), r=[B_g2, B_xbc], w=[B_S])
        tt(V, Lall, Lall, bc(rmask.unsqueeze(2), [128, 8, 8]), MUL, [B_S, B_cst], [B_S])
        act(Dm, Lall, AF.Exp, [B_S], [B_S])
        P.op(V, lambda e: e.memset(hT, 0.0), [B_bn2], [B_h])
        h3 = hT.rearrange("p (h d) -> p h d", h=8)
        for j in range(NCORES):
            tt(V, h3, h3, bc(Dm[:, j, :].unsqueeze(2), [128, 8, 64]), MUL, [B_S, B_h], [B_h])
            stt(V, hT, Sall[:, j, :], rmask[:, j:j + 1], hT, MUL, ADD, [B_S, B_cst, B_h], [B_h])
        cp(A_, hTb, hT, [B_h], [B_h])
        Brhs = AX2.f32(1024)
        eseg = AX2.bf16(1024)
        MT = AX2.bf16(1024)
        cbm = AX2.bf16(256)
        xdt = AX2.bf16(512)
        ytmp = AX2.f32(512)
        sk = AX2.f32(512)
        zsts = [AX2.bf16(512), AX2.bf16(512)]
        junk = AX2.bf16(512)
        sm3 = AX2.f32(16)
        eac8 = sm[:, 32:40]
        ss = sm3[:, 0:1]
        Bz = [Buf(), Buf()]
        BBr, Bes, Bcb, BMT, Bxd, By = Buf(), Buf(), Buf(), Buf(), Buf(), Buf()

        def gate_norm(t, yt, xs_tm, bxs, bufs):
            zst, bz = zsts[t % 2], Bz[t % 2]
            P.dma("sp", zst, zs_scr[t], r=[B_zs], w=[bz])
            tt(V, sk.rearrange("p (h d) -> p h d", h=8), xs_tm.rearrange("p (h d) -> p h d", h=8),
               bc(dsk_b.unsqueeze(2), [128, 8, 64]), MUL, bufs + bxs + [B_par], bufs)
            tt(V, yt, yt, sk, ADD, bufs, bufs)
            tt(V, yt, yt, zst, MUL, bufs + [bz], bufs)
            P.op(V, lambda e: e.memset(ss, 0.0), [], bufs)
            act(junk, yt, AF.Square, bufs, bufs, accum=ss)
            act(ss, ss, AF.Sqrt, bufs, bufs, bias=EPS, scale=1.0 / 512)
            rcp(ss, ss, bufs, bufs)
            stt(V, mo_all[:, t, :], yt, ss, ssmn, MUL, MUL, bufs + [B_par], [B_mo])

        Bt = By
        MT2 = [MT, AX2.bf16(1024)]
        xdt2 = [xdt, AX2.bf16(512)]
        xw2 = [xw, AX2.bf16(512)]
        smX = [AX2.f32(40), AX2.f32(40)]
        BMT2, Bxd2, Bxw2, Ba2 = [BMT, Buf()], [Bxd, Buf()], [Bxw, Buf()], [Buf(), Buf()]

        def stX(t):
            k2 = t % 2
            cols = slice(t * 128, (t + 1) * 128)
            sx = smX[k2]
            a8_, dte_, dec_, eac_ = sx[:, 0:8], sx[:, 8:16], sx[:, 16:24], sx[:, 24:32]
            ba = Ba2[k2]
            xbm, bb = xs_bm_tm(t)
            tt(V, a8_, dtall[:, t, :], aneg_b, MUL, [B_dt, B_par], [ba])
            mm(psf(3)[:, 0:8], triS, a8_, True, True, [ba, B_cst], [BP[3]])
            mm(psf(3)[:, 8:16], onesF, a8_, True, True, [ba, B_cst], [BP[3]])
            mm(psf(3)[:, 16:24], triU, a8_, True, True, [ba, B_cst], [BP[3]])
            tt(G, Brhs.rearrange("p (h l) -> p h l", h=8), bc(triU.unsqueeze(1), [128, 8, 128]),
               bc(a8_.unsqueeze(2), [128, 8, 128]), MUL, [ba, B_cst], [BBr])
            act(dte_, psf(3)[:, 0:8], AF.Exp, [BP[3]], [ba])
            act(dec_, psf(3)[:, 8:16], AF.Exp, [BP[3]], [ba])
            act(eac_, psf(3)[:, 16:24], AF.Exp, [BP[3]], [ba])
            x3 = xbm[:, 0:512].rearrange("p (h d) -> p h d", h=8)
            tt(V, xdt2[k2].rearrange("p (h d) -> p h d", h=8), x3, bc(dtall[:, t, :].unsqueeze(2), [128, 8, 64]), MUL,
               [bb, B_dt], [Bxd2[k2]])
            tt(V, xw2[k2].rearrange("p (h d) -> p h d", h=8), xdt2[k2].rearrange("p (h d) -> p h d", h=8),
               bc(dte_.unsqueeze(2), [128, 8, 64]), MUL, [Bxd2[k2], ba], [Bxw2[k2]])
            mm(psf(4), triS, Brhs[:, 0:512], True, True, [BBr, B_cst], [BP[4]])
            mm(psf(5), triS, Brhs[:, 512:1024], True, True, [BBr, B_cst], [BP[5]])
            act(eseg[:, 0:512], psf(4), AF.Exp, [BP[4]], [Bes])
            act(eseg[:, 512:1024], psf(5), AF.Exp, [BP[5]], [Bes])
            for g in range(2):
                mm(psf(1)[:, g * 128:(g + 1) * 128], xact[:, 4 + g, cols], xact[:, 6 + g, cols], True, True,
                   [B_xact], [BP[1]])
            tt(V, cbm.rearrange("p (g l) -> p g l", g=2), psf(1)[:, 0:256].rearrange("p (g l) -> p g l", g=2),
               bc(causB.unsqueeze(1), [128, 2, 128]), MUL, [BP[1], B_cst], [Bcb])
            tt(V, MT2[k2].rearrange("p (g h l) -> p g h l", g=2, h=4), eseg.rearrange("p (g h l) -> p g h l", g=2, h=4),
               bc(cbm.rearrange("p (g l) -> p g l", g=2).unsqueeze(2), [128, 2, 4, 128]), MUL, [Bes, Bcb], [BMT2[k2]])

        def stY(t):
            k2 = t % 2
            cols = slice(t * 128, (t + 1) * 128)
            sx = smX[k2]
            dec_, eac_ = sx[:, 16:24], sx[:, 24:32]
            ba = Ba2[k2]
            xbm, bb = xbms[k2], Bxb[k2]
            for h in range(8):
                mm(psf(0)[:, h * 64:(h + 1) * 64], MT2[k2][:, h * 128:(h + 1) * 128], xdt2[k2][:, h * 64:(h + 1) * 64],
                   True, True, [BMT2[k2], Bxd2[k2]], [BP[0]])
            for g in range(2):
                mm(psf(2)[:, g * 256:(g + 1) * 256], xact[:, 6 + g, cols], hTb[:, g * 256:(g + 1) * 256], True, True,
                   [B_xact, B_h], [BP[2]])
            for g in range(2):
                mm(psf(7)[:, g * 256:(g + 1) * 256], xbm[:, 512 + g * 128:512 + (g + 1) * 128],
                   xw2[k2][:, g * 256:(g + 1) * 256], True, True, [bb, Bxw2[k2]], [BP[7]])
            tt(V, h3, h3, bc(dec_.unsqueeze(2), [128, 8, 64]), MUL, [ba, B_h, BP[2]], [B_h])
            tt(V, hT, hT, psf(7), ADD, [BP[7], B_h], [B_h])
            cp(A_, hTb, hT, [B_h], [B_h])
            y3 = ytmp.rearrange("p (h d) -> p h d", h=8)
            tt(V, y3, psf(2).rearrange("p (h d) -> p h d", h=8), bc(eac_.unsqueeze(2), [128, 8, 64]), MUL,
               [BP[2], ba], [By])
            tt(V, ytmp, ytmp, psf(0), ADD, [BP[0], By], [By])
            gate_norm(t, ytmp, xbm[:, 0:512], [bb], [By])

        stX(0)
        for t in range(NT):
            if t + 1 < NT:
                stX(t + 1)
            stY(t)
        xsS = Brhs[:, 0:512]
        for hh in range(2):
            P.dma("sp", ytmp[hh * 64:(hh + 1) * 64, :], sy_scr[:, :], r=[B_sy, By], w=[By])
            P.dma("sp", xsS[hh * 64:(hh + 1) * 64, :], sx_scr[:, 0:512], r=[B_sx, BBr], w=[BBr])
        gate_norm(16, ytmp, xsS, [BBr], [By])
        for j in range(4):
            tr(psf(7)[:, j * 128:(j + 1) * 128], hT[:, j * 128:(j + 1) * 128], identF, [B_h, B_cst], [BP[7]])
        cp(V, ytmp, psf(7), [BP[7]], [Bt])
        P.dma("sp", pssm[l].rearrange("(j p) n -> p j n", p=128), ytmp.rearrange("p (j n) -> p j n", j=4),
              r=[Bt], w=OUTW)


    def layer_P4(l):
        mo_all = state["mo_all"]
        AR.off = state["markA"]
        oT = AR.bf16(8 * 2176).rearrange("p (h t) -> p h t", h=8)
        mark3 = AR.off
        qT = AR.bf16(4 * 2176).rearrange("p (j t) -> p j t", j=4)
        kTC = AR.bf16(4096)
        kTA = AR.bf16(2176)
        acc = AR.f32(2 * 2048).rearrange("p (r t) -> p r t", r=2)
        NV = 4
        vr = [AR.bf16(256).rearrange("p (g c) -> p g c", g=2) for _ in range(NV)]
        Bv = [Buf() for _ in range(NV)]
        Et = [AR.bf16(512), AR.bf16(512)]
        BE = [Buf(), Buf()]
        rd = AR.f32(256)
        Brd = Buf()
        B_q, B_k, B_acc, B_o = Buf(), Buf(), Buf(), Buf()
        state.update(oT=oT, B_o=B_o, mark3=mark3)
        P.dma("sp", qT[:, :, 0:2048], qt_v[:, :, 0:2048], r=[B_qt], w=[B_q])
        P.dma("sp", kTC[:, 2048:4096], bn1_kc, r=[B_bn1], w=[B_k])
        dma_fn("sp", lambda e: e.dma_start(out=kTC[:, 0:2048],
                                           in_=prev_rows(e, 2048, 1024).rearrange("(p j) c -> p (j c)", j=8)),
               r=[B_g1], w=[B_k])
        dma_fn("sp", lambda e: e.dma_start(out=kTA[:, 0:128],
                                           in_=prev_rows(e, 3072, 64).rearrange("r (a c) -> (r a) c", a=2)),
               r=[B_g1], w=[B_k])
        P.dma("sp", kTA[:, 128:2176], kta_scr[:, :], r=[B_kta], w=[B_k])
        for i in range(NV):
            P.op(G, lambda e, i=i: e.memset(vr[i][:, :, 64:128], 1.0), [], [Bv[i]])
        dma_fn("sp", lambda e: e.dma_start(out=vprev[:, :], in_=prev_rows(e, 0, 2048)), r=[B_g1], w=[B_vprev])
        cnt = {"v": 0, "e": 0}

        def load_v(d, r, b, voff):
            i = cnt["v"] % NV
            cnt["v"] += 1
            B = 128 * d
            if b >= 0:
                src = bn1[b * B + r:b * B + r + 127 * d + 1:d, voff:voff + 128]
                P.dma("sp", vr[i][:, :, 0:64], src.rearrange("p (g c) -> p g c", g=2), r=[B_bn1], w=[Bv[i]])
            else:
                st = 2048 - B + r
                src = vprev[st:st + 127 * d + 1:d, voff:voff + 128]
                P.dma("sp", vr[i][:, :, 0:64], src.rearrange("p (g c) -> p g c", g=2), r=[B_vprev], w=[Bv[i]])
            return i

        def unit(kT, koff, qj, d, r, b, g, vi_prev, vi_cur, mask, is_A):
            B = 128 * d
            q0 = b * B + r
            qs = slice(q0, q0 + 127 * d + 1, d)
            kc = slice(koff + q0, koff + q0 + 127 * d + 1, d)
            kp = slice(koff + q0 - B, koff + q0 - B + 127 * d + 1, d)
            gs = slice(g * 64, (g + 1) * 64)
            ei = cnt["e"] % 2
            cnt["e"] += 1
            E = Et[ei]
            for kb, ks in enumerate((kp, kc)):
                mm(psf(kb)[:, 0:256].rearrange("p (r t) -> p r t", r=2), kT[gs, ks], qT[gs, qj:qj + 2, qs], True, True,
                   [B_k, B_q], [BP[kb]])
                act(E[:, kb * 256:(kb + 1) * 256], psf(kb)[:, 0:256], AF.Exp, [BP[kb]], [BE[ei]], scale=0.125)
            tt(V, E, E, mask, MUL, [BE[ei], B_cst], [BE[ei]])
            ob = 2 + (cnt["e"] % 2)
            mm(psf(ob)[:, 0:256], vr[vi_prev][:, g, :], E[:, 0:256], True, False, [BE[ei], Bv[vi_prev]], [BP[ob]])
            mm(psf(ob)[:, 0:256], vr[vi_cur][:, g, :], E[:, 256:512], False, True, [BE[ei], Bv[vi_cur]], [BP[ob]])
            O = psf(ob)[:, 0:256].rearrange("p (r t) -> p r t", r=2)
            if is_A:
                for rr in range(2):
                    h = 2 * g + rr
                    ts(V, rd[0:64, rr * 128:(rr + 1) * 128], O[64:128, rr, :], esink_b[64:128, h:h + 1], None, ADD, None,
                       [BP[ob], B_par], [Brd])
                rcp(rd[0:64, :], rd[0:64, :], [Brd], [Brd])
                tt(V, oT[0:64, 2 * g:2 * g + 2, qs], O[0:64, :, :], rd[0:64, :].rearrange("p (r t) -> p r t", r=2), MUL,
                   [BP[ob], Brd], [B_o])
            else:
                if d == 1:
                    cp(V, acc[:, :, qs], O, [BP[ob]], [B_acc])
                else:
                    tt(V, acc[:, :, qs], acc[:, :, qs], O, ADD, [BP[ob], B_acc], [B_acc])

        for g in range(2):
            vi_p = load_v(1, 0, -1, 0)
            for b in range(16):
                vcur = load_v(1, 0, b, 0)
                unit(kTA, 128, 0, 1, 0, b, g, vi_p, vcur, mA0 if b == 0 else mA, True)
                vi_p = vcur
        for g in range(2):
            for d in (1, 4, 16):
                nb = 16 // d
                for r in range(d):
                    vi_p = load_v(d, r, -1, 128)
                    for b in range(nb):
                        vcur = load_v(d, r, b, 128)
                        unit(kTC, 2048, 2, d, r, b, g, vi_p, vcur, mC0 if b == 0 else mC, False)
                        vi_p = vcur
            for rr in range(2):
                for c0 in range(0, 2048, 256):
                    cs = slice(c0, c0 + 256)
                    rcp(rd[0:64, :], acc[64:128, rr, cs], [B_acc], [Brd])
                    tt(V, oT[0:64, 4 + 2 * g + rr, cs], acc[0:64, rr, cs], rd[0:64, :], MUL, [B_acc, Brd], [B_o])


    def layer_S(l):
        xbS, dtall = state["xbS"], state["dtall"]
        AR.off = state["mark"]
        mk0 = AR.off
        ql = AR.f32(256).rearrange("p (j d) -> p j d", j=4)
        B_ql = Buf()
        for g in range(2):
            P.dma("sp", ql[g * 64:(g + 1) * 64, :, :],
                  sq_scr[g * 64:(g + 1) * 64, :].rearrange("p (j gg d) -> p j gg d", j=4, gg=2)[:, :, g, :],
                  r=[B_sq], w=[B_ql])
        Kbs = [AR.bf16(129 * 64).rearrange("p (k d) -> p k d", d=64), AR.bf16(129 * 64).rearrange("p (k d) -> p k d", d=64)]
        Vb = AR.bf16(129 * 64).rearrange("p (k d) -> p k d", d=64)
        prod = AR.bf16(65 * 64).rearrange("p (k d) -> p k d", d=64)
        sc = AR.f32(2 * 132).rearrange("p (r k) -> p r k", r=2)
        eb = AR.bf16(2 * 132).rearrange("p (r k) -> p r k", r=2)
        sm = AR.f32(64)
        numA = AR.f32(128).rearrange("p (r d) -> p r d", r=2)
        numC = AR.f32(3 * 128).rearrange("p (q r d) -> p q r d", q=3, r=2)
        numh = AR.f32(128).rearrange("p (r d) -> p r d", r=2)
        stC = AR.f32(3 * 4).rearrange("p (q r) -> p q r", q=3)
        oC = AR.f32(128).rearrange("p (r d) -> p r d", r=2)
        B_Ks, B_V, Bt = [Buf(), Buf()], Buf(), Buf()
        rdeps = {id(cak): [], id(cav): [], id(cck): [], id(ccv): [],
                 id(sak): B_cpy[(l, "ak")] + [B_new[(l, "ak")]], id(sav): B_cpy[(l, "av")] + [B_new[(l, "av")]],
                 id(sck): B_cpy[(l, "ck")] + [B_new[(l, "ck")]], id(scv): B_cpy[(l, "cv")] + [B_new[(l, "cv")]]}

        def load_k(pi, specs):
            for g in range(2):
                for t in range(4):
                    lanes = slice(g * 64 + t * 16, g * 64 + (t + 1) * 16)
                    for (j0, n, sk_, sv_, rf) in specs:
                        P.dma("pool", Kbs[pi % 2][lanes, j0:j0 + n, :], sk_[l, :, rf(t), g * 64:(g + 1) * 64],
                              r=rdeps[id(sk_)], w=[B_Ks[pi % 2]])

        def load_v(pi, specs):
            for g in range(2):
                for t in range(4):
                    lanes = slice(g * 64 + t * 16, g * 64 + (t + 1) * 16)
                    for (j0, n, sk_, sv_, rf) in specs:
                        P.dma("pool", Vb[lanes, j0:j0 + n, :], sv_[l, :, rf(t), g * 64:(g + 1) * 64],
                              r=rdeps[id(sv_)], w=[B_V])

        def scores(pi, nk, qj, nm, den):
            Kb, B_K = Kbs[pi % 2], B_Ks[pi % 2]
            halves = ((0, 65), (65, nk))
            for r in range(2):
                for (k0, k1) in halves:
                    tt(V, prod[:, 0:k1 - k0, :], Kb[:, k0:k1, :], bc(ql[:, qj + r, :].unsqueeze(1), [128, k1 - k0, 64]), MUL,
                       [B_K, B_ql, Bt], [Bt])
                    red(V, sc[:, r, k0:k1], prod[:, 0:k1 - k0, :], ADD, [Bt], [Bt])
                red(V, nm[:, r:r + 1], sc[:, r, 0:nk], MAXOP, [Bt], [Bt])
            ts(V, nm, nm, -0.125, None, MUL, None, [Bt], [Bt])
            P.op(V, lambda e: e.memset(den, 0.0), [Bt], [Bt])
            for r in range(2):
                act(eb[:, r, 0:nk], sc[:, r, 0:nk], AF.Exp, [Bt], [Bt], bias=nm[:, r:r + 1], scale=0.125,
                    accum=den[:, r:r + 1])

        def pv(nk, num):
            halves = ((0, 65), (65, nk))
            for r in range(2):
                for hi, (k0, k1) in enumerate(halves):
                    tt(V, prod[:, 0:k1 - k0, :], Vb[:, k0:k1, :], bc(eb[:, r, k0:k1].unsqueeze(2), [128, k1 - k0, 64]), MUL,
                       [B_V, Bt], [Bt])
                    red(V, (num if hi == 0 else numh)[:, r, :], prod[:, 0:k1 - k0, :].rearrange("p k d -> p d k"), ADD,
                        [Bt], [Bt])
                tt(V, num[:, r, :], num[:, r, :], numh[:, r, :], ADD, [Bt], [Bt])

        specA = [(0, 124, cak, cav, lambda t: slice(t + 1, t + 125)),
                 (124, 4, sak, sav, lambda t: slice(t + 121, t + 125))]
        pats = [
            [(0, 125, cck, ccv, lambda t: slice(t + 1920, t + 2045)), (125, 4, sck, scv, lambda t: slice(t + 2041, t + 2045))],
            [(0, 128, cck, ccv, lambda t: slice(1536 + t, 1536 + t + 4 * 127 + 1, 4)), (128, 1, sck, scv, lambda t: slice(2044 + t, 2045 + t))],
            [(0, 128, cck, ccv, lambda t: slice(t, t + 16 * 127 + 1, 16)), (128, 1, sck, scv, lambda t: slice(2044 + t, 2045 + t))],
        ]
        allp = [(specA, 128, 0)] + [(p_, 129, 2) for p_ in pats]
        nmA, denA, t2 = sm[:, 0:2], sm[:, 2:4], sm[:, 4:6]
        outs = [(numA, nmA, denA)] + [(numC[:, q_], stC[:, q_, 0:2], stC[:, q_, 2:4]) for q_ in range(3)]
        load_k(0, allp[0][0])
        load_v(0, allp[0][0])
        for pi, (specs, nk, qj) in enumerate(allp):
            if pi + 1 < 4:
                load_k(pi + 1, allp[pi + 1][0])
            num, nm, den = outs[pi]
            scores(pi, nk, qj, nm, den)
            pv(nk, num)
            if pi + 1 < 4:
                load_v(pi + 1, allp[pi + 1][0])
        for r in range(2):
            act(t2[:, r:r + 1], nmA[:, r:r + 1], AF.Exp, [Bt, B_par], [Bt], bias=sinkl[:, r:r + 1])
        tt(V, denA, denA, t2, ADD, [Bt], [Bt])
        rcp(denA, denA, [Bt], [Bt])
        tt(V, numA, numA, bc(denA.unsqueeze(2), [128, 2, 64]), MUL, [Bt], [Bt])
        for g in range(2):
            P.dma("sp", smix[:, g * 128:(g + 1) * 128], numA[g * 64:(g + 1) * 64, :, :], r=[Bt], w=[B_smix])
        mn, w3 = sm[:, 8:10], sm[:, 10:16].rearrange("p (q r) -> p q r", q=3)
        tt(V, mn, stC[:, 0, 0:2], stC[:, 1, 0:2], ALU.min, [Bt], [Bt])
        tt(V, mn, mn, stC[:, 2, 0:2], ALU.min, [Bt], [Bt])
        for q_ in range(3):
            for r in range(2):
                act(w3[:, q_, r:r + 1], stC[:, q_, r:r + 1], AF.Exp, [Bt], [Bt], bias=mn[:, r:r + 1], scale=-1.0)
        dn = sm[:, 16:18]
        tt(V, stC[:, :, 2:4], stC[:, :, 2:4], w3, MUL, [Bt], [Bt])
        tt(V, dn, stC[:, 0, 2:4], stC[:, 1, 2:4], ADD, [Bt], [Bt])
        tt(V, dn, dn, stC[:, 2, 2:4], ADD, [Bt], [Bt])
        rcp(dn, dn, [Bt], [Bt])
        for q_ in range(3):
            tt(V, numC[:, q_], numC[:, q_], bc(w3[:, q_, :].unsqueeze(2), [128, 2, 64]), MUL, [Bt], [Bt])
        tt(V, oC, numC[:, 0], numC[:, 1], ADD, [Bt], [Bt])
        tt(V, oC, oC, numC[:, 2], ADD, [Bt], [Bt])
        tt(V, oC, oC, bc(dn.unsqueeze(2), [128, 2, 64]), MUL, [Bt], [Bt])
        for g in range(2):
            P.dma("sp", smix[:, 256 + g * 128:256 + (g + 1) * 128], oC[g * 64:(g + 1) * 64, :, :], r=[Bt], w=[B_smix])

        P.barrier()
        AR.off = mk0
        pre = AR.f32(1024)
        xcS = AR.bf16(8 * 112).rearrange("p (c s b) -> p c s b", c=8, s=7)
        acc = AR.f32(512).rearrange("p (c k) -> p c k", c=8)
        xaS = AR.bf16(512).rearrange("p (c k) -> p c k", c=8)
        xaT = AR.f32(1024)
        Bc = Buf()
        P.dma("sp", pre[0:48, :], sconv[l], w=[Bc])
        for c in range(8):
            tr(psf(4 + c // 4)[:, (c % 4) * 48:(c % 4 + 1) * 48], pre[0:48, c * 128:(c + 1) * 128], identF[0:48, 0:48],
               [Bc, B_cst], [BP[4 + c // 4]])
        for bk in range(2):
            cp(V, xcS[:, 4 * bk:4 * bk + 4, 0:3, :],
               psf(4 + bk)[:, 0:192].rearrange("p (c b j) -> p c j b", c=4, j=3), [BP[4 + bk]], [Bc])
        cp(V, xcS[:, :, 3:7, :].rearrange("p c s b -> p c (s b)"), xbS, [state["B_xbc"]], [Bc])
        xf = xcS.rearrange("p c s b -> p c (s b)")
        for c in range(8):
            ts(V, acc[:, c, :], xf[:, c, 0:64], convw[:, c, 0:1], None, MUL, None, [Bc, B_par], [Bc])
            for j in range(1, 4):
                stt(V, acc[:, c, :], xf[:, c, 16 * j:16 * j + 64], convw[:, c, j:j + 1], acc[:, c, :], MUL, ADD,
                    [Bc, B_par], [Bc])
            act(xaS[:, c, :], acc[:, c, :], AF.Silu, [Bc, B_par], [Bc], bias=convb[:, c:c + 1])
        for c in range(8):
            tr(psb(6)[0:64, c * 128:(c + 1) * 128], xaS[:, c, :], identB, [Bc, B_cst], [BP[6]])
        cp(V, xaT[0:64, :], psb(6)[0:64, 0:1024], [BP[6]], [Bc])
        P.dma("sp", sx_scr[:, :], xaT[0:64, :], r=[Bc], w=[B_sx])
        P.barrier()
        AR.off = mk0
        xl = AR.f32(256).rearrange("p (t d) -> p t d", t=4)
        bl = AR.f32(512).rearrange("p (t n) -> p t n", t=4)
        cl = AR.f32(512).rearrange("p (t n) -> p t n", t=4)
        yl = AR.f32(256).rearrange("p (t d) -> p t d", t=4)
        dl = AR.f32(4)
        s2 = AR.f32(16)
        xd = AR.f32(32)
        hS = AR.f32(4096).rearrange("p (d n) -> p d n", d=32)
        tmp = AR.f32(4096).rearrange("p (d n) -> p d n", d=32)
        Bl, Bh, Bs = Buf(), Buf(), Buf()
        for t in range(4):
            P.dma("sp", xl[:, t, :], bass.AP(sx_scr.tensor, t * 16 * 1024, [[64, 8], [1024, 16], [1, 64]]), r=[B_sx], w=[Bl])
            for g in range(2):
                P.dma("sp", bl[g * 64:(g + 1) * 64, t, :],
                      bass.AP(sx_scr.tensor, t * 16 * 1024 + 512 + g * 128, [[0, 4], [1024, 16], [1, 128]]), r=[B_sx], w=[Bl])
                P.dma("sp", cl[g * 64:(g + 1) * 64, t, :],
                      bass.AP(sx_scr.tensor, t * 16 * 1024 + 768 + g * 128, [[0, 4], [1024, 16], [1, 128]]), r=[B_sx], w=[Bl])
        for t in range(4):
            P.dma("sp", dl[:, t:t + 1], bass.AP(sdt_scr.tensor, t * 128, [[1, 8], [8, 16], [1, 1]]), r=[B_sdt], w=[Bl],
                  allow_slow_non_contiguous=True)
        a4, dA4 = s2[:, 0:4], s2[:, 4:8]
        ts(V, a4, dl, anegl, None, MUL, None, [Bl, B_par], [Bs])
        act(dA4, a4, AF.Exp, [Bs], [Bs])
        for ph in range(2):
            st_in = bass.AP(sssm.tensor, l * SB * 8 * 8192 + ph * 4096, [[8192, 8], [65536, 16], [1, 4096]])
            st_out = bass.AP(sssm_o.tensor, l * SB * 8 * 8192 + ph * 4096, [[8192, 8], [65536, 16], [1, 4096]])
            P.dma("sp", hS.rearrange("p d n -> p (d n)"), st_in, w=[Bh])
            for t in range(4):
                ts(V, xd, xl[:, t, ph * 32:(ph + 1) * 32], dl[:, t:t + 1], None, MUL, None, [Bl], [Bs])
                tt(V, tmp, bc(xd.unsqueeze(2), [128, 32, 128]), bc(bl[:, t, :].unsqueeze(1), [128, 32, 128]), MUL,
                   [Bs, Bl], [Bs])
                stt(V, hS, hS, dA4[:, t:t + 1], tmp, MUL, ADD, [Bs, Bh], [Bh])
                tt(V, tmp, hS, bc(cl[:, t, :].unsqueeze(1), [128, 32, 128]), MUL, [Bh, Bl, Bs], [Bs])
                red(V, yl[:, t, ph * 32:(ph + 1) * 32], tmp, ADD, [Bs], [Bs])
            P.dma("sp", st_out, hS.rearrange("p d n -> p (d n)"), r=[Bh], w=OUTW)
        for t in range(4):
            P.dma("sp", bass.AP(sy_scr.tensor, t * 16 * 512, [[64, 8], [512, 16], [1, 64]]), yl[:, t, :], r=[Bs], w=[B_sy])

    def layer_P4s(l):
        oT, B_o = state["oT"], state["B_o"]
        AR.off = state["mark3"]
        sf = AR.f32(512)
        sbf = AR.bf16(512)
        Bt = Buf()
        for hh in range(2):
            P.dma("sp", sf[hh * 64:(hh + 1) * 64, :], smix[:, :], r=[B_smix], w=[Bt])
        cp(V, sbf, sf, [Bt], [Bt])
        for hi in range(8):
            tr(psb(7)[0:64, hi * 128:(hi + 1) * 128], sbf[:, hi * 64:(hi + 1) * 64], identB, [Bt, B_cst], [BP[7]])
        cp(V, oT[0:64, :, 2048:2176], psb(7)[0:64, 0:1024].rearrange("p (h t) -> p h t", h=8), [BP[7]], [B_o])

    def layer_P5(l, tiles):
        mo_all, B_mo, oT, B_o = state["mo_all"], state["B_mo"], state["oT"], state["B_o"]
        AR.off = state["mark3"]
        WoA = AR.bf16(8 * 1024).rearrange("p (h n) -> p h n", h=8)
        WoM = AR.bf16(4 * 1024).rearrange("p (c n) -> p c n", c=4)
        moT = AR.bf16(512)
        B_wo, Bt = Buf(), Buf()
        P.dma("pool", WoA[0:64, 0:4, :], w_out[l, 0:256, :].rearrange("(h p) n -> p h n", p=64), w=[B_wo])
        P.dma("pool", WoA[0:64, 4:8, :], w_out[l, 768:1024, :].rearrange("(h p) n -> p h n", p=64), w=[B_wo])
        for c in range(4):
            P.dma("pool", WoM[:, c, :], w_out[l, 256 + c * 128:256 + (c + 1) * 128, :], w=[B_wo])
        for t in tiles:
            for j in range(4):
                tr(psb(6)[:, j * 128:(j + 1) * 128], mo_all[:, t, j * 128:(j + 1) * 128], identB, [B_mo, B_cst], [BP[6]])
            cp(A_, moT, psb(6)[:, 0:512], [BP[6]], [Bt])
            hx, bx = (H[:, t, :], BH[t]) if t < 16 else (HS, BH[16])
            for half in range(2):
                hs_ = slice(half * 512, (half + 1) * 512)
                for hi in range(8):
                    mm(psf(half), oT[0:64, hi, t * 128:(t + 1) * 128], WoA[0:64, hi, hs_], hi == 0, False,
                       [B_o, B_wo], [BP[half]])
                for c in range(4):
                    mm(psf(half), moT[:, c * 128:(c + 1) * 128], WoM[:, c, hs_], False, c == 3, [Bt, B_wo], [BP[half]])
                tt(V, hx[:, hs_], hx[:, hs_], psf(half), ADD, [BP[half], bx], [bx])

    def layer_P6(l, tiles):
        AR.reset()

        uT2 = AR.bf16(17 * 1024).rearrange("p (t c k) -> p t c k", t=17, c=8)
        ubf = AR.bf16(1024)
        junk = AR.bf16(1024)
        sm = AR.f32(32)
        NRING = 3
        ring = [(AR.bf16(8 * 256).rearrange("p (c f) -> p c f", c=8), AR.bf16(8 * 256).rearrange("p (c f) -> p c f", c=8),
                 AR.bf16(2 * 1024).rearrange("p (c n) -> p c n", c=2), Buf()) for _ in range(NRING)]
        sg = [AR.f32(512), AR.f32(512)]
        aT = [AR.bf16(1024).rearrange("p (c k) -> p c k", c=2), AR.bf16(1024).rearrange("p (c k) -> p c k", c=2)]
        Bsg = [Buf(), Buf()]
        BaT = [Buf(), Buf()]
        B_u, Bt = Buf(), Buf()
        for t in tiles:
            hx, bx = (H[:, t, :], BH[t]) if t < 16 else (HS, BH[16])
            rmsnorm_T(hx, bx, g2b, ubf, uT2[:, t].rearrange("p c k -> p (c k)"), sm, junk, [Bt])
        P.op(V, lambda e: e.memset(sm[:, 0:1], 0.0), [Bt], [B_u])
        blocks = [[t for t in tiles if t // 4 == b] for b in range(5)]
        blocks = [b for b in blocks if b]
        ng = DFF // 256
        k = 0
        for gi in range(ng):
            Wg, Wu, Wd, Bw = ring[gi % NRING]
            f0 = gi * 256
            P.dma("pool", Wg, w_gate[l, :, f0:f0 + 256].rearrange("(c p) f -> p c f", p=128), w=[Bw])
            P.dma("pool", Wu, w_up[l, :, f0:f0 + 256].rearrange("(c p) f -> p c f", p=128), w=[Bw])
            P.dma("pool", Wd, w_down[l, f0:f0 + 256, :].rearrange("(c p) n -> p c n", p=128), w=[Bw])
            for blk in blocks:
                nt = len(blk)
                ntok = 128 * nt
                t0 = blk[0]
                i2 = k % 2
                k += 1
                for fc in range(2):
                    for (W_, bk) in ((Wg, fc), (Wu, 2 + fc)):
                        for c in range(8):
                            mm(psf(bk)[:, 0:ntok].rearrange("p (t k) -> p t k", t=nt), W_[:, c, fc * 128:(fc + 1) * 128],
                               uT2[:, t0:t0 + nt, c, :], c == 0, c == 7, [Bw, B_u], [BP[bk]])
                    act(sg[i2][:, 0:ntok], psf(fc)[:, 0:ntok], AF.Silu, [BP[fc]], [Bsg[i2]])
                    tt(V, aT[i2][:, fc, 0:ntok], sg[i2][:, 0:ntok], psf(2 + fc)[:, 0:ntok], MUL, [Bsg[i2], BP[2 + fc]], [BaT[i2]])
                for ti, t in enumerate(blk):
                    hx, bx = (H[:, t, :], BH[t]) if t < 16 else (HS, BH[16])
                    for half in range(2):
                        bk = 4 + 2 * (ti % 2) + half
                        hs_ = slice(half * 512, (half + 1) * 512)
                        for fc in range(2):
                            mm(psf(bk), aT[i2][:, fc, ti * 128:(ti + 1) * 128], Wd[:, fc, hs_], fc == 0, fc == 1,
                               [BaT[i2], Bw], [BP[bk]])
                        tt(V, hx[:, hs_], hx[:, hs_], psf(bk), ADD, [BP[bk], bx], [bx])

    def write_y():
        for t in range(NT):
            P.dma("sp", yp[t * 128:(t + 1) * 128, :], H[:, t, :], r=[BH[t]], w=OUTW)
        P.dma("sp", ys[:, :], HS[0:64, :], r=[BH[16]], w=OUTW)

    PT = list(range(NT))
    for l in range(DEPTH if STAGE >= 2 else 1):
        layer_P1(l)
        P.barrier()
        layer_S(l)
        P.barrier()
        layer_P23(l)
        P.barrier()
        layer_P4(l)
        P.barrier()
        layer_P4s(l)
        P.barrier()
        layer_P5(l, PT + [16])
        P.barrier()
        layer_P6(l, PT + [16])
        P.barrier()
    write_y()
    P.finish(es)
    return nc, P, es


def _perm_cols():
    idx = []
    for base in (0, 640):
        for r in range(2):
            for g in range(2):
                h = 2 * g + r
                idx += list(range(base + h * 64, base + (h + 1) * 64))
    return idx


def _w_in_perm():
    ref = dict(aq=0, ak=256, av=384, cq=512, ck=768, cv=896, z=1024, xbc=1536, dt=2560)
    idx = []
    for nm in ("aq", "cq"):
        for r in range(2):
            for g in range(2):
                h = 2 * g + r
                idx += list(range(ref[nm] + h * 64, ref[nm] + (h + 1) * 64))
    idx += list(range(ref["ak"], ref["ak"] + 128))
    idx += list(range(ref["ck"], ref["ck"] + 128))
    idx += list(range(ref["av"], ref["av"] + 128))
    idx += list(range(ref["cv"], ref["cv"] + 128))
    idx += list(range(ref["z"], ref["z"] + 512))
    idx += list(range(ref["dt"], ref["dt"] + 8))
    idx += list(range(ref["xbc"], ref["xbc"] + 1024))
    assert len(idx) == NIN
    return np.array(idx)


def _consts(c):
    cf = np.zeros((128, 1344), np.float32)
    i = np.arange(128)
    cf[:, 0:128] = np.eye(128, dtype=np.float32)
    cf[:, 128:256] = (i[:, None] <= i[None, :])
    cf[:, 256:384] = (i[:, None] > i[None, :])
    cf[:, 384:512] = 1.0
    half = 8
    inv = (np.float32(500000.0) ** (-(np.arange(half, dtype=np.float32) * np.float32(2.0) / np.float32(16)))).astype(np.float32)
    pos = np.zeros((128, 17), np.float32)
    for t in range(16):
        pos[:, t] = c * TOK + t * 128 + i
    pos[:, 16] = 16384 + (i % 64) // 16
    ang = pos[:, :, None] * inv[None, None, :]
    cf[:, 512:648] = np.cos(ang).astype(np.float32).reshape(128, 136)
    cf[:, 648:784] = np.sin(ang).astype(np.float32).reshape(128, 136)
    co, si = np.cos(ang).astype(np.float32), np.sin(ang).astype(np.float32)
    cf[:, 800:1072] = np.concatenate([co, si], axis=2).reshape(128, 272)
    cf[:, 1072:1344] = np.concatenate([si, co], axis=2).reshape(128, 272)
    cf[:, 784] = 1.0 if c > 0 else 0.0
    cf[:, 785:793] = (np.arange(8)[None, :] < c).astype(np.float32)
    cb = np.zeros((128, 2304), np.float32)
    cb[:, 0:128] = np.eye(128)
    cur = (i[:, None] <= i[None, :]).astype(np.float32)
    prevA = (i[:, None] >= i[None, :] + 1).astype(np.float32)
    prevC = (i[:, None] >= i[None, :]).astype(np.float32)
    pvf = 1.0 if c > 0 else 0.0
    cb[:, 128:640] = np.concatenate([prevA * pvf, prevA * pvf, cur, cur], axis=1)
    cb[:, 640:1152] = np.concatenate([prevA, prevA, cur, cur], axis=1)
    cb[:, 1152:1664] = np.concatenate([prevC * pvf, prevC * pvf, cur, cur], axis=1)
    cb[:, 1664:2176] = np.concatenate([prevC, prevC, cur, cur], axis=1)
    cb[:, 2176:2304] = cur
    return cf, cb.astype(ml_dtypes.bfloat16)


def _params(inp):
    par = np.zeros((DEPTH, 128, 3456), np.float32)
    p = np.arange(128)
    for l in range(DEPTH):
        a = par[l]
        a[:, 0:1024] = inp["norm1"][l][None, :]
        a[:, 1024:2048] = inp["norm2"][l][None, :]
        a[:, 2048:2560] = np.concatenate([np.tile(inp["a_qn"][l], 4), np.tile(inp["c_qn"][l], 4)])[None, :]
        a[:, 2560:2816] = np.concatenate([np.tile(inp["a_kn"][l], 2), np.tile(inp["c_kn"][l], 2)])[None, :]
        a[:, 2816:3328] = inp["ssm_norm"][l][None, :]
        a[:, 3328:3332] = inp["a_sinks"][l][None, :]
        a[:, 3332:3340] = inp["dt_bias"][l][None, :]
        a[:, 3340:3348] = inp["a_log"][l][None, :]
        a[:, 3348:3356] = inp["d_skip"][l][None, :]
        cw = inp["conv_w"][l].reshape(4, 8, 128)
        a[:, 3356:3388] = cw.transpose(2, 1, 0).reshape(128, 32)
        a[:, 3388:3396] = inp["conv_b"][l].reshape(8, 128).T
        a[:, 3400:3402] = inp["a_sinks"][l].reshape(2, 2)[p // 64]
        a[:, 3402] = inp["a_log"][l][p // 16]
        a[:, 3403] = inp["d_skip"][l][p // 16]
        a[:, 3404] = inp["dt_bias"][l][p // 16]
    return par


_CACHE = {}


def kernel(**inp):
    inp = {k: np.asarray(v) for k, v in inp.items()}
    if "nc" not in _CACHE:
        _CACHE["nc"] = build()
    nc, P, es = _CACHE["nc"]
    perm = _w_in_perm()
    w_in_p = np.ascontiguousarray(inp["w_in"][:, :, perm])
    par = _params(inp)
    in_maps = []
    for c in range(NCORES):
        cf, cb = _consts(c)
        bs = slice(c * SB, (c + 1) * SB)
        m = {
            "xp": np.ascontiguousarray(inp["x_prompt"][0, c * TOK:(c + 1) * TOK]),
            "xs": np.ascontiguousarray(inp["x_sample"][bs].transpose(1, 0, 2).reshape(64, D)),
            "cak": np.ascontiguousarray(inp["cache_a_k"][:, bs].reshape(DEPTH, SB, 128, 128)),
            "cav": np.ascontiguousarray(inp["cache_a_v"][:, bs].reshape(DEPTH, SB, 128, 128)),
            "cck": np.ascontiguousarray(inp["cache_c_k"][:, bs].reshape(DEPTH, SB, 2048, 128)),
            "ccv": np.ascontiguousarray(inp["cache_c_v"][:, bs].reshape(DEPTH, SB, 2048, 128)),
            "sssm": np.ascontiguousarray(inp["state_ssm"][:, bs].reshape(DEPTH, SB, 8, 8192)),
            "sconv": np.ascontiguousarray(inp["state_conv"][:, bs].reshape(DEPTH, SB * 3, 1024)),
            "w_in": w_in_p, "w_out": inp["w_out"], "w_gate": inp["w_gate"], "w_up": inp["w_up"],
            "w_down": inp["w_down"], "par": par, "cst_f": cf, "cst_b": cb,
        }
        in_maps.append(m)
    res = run_bass_kernel_spmd(nc, in_maps, core_ids=list(range(NCORES)))
    R = res.results
    _CACHE["last"] = R
    y_prompt = np.concatenate([R[c]["yp"] for c in range(NCORES)], axis=0)[None]
    y_sample = np.concatenate([R[c]["ys"].reshape(4, SB, D).transpose(1, 0, 2) for c in range(NCORES)], axis=0)
    L = NCORES - 1
    p_a_k = R[L]["pak"].reshape(DEPTH, 1, 128, 2, 64)
    p_a_v = R[L]["pav"].reshape(DEPTH, 1, 128, 2, 64)
    p_c_k = R[L]["pck"].reshape(DEPTH, 1, 2048, 2, 64)
    p_c_v = R[L]["pcv"].reshape(DEPTH, 1, 2048, 2, 64)
    p_ssm = R[L]["pssm"].reshape(DEPTH, 1, 8, 64, 128)
    p_conv = R[L]["pconv"].reshape(DEPTH, 1, 3, 1024)

    def cat(name, shape):
        return np.concatenate([R[c][name].reshape((DEPTH, SB) + shape) for c in range(NCORES)], axis=1)

    s_a_k = cat("sak", (128, 2, 64))
    s_a_v = cat("sav", (128, 2, 64))
    s_c_k = cat("sck", (2048, 2, 64))
    s_c_v = cat("scv", (2048, 2, 64))
    s_ssm = cat("sssm_o", (8, 64, 128))
    s_conv = cat("sconv_o", (3, 1024))
    return (y_prompt, y_sample, p_a_k, p_a_v, p_c_k, p_c_v, p_ssm, p_conv,
            s_a_k, s_a_v, s_c_k, s_c_v, s_ssm, s_conv)
```

```python
import numpy as np
import ml_dtypes
from contextlib import ExitStack
import concourse.bass as bass
import concourse.mybir as mybir
from concourse.bass_utils import run_bass_kernel_spmd

F32 = mybir.dt.float32
BF16 = mybir.dt.bfloat16
AF = mybir.ActivationFunctionType
ALU = mybir.AluOpType
AX = mybir.AxisListType

NCORES = 8
D = 1024
DEPTH = 2
TOK = 2048
NT = 16
SB = 16
NIN = 2568
DFF = 2816
EPS = 1e-6
C_QA, C_QC, C_KA, C_KC, C_VA, C_VC, C_Z, C_DT, C_X = 0, 256, 512, 640, 768, 896, 1024, 1536, 1544
NR1 = 2048 + 1024 + 64 + 12
STAGE = 2


class Buf:
    __slots__ = ("w", "r")

    def __init__(self):
        self.w = None
        self.r = []


class Prog:
    ENG = ("pe", "act", "dve", "pool", "sp")
    NDSEM = {"sp": 12, "pool": 24, "act": 20}

    def __init__(self, nc):
        self.nc = nc
        self.ops = []
        self.last = {e: None for e in self.ENG}
        self.pend = {e: {} for e in self.ENG}
        self.dmas = []

    def _emit(self, eng, fn, r, w, kind):
        deps = dict(self.pend[eng])
        self.pend[eng] = {}
        for b in r:
            if b.w is not None:
                deps[b.w] = "raw"
        for b in w:
            if b.w is not None:
                deps[b.w] = "raw"
            for j in b.r:
                deps.setdefault(j, "war")
        i = len(self.ops)
        self.ops.append(dict(eng=eng, fn=fn, deps=deps, kind=kind))
        for b in r:
            if kind == "c":
                b.r = [j for j in b.r if not (self.ops[j]["kind"] == "c" and self.ops[j]["eng"] == eng)]
            b.r.append(i)
        for b in w:
            b.w = i
            b.r = []
        if kind in ("dma", "cc"):
            self.dmas.append(i)
        else:
            self.last[eng] = i
        return i

    def op(self, eng, fn, r=(), w=()):
        return self._emit(eng, fn, r, w, "c")

    def dma(self, q, out, in_, r=(), w=(), **kw):
        return self._emit(q, lambda e: e.dma_start(out=out, in_=in_, **kw), r, w, "dma")

    def cc(self, fn, r=(), w=()):
        return self._emit("pool", fn, r, w, "cc")

    def barrier(self):
        allp = {}
        for e in self.ENG:
            if self.last[e] is not None:
                allp[self.last[e]] = "raw"
        for j in self.dmas:
            allp[j] = "raw"
        self.dmas = []
        for e in self.ENG:
            self.pend[e].update(allp)

    def finish(self, es):
        nc = self.nc
        ops = self.ops
        self.barrier()
        fin = dict(self.pend["sp"])
        needed = set()

        def real_dep(i, j, kind):
            oi, oj = ops[i], ops[j]
            if oj["kind"] in ("dma", "cc"):
                return True
            if oi["eng"] == oj["eng"] and oi["kind"] != "dma":
                if oi["eng"] == "pe":
                    return False
                return kind == "raw"
            return True

        for i, o in enumerate(ops):
            for j, k in o["deps"].items():
                if ops[j]["kind"] == "c" and real_dep(i, j, k):
                    needed.add(j)
        for j in fin:
            if ops[j]["kind"] == "c":
                needed.add(j)
        seq = {}
        cnt = {e: 0 for e in self.ENG}
        dcount = {q: 0 for q in self.NDSEM}
        dtok = {}
        for i, o in enumerate(ops):
            if o["kind"] == "dma":
                q = o["eng"]
                k = dcount[q]
                dcount[q] += 1
                R = self.NDSEM[q]
                dtok[i] = (q, k % R, 16 * (k // R + 1))
            elif o["kind"] == "cc":
                dtok[i] = ("cc", i, 1)
            elif i in needed:
                cnt[o["eng"]] += 1
                seq[i] = cnt[o["eng"]]
        ccsem = {i: es.enter_context(nc.semaphore("cc%d" % i)) for i, o in enumerate(ops) if o["kind"] == "cc"}
        csem = {e: es.enter_context(nc.semaphore("c_" + e)) for e in ("pe", "act", "dve", "pool")}
        dsem = {q: [es.enter_context(nc.semaphore("d_%s%d" % (q, k))) for k in range(R)]
                for q, R in self.NDSEM.items()}
        streams = {e: [] for e in self.ENG}
        waited = {e: {} for e in self.ENG}

        def add_wait(e, lst, key, sem, val):
            if waited[e].get(key, 0) >= val:
                return
            waited[e][key] = val
            lst.append((sem, val))

        for i, o in enumerate(ops):
            e = o["eng"]
            waits = []
            for j, k in sorted(o["deps"].items()):
                if not real_dep(i, j, k):
                    continue
                if ops[j]["kind"] == "dma":
                    q, s, tgt = dtok[j]
                    add_wait(e, waits, ("d", q, s), dsem[q][s], tgt)
                elif ops[j]["kind"] == "cc":
                    add_wait(e, waits, ("cc", j), ccsem[j], 1)
                else:
                    f = ops[j]["eng"]
                    add_wait(e, waits, ("c", f), csem[f], seq[j])
            inc = None
            if o["kind"] == "dma":
                q, s, tgt = dtok[i]
                if tgt > 16:
                    add_wait(e, waits, ("d", q, s), dsem[q][s], tgt - 16)
                inc = (dsem[q][s], 16)
            elif o["kind"] == "cc":
                inc = (ccsem[i], 1)
            elif i in needed:
                inc = (csem[e], 1)
            streams[e].append((waits, o["fn"], inc))
        finw = []
        for j in sorted(fin):
            if ops[j]["kind"] == "dma":
                q, s, tgt = dtok[j]
                add_wait("sp", finw, ("d", q, s), dsem[q][s], tgt)
            elif ops[j]["kind"] == "cc":
                add_wait("sp", finw, ("cc", j), ccsem[j], 1)
            else:
                f = ops[j]["eng"]
                add_wait("sp", finw, ("c", f), csem[f], seq[j])
        for q, R in self.NDSEM.items():
            for s in range(R):
                k = dcount[q]
                n = (k - s + R - 1) // R if k > s else 0
                if n > 0:
                    add_wait("sp", finw, ("d", q, s), dsem[q][s], 16 * n)
        self.stats = dict(n_ops=len(ops), needed=len(needed), cnt=cnt, dcount=dcount)

        def run(eng_obj, lst, tail=()):
            for waits, fn, inc in lst:
                for sem, val in waits:
                    eng_obj.wait_ge(sem, val)
                ins = fn(eng_obj)
                if inc is not None:
                    ins.then_inc(inc[0], inc[1])
            for sem, val in tail:
                eng_obj.wait_ge(sem, val)

        with nc.Block() as block:
            @block.tensor
            def _(e):
                run(e, streams["pe"])

            @block.scalar
            def _(e):
                run(e, streams["act"])

            @block.vector
            def _(e):
                run(e, streams["dve"])

            @block.gpsimd
            def _(e):
                run(e, streams["pool"])

            @block.sync
            def _(e):
                run(e, streams["sp"], finw)


class Arena:
    def __init__(self, handle, nf32):
        self.h = handle
        self.n = nf32
        self.off = 0

    def reset(self):
        self.off = 0

    def f32(self, n):
        a = self.h[:, self.off:self.off + n]
        self.off += n
        assert self.off <= self.n, (self.off, self.n)
        return a

    def bf16(self, n):
        m = (n + 1) // 2
        a = self.h[:, self.off:self.off + m].bitcast(BF16)
        self.off += m
        assert self.off <= self.n, (self.off, self.n)
        return a[:, 0:n]


def build():
    nc = bass.Bass("TRN2", target_bir_lowering=False)
    P = Prog(nc)
    es = ExitStack()

    def din(name, shape, dt=F32):
        return nc.dram_tensor(name, list(shape), dt, kind="ExternalInput").ap()

    def dout(name, shape, dt=F32):
        return nc.dram_tensor(name, list(shape), dt, kind="ExternalOutput").ap()

    def dscr(name, shape, dt=F32):
        return nc.dram_tensor(name, list(shape), dt).ap()

    def sb(name, shape, dt=F32):
        return es.enter_context(nc.sbuf_tensor(name, list(shape), dt))

    xp = din("xp", [TOK, D])
    xs = din("xs", [64, D])
    cak = din("cak", [DEPTH, SB, 128, 128])
    cav = din("cav", [DEPTH, SB, 128, 128])
    cck = din("cck", [DEPTH, SB, 2048, 128])
    ccv = din("ccv", [DEPTH, SB, 2048, 128])
    sssm = din("sssm", [DEPTH, SB, 8, 8192])
    sconv = din("sconv", [DEPTH, SB * 3, 1024])
    w_in = din("w_in", [DEPTH, D, NIN])
    w_out = din("w_out", [DEPTH, D, D])
    w_gate = din("w_gate", [DEPTH, D, DFF])
    w_up = din("w_up", [DEPTH, D, DFF])
    w_down = din("w_down", [DEPTH, DFF, D])
    par_in = din("par", [DEPTH, 128, 3456])
    cst_f = din("cst_f", [128, 1344])
    cst_b = din("cst_b", [128, 2304], BF16)

    yp = dout("yp", [TOK, D])
    ys = dout("ys", [64, D])
    pak = dout("pak", [DEPTH, 128, 128])
    pav = dout("pav", [DEPTH, 128, 128])
    pck = dout("pck", [DEPTH, TOK, 128])
    pcv = dout("pcv", [DEPTH, TOK, 128])
    pssm = dout("pssm", [DEPTH, 512, 128])
    pconv = dout("pconv", [DEPTH, 3, 1024])
    sak = dout("sak", [DEPTH, SB, 128, 128])
    sav = dout("sav", [DEPTH, SB, 128, 128])
    sck = dout("sck", [DEPTH, SB, 2048, 128])
    scv = dout("scv", [DEPTH, SB, 2048, 128])
    sssm_o = dout("sssm_o", [DEPTH, SB, 8, 8192])
    sconv_o = dout("sconv_o", [DEPTH, SB * 3, 1024])

    bn1 = dscr("bn1", [NR1, 256], BF16)
    g1 = dscr("g1", [NCORES * NR1, 256], BF16)
    bn2 = dscr("bn2", [129, 512])
    g2 = dscr("g2", [NCORES * 129, 512])
    zs_scr = dscr("zs_scr", [17, 128, 512], BF16)
    qt_scr = dscr("qt_scr", [128, 4 * 2176], BF16)
    kta_scr = dscr("kta_scr", [128, 2048], BF16)
    sq_scr = dscr("sq_scr", [128, 512])
    sx_scr = dscr("sx_scr", [64, 1024])
    sdt_scr = dscr("sdt_scr", [64, 8])
    sy_scr = dscr("sy_scr", [64, 512])
    smix = dscr("smix", [64, 512])
    B_sq, B_sx, B_sdt, B_sy, B_smix = Buf(), Buf(), Buf(), Buf(), Buf()
    vprev = dscr("vprev", [2048, 256], BF16)
    B_vprev = Buf()
    B_kta = Buf()
    B_bn1, B_g1, B_bn2, B_g2, B_zs, B_qt = Buf(), Buf(), Buf(), Buf(), Buf(), Buf()
    RG = [list(range(NCORES))]

    Hh = sb("H", [128, NT * D])
    H = Hh[:, :].rearrange("p (t d) -> p t d", t=NT)
    BH = [Buf() for _ in range(NT + 1)]
    HSh = sb("HS", [128, D])
    HS = HSh[:, :]
    CFh = sb("CF", [128, 1344])
    CF = CFh[:, :]
    CBh = sb("CB", [128, 2304], BF16)
    CB = CBh[:, :]
    B_cst = Buf()
    identF, triU, triS, onesF = CF[:, 0:128], CF[:, 128:256], CF[:, 256:384], CF[:, 384:512]
    cosT = CF[:, 512:648].rearrange("p (t k) -> p t k", t=17)
    sinT = CF[:, 648:784].rearrange("p (t k) -> p t k", t=17)
    cs1T = CF[:, 800:1072].rearrange("p (t k) -> p t k", t=17)
    cs2T = CF[:, 1072:1344].rearrange("p (t k) -> p t k", t=17)
    pv = CF[:, 784:785]
    rmask = CF[:, 785:793]
    identB = CB[:, 0:128]
    mA0, mA, mC0, mC = CB[:, 128:640], CB[:, 640:1152], CB[:, 1152:1664], CB[:, 1664:2176]
    causB = CB[:, 2176:2304]
    PARh = sb("PAR", [128, 3456])
    PAR = PARh[:, :]
    B_par = Buf()
    g1b, g2b = PAR[:, 0:1024], PAR[:, 1024:2048]
    qkg = PAR[:, 2048:2816].rearrange("p (h d) -> p h d", h=12)
    qg = PAR[:, 2048:2560].rearrange("p (h d) -> p h d", h=8)
    kg = PAR[:, 2560:2816].rearrange("p (h d) -> p h d", h=4)
    ssmn = PAR[:, 2816:3328]
    sinks_b = PAR[:, 3328:3332]
    dtb_b = PAR[:, 3332:3340]
    aneg_b = PAR[:, 3340:3348]
    dsk_b = PAR[:, 3348:3356]
    convw = PAR[:, 3356:3388].rearrange("p (c j) -> p c j", c=8)
    convb = PAR[:, 3388:3396]
    esink_b = PAR[:, 3396:3400]
    sinkl = PAR[:, 3400:3402]
    anegl = PAR[:, 3402:3403]
    dskl = PAR[:, 3403:3404]
    dtbl = PAR[:, 3404:3405]
    ARN = 29000
    ARh = sb("ARENA", [128, ARN])
    AR = Arena(ARh, ARN)

    PS = [es.enter_context(nc.psum_tensor("ps%d" % i, [128, 512], F32)) for i in range(8)]
    BP = [Buf() for _ in range(8)]

    def psf(i):
        return PS[i][:, :]

    def psb(i):
        return PS[i][:, :].bitcast(BF16)

    def mm(out, lhsT, rhs, st, sp_, r, w):
        P.op("pe", lambda e: e.matmul(out, lhsT, rhs, start=st, stop=sp_), r, w)

    def tr(out, in_, idn, r, w):
        P.op("pe", lambda e: e.transpose(out, in_, idn), r, w)

    def act(out, in_, fn, r, w, bias=0.0, scale=1.0, accum=None):
        if accum is None:
            P.op("act", lambda e: e.activation(out, in_, fn, bias=bias, scale=scale), r, w)
        else:
            P.op("act", lambda e: e.activation(out, in_, fn, bias=bias, scale=scale, accum_out=accum), r, w)

    def tt(eng, out, in0, in1, op, r, w):
        P.op(eng, lambda e: e.tensor_tensor(out, in0, in1, op), r, w)

    def ts(eng, out, in0, s1, s2, op0, op1, r, w):
        if op1 is None:
            P.op(eng, lambda e: e.tensor_scalar(out, in0, s1, None, op0), r, w)
        else:
            P.op(eng, lambda e: e.tensor_scalar(out, in0, s1, s2, op0, op1), r, w)

    def stt(eng, out, in0, sc, in1, op0, op1, r, w):
        P.op(eng, lambda e: e.scalar_tensor_tensor(out, in0, sc, in1, op0, op1), r, w)

    def red(eng, out, in_, op, r, w):
        P.op(eng, lambda e: e.tensor_reduce(out, in_, AX.X, op), r, w)

    def cp(eng, out, in_, r, w):
        if eng == "act":
            P.op("act", lambda e: e.activation(out, in_, AF.Copy), r, w)
        else:
            P.op(eng, lambda e: e.tensor_copy(out, in_), r, w)

    def rcp(out, in_, r, w):
        P.op("dve", lambda e: e.reciprocal(out, in_), r, w)

    def bc(ap, shape):
        return ap.to_broadcast(list(shape))

    P.dma("sp", CF, cst_f[:, :], w=[B_cst])
    P.dma("sp", CB, cst_b[:, :], w=[B_cst])
    class _Fresh(list):
        def __iter__(self):
            return iter([Buf()])

    B_out = Buf()
    OUTW = _Fresh()
    B_cpy = {}
    B_new = {}
    def issue_copies(l, q):
        for (nm, src, dst, n) in (("ak", cak, sak, 128), ("av", cav, sav, 128), ("ck", cck, sck, 2048), ("cv", ccv, scv, 2048)):
            B_cpy[(l, nm)] = []
            B_new[(l, nm)] = Buf()
            step = SB if n == 128 else 4
            for b0 in range(0, SB, step):
                bb = Buf()
                B_cpy[(l, nm)].append(bb)
                P.dma(q, dst[l, b0:b0 + step, 0:n - 4, :].rearrange("b (r f) c -> b r (f c)", f=4),
                      src[l, b0:b0 + step, 4:n, :].rearrange("b (r f) c -> b r (f c)", f=4), w=[bb])

    issue_copies(0, "act")
    issue_copies(1, "act")

    def load_params(l):
        P.dma("sp", PAR, par_in[l], w=[B_par])
        act(aneg_b, aneg_b, AF.Exp, [B_par], [B_par])
        ts("dve", aneg_b, aneg_b, -1.0, None, ALU.mult, None, [B_par], [B_par])
        act(anegl, anegl, AF.Exp, [B_par], [B_par])
        ts("dve", anegl, anegl, -1.0, None, ALU.mult, None, [B_par], [B_par])
        act(esink_b, sinks_b, AF.Exp, [B_par], [B_par])

    V, A_, G = "dve", "act", "pool"
    MUL, ADD, SUB, POW, MAXOP = ALU.mult, ALU.add, ALU.subtract, ALU.pow, ALU.max
    bn1_kc = bn1[2048:3072, :].rearrange("(p j) c -> p (j c)", j=8)
    bn1_ka = bn1[3072:3136, :].rearrange("r (a c) -> (r a) c", a=2)
    bn1_tail = bn1[3136:3148, :].rearrange("r c -> (r c)").rearrange("(p k) -> p k", p=128)
    qt_v = qt_scr.rearrange("p (j t) -> p j t", j=4)

    def normrope(X, nh, gain, t, tmp, sm, bufs):
        sq = tmp[:, 0:nh * 64].rearrange("p (h d) -> p h d", h=nh)
        tt(V, sq, X, X, MUL, bufs, bufs)
        ss = sm[:, 0:nh]
        red(V, ss, sq, ADD, bufs, bufs)
        act(ss, ss, AF.Sqrt, bufs, bufs, bias=EPS, scale=1.0 / 64)
        rcp(ss, ss, bufs, bufs)
        tt(V, X, X, bc(ss.unsqueeze(2), [128, nh, 64]), MUL, bufs, bufs)
        tt(V, X, X, gain, MUL, bufs + [B_par], bufs)
        tA = tmp[:, 0:nh * 16].rearrange("p (h d) -> p h d", h=nh)
        tB = tmp[:, nh * 16:nh * 32].rearrange("p (h d) -> p h d", h=nh)
        rb = bufs + [B_cst]
        tt(V, tA, X[:, :, 0:16], bc(cs1T[:, t, :].unsqueeze(1), [128, nh, 16]), MUL, rb, bufs)
        tt(V, tB, X[:, :, 0:16], bc(cs2T[:, t, :].unsqueeze(1), [128, nh, 16]), MUL, rb, bufs)
        tt(V, X[:, :, 0:8], tA[:, :, 0:8], tA[:, :, 8:16], SUB, bufs, bufs)
        tt(V, X[:, :, 8:16], tB[:, :, 0:8], tB[:, :, 8:16], ADD, bufs, bufs)

    def rmsnorm_T(xin, bx, gb, ubf, uT, sm, junk, bufs):
        ss = sm[:, 16:17]
        P.op(V, lambda e: e.memset(ss, 0.0), [], bufs)
        act(junk, xin, AF.Square, [bx] + bufs, bufs, accum=ss)
        act(ss, ss, AF.Sqrt, bufs, bufs, bias=EPS, scale=1.0 / D)
        rcp(ss, ss, bufs, bufs)
        stt(V, ubf, xin, ss, gb, MUL, MUL, [bx, B_par] + bufs, bufs)
        for c in range(8):
            tr(psb(6)[:, c * 128:(c + 1) * 128], ubf[:, c * 128:(c + 1) * 128], identB, bufs + [B_cst], [BP[6]])
        cp(A_, uT, psb(6)[:, 0:1024], [BP[6]], bufs)

    state = {}

    def layer_P1(l):
        load_params(l)
        AR.reset()
        xbS = AR.bf16(8 * 64).rearrange("p (c t) -> p c t", c=8)
        dtall = AR.f32(17 * 8).rearrange("p (t h) -> p t h", t=17)
        mo_all = AR.bf16(17 * 512).rearrange("p (t c) -> p t c", t=17)
        markA = AR.off
        xbcT = AR.bf16(8 * 2052).rearrange("p (c t) -> p c t", c=8)
        B_xbc, B_dt, B_mo = Buf(), Buf(), Buf()
        state.update(xbcT=xbcT, xbS=xbS, dtall=dtall, B_xbc=B_xbc, B_dt=B_dt, mark=AR.off, markA=markA,
                     mo_all=mo_all, B_mo=B_mo)
        Wb = AR.bf16(8 * NIN).rearrange("p (c n) -> p c n", c=8)
        B_W = Buf()
        for c in range(8):
            P.dma("pool", Wb[:, c, :], w_in[l, c * 128:(c + 1) * 128, :], w=[B_W])
        usets = []
        for k in range(2):
            usets.append(dict(ubf=AR.bf16(1024), uT=AR.bf16(1024), sm=AR.f32(32), B=Buf()))
        qkf = AR.f32(768)
        vf = AR.f32(256)
        tmpq = AR.f32(768)
        smq = AR.f32(32)
        smd = AR.f32(8)
        qkT = AR.bf16(768)
        zst = AR.bf16(512)
        vbf = AR.bf16(256)
        Bq, Bq2, Bkv, Bk2, Bz, Bd, Bvb = Buf(), Buf(), Buf(), Buf(), Buf(), Buf(), Buf()
        smd_t = AR.f32(24)
        Btl = Buf()
        qkbs = [AR.bf16(768), AR.bf16(768)]
        Bqn = [Buf(), Buf()]

        def tile_io(t):
            smp = (t == 16)
            us = usets[t % 2]
            if not smp:
                return smp, us, H[:, t, :], BH[t]
            return smp, us, HS, BH[16]

        def stA(t):
            smp, us, xin, bx = tile_io(t)
            if l == 0:
                if not smp:
                    P.dma("sp", xin, xp[t * 128:(t + 1) * 128, :], w=[bx])
                else:
                    P.dma("sp", HS[0:64, :], xs[:, :], w=[bx])
                    P.dma("sp", HS[64:128, :], xs[:, :], w=[bx])
            rmsnorm_T(xin, bx, g1b, us["ubf"], us["uT"], us["sm"], us["ubf"], [us["B"]])

        def stB(t):
            smp, us, xin, bx = tile_io(t)
            Bu, uT = us["B"], us["uT"]
            for (bk, c0, n) in ((0, 0, 512), (1, 512, 512), (2, 1024, 512), (3, 1536, 8)):
                for c in range(8):
                    mm(psf(bk)[:, 0:n], uT[:, c * 128:(c + 1) * 128], Wb[:, c, c0:c0 + n], c == 0, c == 7,
                       [Bu, B_W], [BP[bk]])
            for cc in range(8):
                for c in range(8):
                    mm(psf(4 + cc // 4)[:, (cc % 4) * 128:(cc % 4 + 1) * 128],
                       Wb[:, c, C_X + cc * 128:C_X + (cc + 1) * 128], uT[:, c * 128:(c + 1) * 128],
                       c == 0, c == 7, [Bu, B_W], [BP[4 + cc // 4]])

        def stC1(t):
            smp, us, xin, bx = tile_io(t)
            Bu, uT = us["B"], us["uT"]
            k2 = t % 2
            cp(A_, qkf[:, 0:512], psf(0), [BP[0]], [Bq])
            cp(A_, qkf[:, 512:768], psf(1)[:, 0:256], [BP[1]], [Bq])
            cp(A_, vf, psf(1)[:, 256:512], [BP[1]], [Bkv])
            normrope(qkf.rearrange("p (h d) -> p h d", h=12), 12, qkg, t, tmpq, smq, [Bq])
            cp(V, qkbs[k2], qkf, [Bq], [Bqn[k2]])
            if smp:
                P.dma("sp", sq_scr[:, :], qkf[:, 0:512], r=[Bq], w=[B_sq])
            if not smp:
                P.dma("sp", pck[l, t * 128:(t + 1) * 128, :], qkf[:, 640:768], r=[Bq], w=OUTW)
                P.dma("sp", pcv[l, t * 128:(t + 1) * 128, :], vf[:, 128:256], r=[Bkv], w=OUTW)
                if t == 15:
                    P.dma("sp", pak[l], qkf[:, 512:640], r=[Bq], w=OUTW)
                    P.dma("sp", pav[l], vf[:, 0:128], r=[Bkv], w=OUTW)
                cp(G, vbf, vf, [Bkv], [Bvb])
                P.dma("sp", bn1[t * 128:(t + 1) * 128, :], vbf, r=[Bvb], w=[B_bn1])
            else:
                for tk in range(4):
                    rows = slice(tk * 16, (tk + 1) * 16)
                    P.dma("sp", sak[l, :, 124 + tk, :], qkf[rows, 512:640], r=[Bq], w=[B_new[(l, "ak")]])
                    P.dma("sp", sck[l, :, 2044 + tk, :], qkf[rows, 640:768], r=[Bq], w=[B_new[(l, "ck")]])
                    P.dma("sp", sav[l, :, 124 + tk, :], vf[rows, 0:128], r=[Bkv], w=[B_new[(l, "av")]])
                    P.dma("sp", scv[l, :, 2044 + tk, :], vf[rows, 128:256], r=[Bkv], w=[B_new[(l, "cv")]])
            act(zst, psf(2), AF.Silu, [BP[2]], [Bz])
            P.dma("sp", zs_scr[t], zst, r=[Bz], w=[B_zs])
            tt(V, smd, psf(3)[:, 0:8], dtb_b, ADD, [BP[3], B_par], [Bd])
            act(smd, smd, AF.Exp, [Bd], [Bd])
            act(dtall[:, t, :], smd, AF.Ln, [Bd], [B_dt], bias=1.0)
            if smp:
                P.dma("sp", sdt_scr[:, :], dtall[0:64, t, :], r=[B_dt], w=[B_sdt])
            for bk in range(2):
                src = psf(4 + bk).rearrange("p (c t) -> p c t", c=4)
                if not smp:
                    cp(A_, xbcT[:, 4 * bk:4 * bk + 4, 3 + t * 128:3 + (t + 1) * 128], src, [BP[4 + bk]], [B_xbc])
                else:
                    cp(A_, xbS[:, 4 * bk:4 * bk + 4, :], src[:, :, 0:64], [BP[4 + bk]], [B_xbc])
            if t == 15:
                P.dma("sp", bn1_tail, xbcT[:, :, 3 + 2045:3 + 2048], r=[B_xbc], w=[B_bn1])
                tl = smd_t.rearrange("p (j c) -> p j c", j=3)
                cp(V, tl, xbcT[:, :, 3 + 2045:3 + 2048].rearrange("p c j -> p j c"), [B_xbc, Btl], [Btl])
                for j in range(3):
                    P.dma("sp", pconv[l, j].rearrange("(c p) -> p c", p=128), tl[:, j, :], r=[Btl], w=OUTW,
                          allow_slow_non_contiguous=True)
            if smp:
                xtm = (tmpq[:, 0:512], qkf[:, 0:512])
                for bk in range(2):
                    for c in range(8):
                        mm(psf(4 + bk), uT[:, c * 128:(c + 1) * 128],
                           Wb[:, c, C_X + bk * 512:C_X + (bk + 1) * 512], c == 0, c == 7, [Bu, B_W], [BP[4 + bk]])
                    cp(A_, xtm[bk], psf(4 + bk), [BP[4 + bk]], [Bq])
                sco = sconv_o[l].rearrange("(b j) c -> b j c", j=3)
                for tk in range(1, 4):
                    for bk in range(2):
                        P.dma("sp", sco[:, tk - 1, bk * 512:(bk + 1) * 512], xtm[bk][tk * 16:(tk + 1) * 16, :],
                              r=[Bq], w=OUTW)

        def stC2(t):
            smp = (t == 16)
            k2 = t % 2
            for j in range(6):
                tr(psb(7)[:, j * 128:(j + 1) * 128], qkbs[k2][:, j * 128:(j + 1) * 128], identB, [Bqn[k2], B_cst], [BP[7]])
            cp(A_, qkT, psb(7)[:, 0:768], [BP[7]], [Bq2])
            P.dma("sp", qt_v[:, :, t * 128:(t + 1) * 128], qkT[:, 0:512].rearrange("p (j t) -> p j t", j=4), r=[Bq2], w=[B_qt])
            if not smp:
                P.dma("sp", bn1_kc[:, t * 128:(t + 1) * 128], qkT[:, 640:768], r=[Bq2], w=[B_bn1])
                P.dma("sp", kta_scr[:, t * 128:(t + 1) * 128], qkT[:, 512:640], r=[Bq2], w=[B_kta])
                if t == 15:
                    P.dma("sp", bn1_ka, qkT[:, 512:640], r=[Bq2], w=[B_bn1])

        stA(0)
        for t in range(17):
            stB(t)
            if t + 1 < 17:
                stA(t + 1)
            stC1(t)
            if t >= 1:
                stC2(t - 1)
        stC2(16)
        P.cc(lambda e: e.collective_compute("AllGather", ALU.bypass, replica_groups=RG,
                                            ins=[bn1[:, :]], outs=[g1[:, :]]), r=[B_bn1], w=[B_g1])


    def dma_fn(q, fn, r=(), w=()):
        return P._emit(q, fn, r, w, "dma")

    pcache = {}

    def prev_base(e):
        k = id(e)
        if k not in pcache:
            pcache[k] = ((e.partition_id() + (NCORES - 1)) % NCORES) * (NR1 * 256)
        return pcache[k]

    def prev_rows(e, off, n):
        return bass.AP(g1.tensor, prev_base(e) + off * 256, [[256, n], [1, 256]])

    def layer_P23(l):
        xbcT, dtall, B_xbc, B_dt = state["xbcT"], state["dtall"], state["B_xbc"], state["B_dt"]
        AR.off = state["mark"]
        xact = AR.bf16(8 * 2048).rearrange("p (c t) -> p c t", c=8)
        hT = AR.f32(512)
        hTb = AR.bf16(512)
        Ltot = AR.f32(8)
        mo_all, B_mo = state["mo_all"], state["B_mo"]
        B_xact, B_h = Buf(), Buf()
        mk = AR.off
        dma_fn("sp", lambda e: e.dma_start(
            out=xbcT[:, :, 0:3],
            in_=prev_rows(e, 3136, 12).rearrange("r c -> (r c)").rearrange("(p k) -> p k", p=128)),
            r=[B_g1], w=[B_xbc])
        ts(V, xbcT[:, :, 0:3], xbcT[:, :, 0:3], pv, None, MUL, None, [B_xbc, B_cst], [B_xbc])
        accs = [AR.f32(2048), AR.f32(2048)]
        Bacc = [Buf(), Buf()]
        for c in range(8):
            eng = V
            acc, ba = accs[c % 2], Bacc[c % 2]
            ts(eng, acc, xbcT[:, c, 0:2048], convw[:, c, 0:1], None, MUL, None, [B_xbc, B_par], [ba])
            for j in range(1, 4):
                stt(eng, acc, xbcT[:, c, j:j + 2048], convw[:, c, j:j + 1], acc, MUL, ADD, [B_xbc, B_par, ba], [ba])
            act(xact[:, c, :], acc, AF.Silu, [ba, B_par], [B_xact], bias=convb[:, c:c + 1])
        AR.off = mk
        xbms = [AR.bf16(768), AR.bf16(768)]
        Bxb = [Buf(), Buf()]
        xw = AR.bf16(512)
        sm = AR.f32(64)
        Bt = Buf()
        Ba, Bxw = Buf(), Buf()
        a8, dte8, dec8, w8 = sm[:, 0:8], sm[:, 8:16], sm[:, 16:24], sm[:, 24:32]
        P.op(V, lambda e: e.memset(hT, 0.0), [], [B_h])
        P.op(V, lambda e: e.memset(Ltot, 0.0), [], [B_h])

        def xs_bm_tm(t):
            xbm, bb = xbms[t % 2], Bxb[t % 2]
            for j in range(6):
                tr(psb(6)[:, j * 128:(j + 1) * 128], xact[:, j, t * 128:(t + 1) * 128], identB, [B_xact, B_cst], [BP[6]])
            cp(A_, xbm, psb(6)[:, 0:768], [BP[6]], [bb])
            return xbm, bb

        for t in range(NT):
            xbm, bb = xs_bm_tm(t)
            tt(V, a8, dtall[:, t, :], aneg_b, MUL, [B_dt, B_par], [Ba])
            mm(psf(3)[:, 0:8], triS, a8, True, True, [Ba, B_cst], [BP[3]])
            mm(psf(3)[:, 8:16], onesF, a8, True, True, [Ba, B_cst], [BP[3]])
            act(dte8, psf(3)[:, 0:8], AF.Exp, [BP[3]], [Ba])
            act(dec8, psf(3)[:, 8:16], AF.Exp, [BP[3]], [Ba])
            act(sm[:, 48:56], psf(3)[:, 8:16], AF.Copy, [BP[3]], [Ba])
            tt(V, Ltot, Ltot, sm[:, 48:56], ADD, [Ba, B_h], [B_h])
            tt(V, w8, dtall[:, t, :], dte8, MUL, [B_dt, Ba], [Ba])
            tt(V, xw.rearrange("p (h d) -> p h d", h=8), xbm[:, 0:512].rearrange("p (h d) -> p h d", h=8),
               bc(w8.unsqueeze(2), [128, 8, 64]), MUL, [Ba, bb], [Bxw])
            for g in range(2):
                mm(psf(2)[:, g * 256:(g + 1) * 256], xbm[:, 512 + g * 128:512 + (g + 1) * 128],
                   xw[:, g * 256:(g + 1) * 256], True, True, [bb, Bxw], [BP[2]])
            tt(V, hT.rearrange("p (h d) -> p h d", h=8), hT.rearrange("p (h d) -> p h d", h=8),
               bc(dec8.unsqueeze(2), [128, 8, 64]), MUL, [Ba, B_h], [B_h])
            tt(V, hT, hT, psf(2), ADD, [BP[2], B_h], [B_h])
        P.dma("sp", bn2[0:128, :], hT, r=[B_h], w=[B_bn2])
        P.dma("sp", bn2[128:129, 0:8], Ltot[0:1, :], r=[B_h], w=[B_bn2])
        P.cc(lambda e: e.collective_compute("AllGather", ALU.bypass, replica_groups=RG,
                                            ins=[bn2[:, :]], outs=[g2[:, :]]), r=[B_bn2], w=[B_g2])
        Sall = AR.f32(8 * 512).rearrange("p (r c) -> p r c", r=8)
        B_S = Buf()
        P.dma("sp", Sall, g2.rearrange("(r k) c -> k r c", k=129)[0:128], r=[B_g2], w=[B_S])
        AX2 = Arena(ARh, state["mark"])
        AX2.off = state["markA"]
        Lall = AX2.f32(64).rearrange("p (r h) -> p r h", r=8)
        Dm = AX2.f32(64).rearrange("p (r h) -> p r h", r=8)
        P.dma("sp", Lall, bass.AP(g2.tensor, 128 * 512, [[0, 128], [129 * 512, 8], [1, 8]]), r=[B_g2, B_xbc], w=[B_S])
        tt(V, Lall, Lall, bc(rmask.unsqueeze(2), [128, 8, 8]), MUL, [B_S, B_cst], [B_S])
        act(Dm, Lall, AF.Exp, [B_S], [B_S])
        P.op(V, lambda e: e.memset(hT, 0.0), [B_bn2], [B_h])
        h3 = hT.rearrange("p (h d) -> p h d", h=8)
        for j in range(NCORES):
            tt(V, h3, h3, bc(Dm[:, j, :].unsqueeze(2), [128, 8, 64]), MUL, [B_S, B_h], [B_h])
            stt(V, hT, Sall[:, j, :], rmask[:, j:j + 1], hT, MUL, ADD, [B_S, B_cst, B_h], [B_h])
        cp(A_, hTb, hT, [B_h], [B_h])
        Brhs = AX2.f32(1024)
        eseg = AX2.bf16(1024)
        MT = AX2.bf16(1024)
        cbm = AX2.bf16(256)
        xdt = AX2.bf16(512)
        ytmp = AX2.f32(512)
        sk = AX2.f32(512)
        zsts = [AX2.bf16(512), AX2.bf16(512)]
        junk = AX2.bf16(512)
        sm3 = AX2.f32(16)
        eac8 = sm[:, 32:40]
        ss = sm3[:, 0:1]
        Bz = [Buf(), Buf()]
        BBr, Bes, Bcb, BMT, Bxd, By = Buf(), Buf(), Buf(), Buf(), Buf(), Buf()

        def gate_norm(t, yt, xs_tm, bxs, bufs):
            zst, bz = zsts[t % 2], Bz[t % 2]
            P.dma("sp", zst, zs_scr[t], r=[B_zs], w=[bz])
            tt(V, sk.rearrange("p (h d) -> p h d", h=8), xs_tm.rearrange("p (h d) -> p h d", h=8),
               bc(dsk_b.unsqueeze(2), [128, 8, 64]), MUL, bufs + bxs + [B_par], bufs)
            tt(V, yt, yt, sk, ADD, bufs, bufs)
            tt(V, yt, yt, zst, MUL, bufs + [bz], bufs)
            P.op(V, lambda e: e.memset(ss, 0.0), [], bufs)
            act(junk, yt, AF.Square, bufs, bufs, accum=ss)
            act(ss, ss, AF.Sqrt, bufs, bufs, bias=EPS, scale=1.0 / 512)
            rcp(ss, ss, bufs, bufs)
            stt(V, mo_all[:, t, :], yt, ss, ssmn, MUL, MUL, bufs + [B_par], [B_mo])

        Bt = By
        MT2 = [MT, AX2.bf16(1024)]
        xdt2 = [xdt, AX2.bf16(512)]
        xw2 = [xw, AX2.bf16(512)]
        smX = [AX2.f32(40), AX2.f32(40)]
        BMT2, Bxd2, Bxw2, Ba2 = [BMT, Buf()], [Bxd, Buf()], [Bxw, Buf()], [Buf(), Buf()]

        def stX(t):
            k2 = t % 2
            cols = slice(t * 128, (t + 1) * 128)
            sx = smX[k2]
            a8_, dte_, dec_, eac_ = sx[:, 0:8], sx[:, 8:16], sx[:, 16:24], sx[:, 24:32]
            ba = Ba2[k2]
            xbm, bb = xs_bm_tm(t)
            tt(V, a8_, dtall[:, t, :], aneg_b, MUL, [B_dt, B_par], [ba])
            mm(psf(3)[:, 0:8], triS, a8_, True, True, [ba, B_cst], [BP[3]])
            mm(psf(3)[:, 8:16], onesF, a8_, True, True, [ba, B_cst], [BP[3]])
            mm(psf(3)[:, 16:24], triU, a8_, True, True, [ba, B_cst], [BP[3]])
            tt(G, Brhs.rearrange("p (h l) -> p h l", h=8), bc(triU.unsqueeze(1), [128, 8, 128]),
               bc(a8_.unsqueeze(2), [128, 8, 128]), MUL, [ba, B_cst], [BBr])
            act(dte_, psf(3)[:, 0:8], AF.Exp, [BP[3]], [ba])
            act(dec_, psf(3)[:, 8:16], AF.Exp, [BP[3]], [ba])
            act(eac_, psf(3)[:, 16:24], AF.Exp, [BP[3]], [ba])
            x3 = xbm[:, 0:512].rearrange("p (h d) -> p h d", h=8)
            tt(V, xdt2[k2].rearrange("p (h d) -> p h d", h=8), x3, bc(dtall[:, t, :].unsqueeze(2), [128, 8, 64]), MUL,
               [bb, B_dt], [Bxd2[k2]])
            tt(V, xw2[k2].rearrange("p (h d) -> p h d", h=8), xdt2[k2].rearrange("p (h d) -> p h d", h=8),
               bc(dte_.unsqueeze(2), [128, 8, 64]), MUL, [Bxd2[k2], ba], [Bxw2[k2]])
            mm(psf(4), triS, Brhs[:, 0:512], True, True, [BBr, B_cst], [BP[4]])
            mm(psf(5), triS, Brhs[:, 512:1024], True, True, [BBr, B_cst], [BP[5]])
            act(eseg[:, 0:512], psf(4), AF.Exp, [BP[4]], [Bes])
            act(eseg[:, 512:1024], psf(5), AF.Exp, [BP[5]], [Bes])
            for g in range(2):
                mm(psf(1)[:, g * 128:(g + 1) * 128], xact[:, 4 + g, cols], xact[:, 6 + g, cols], True, True,
                   [B_xact], [BP[1]])
            tt(V, cbm.rearrange("p (g l) -> p g l", g=2), psf(1)[:, 0:256].rearrange("p (g l) -> p g l", g=2),
               bc(causB.unsqueeze(1), [128, 2, 128]), MUL, [BP[1], B_cst], [Bcb])
            tt(V, MT2[k2].rearrange("p (g h l) -> p g h l", g=2, h=4), eseg.rearrange("p (g h l) -> p g h l", g=2, h=4),
               bc(cbm.rearrange("p (g l) -> p g l", g=2).unsqueeze(2), [128, 2, 4, 128]), MUL, [Bes, Bcb], [BMT2[k2]])

        def stY(t):
            k2 = t % 2
            cols = slice(t * 128, (t + 1) * 128)
            sx = smX[k2]
            dec_, eac_ = sx[:, 16:24], sx[:, 24:32]
            ba = Ba2[k2]
            xbm, bb = xbms[k2], Bxb[k2]
            for h in range(8):
                mm(psf(0)[:, h * 64:(h + 1) * 64], MT2[k2][:, h * 128:(h + 1) * 128], xdt2[k2][:, h * 64:(h + 1) * 64],
                   True, True, [BMT2[k2], Bxd2[k2]], [BP[0]])
            for g in range(2):
                mm(psf(2)[:, g * 256:(g + 1) * 256], xact[:, 6 + g, cols], hTb[:, g * 256:(g + 1) * 256], True, True,
                   [B_xact, B_h], [BP[2]])
            for g in range(2):
                mm(psf(7)[:, g * 256:(g + 1) * 256], xbm[:, 512 + g * 128:512 + (g + 1) * 128],
                   xw2[k2][:, g * 256:(g + 1) * 256], True, True, [bb, Bxw2[k2]], [BP[7]])
            tt(V, h3, h3, bc(dec_.unsqueeze(2), [128, 8, 64]), MUL, [ba, B_h, BP[2]], [B_h])
            tt(V, hT, hT, psf(7), ADD, [BP[7], B_h], [B_h])
            cp(A_, hTb, hT, [B_h], [B_h])
            y3 = ytmp.rearrange("p (h d) -> p h d", h=8)
            tt(V, y3, psf(2).rearrange("p (h d) -> p h d", h=8), bc(eac_.unsqueeze(2), [128, 8, 64]), MUL,
               [BP[2], ba], [By])
            tt(V, ytmp, ytmp, psf(0), ADD, [BP[0], By], [By])
            gate_norm(t, ytmp, xbm[:, 0:512], [bb], [By])

        stX(0)
        for t in range(NT):
            if t + 1 < NT:
                stX(t + 1)
            stY(t)
        xsS = Brhs[:, 0:512]
        for hh in range(2):
            P.dma("sp", ytmp[hh * 64:(hh + 1) * 64, :], sy_scr[:, :], r=[B_sy, By], w=[By])
            P.dma("sp", xsS[hh * 64:(hh + 1) * 64, :], sx_scr[:, 0:512], r=[B_sx, BBr], w=[BBr])
        gate_norm(16, ytmp, xsS, [BBr], [By])
        for j in range(4):
            tr(psf(7)[:, j * 128:(j + 1) * 128], hT[:, j * 128:(j + 1) * 128], identF, [B_h, B_cst], [BP[7]])
        cp(V, ytmp, psf(7), [BP[7]], [Bt])
        P.dma("sp", pssm[l].rearrange("(j p) n -> p j n", p=128), ytmp.rearrange("p (j n) -> p j n", j=4),
              r=[Bt], w=OUTW)


    def layer_P4(l):
        mo_all = state["mo_all"]
        AR.off = state["markA"]
        oT = AR.bf16(8 * 2176).rearrange("p (h t) -> p h t", h=8)
        mark3 = AR.off
        qT = AR.bf16(4 * 2176).rearrange("p (j t) -> p j t", j=4)
        kTC = AR.bf16(4096)
        kTA = AR.bf16(2176)
        acc = AR.f32(2 * 2048).rearrange("p (r t) -> p r t", r=2)
        NV = 4
        vr = [AR.bf16(256).rearrange("p (g c) -> p g c", g=2) for _ in range(NV)]
        Bv = [Buf() for _ in range(NV)]
        Et = [AR.bf16(512), AR.bf16(512)]
        BE = [Buf(), Buf()]
        rd = AR.f32(256)
        Brd = Buf()
        B_q, B_k, B_acc, B_o = Buf(), Buf(), Buf(), Buf()
        state.update(oT=oT, B_o=B_o, mark3=mark3)
        P.dma("sp", qT[:, :, 0:2048], qt_v[:, :, 0:2048], r=[B_qt], w=[B_q])
        P.dma("sp", kTC[:, 2048:4096], bn1_kc, r=[B_bn1], w=[B_k])
        dma_fn("sp", lambda e: e.dma_start(out=kTC[:, 0:2048],
                                           in_=prev_rows(e, 2048, 1024).rearrange("(p j) c -> p (j c)", j=8)),
               r=[B_g1], w=[B_k])
        dma_fn("sp", lambda e: e.dma_start(out=kTA[:, 0:128],
                                           in_=prev_rows(e, 3072, 64).rearrange("r (a c) -> (r a) c", a=2)),
               r=[B_g1], w=[B_k])
        P.dma("sp", kTA[:, 128:2176], kta_scr[:, :], r=[B_kta], w=[B_k])
        for i in range(NV):
            P.op(G, lambda e, i=i: e.memset(vr[i][:, :, 64:128], 1.0), [], [Bv[i]])
        dma_fn("sp", lambda e: e.dma_start(out=vprev[:, :], in_=prev_rows(e, 0, 2048)), r=[B_g1], w=[B_vprev])
        cnt = {"v": 0, "e": 0}

        def load_v(d, r, b, voff):
            i = cnt["v"] % NV
            cnt["v"] += 1
            B = 128 * d
            if b >= 0:
                src = bn1[b * B + r:b * B + r + 127 * d + 1:d, voff:voff + 128]
                P.dma("sp", vr[i][:, :, 0:64], src.rearrange("p (g c) -> p g c", g=2), r=[B_bn1], w=[Bv[i]])
            else:
                st = 2048 - B + r
                src = vprev[st:st + 127 * d + 1:d, voff:voff + 128]
                P.dma("sp", vr[i][:, :, 0:64], src.rearrange("p (g c) -> p g c", g=2), r=[B_vprev], w=[Bv[i]])
            return i

        def unit(kT, koff, qj, d, r, b, g, vi_prev, vi_cur, mask, is_A):
            B = 128 * d
            q0 = b * B + r
            qs = slice(q0, q0 + 127 * d + 1, d)
            kc = slice(koff + q0, koff + q0 + 127 * d + 1, d)
            kp = slice(koff + q0 - B, koff + q0 - B + 127 * d + 1, d)
            gs = slice(g * 64, (g + 1) * 64)
            ei = cnt["e"] % 2
            cnt["e"] += 1
            E = Et[ei]
            for kb, ks in enumerate((kp, kc)):
                mm(psf(kb)[:, 0:256].rearrange("p (r t) -> p r t", r=2), kT[gs, ks], qT[gs, qj:qj + 2, qs], True, True,
                   [B_k, B_q], [BP[kb]])
                act(E[:, kb * 256:(kb + 1) * 256], psf(kb)[:, 0:256], AF.Exp, [BP[kb]], [BE[ei]], scale=0.125)
            tt(V, E, E, mask, MUL, [BE[ei], B_cst], [BE[ei]])
            ob = 2 + (cnt["e"] % 2)
            mm(psf(ob)[:, 0:256], vr[vi_prev][:, g, :], E[:, 0:256], True, False, [BE[ei], Bv[vi_prev]], [BP[ob]])
            mm(psf(ob)[:, 0:256], vr[vi_cur][:, g, :], E[:, 256:512], False, True, [BE[ei], Bv[vi_cur]], [BP[ob]])
            O = psf(ob)[:, 0:256].rearrange("p (r t) -> p r t", r=2)
            if is_A:
                for rr in range(2):
                    h = 2 * g + rr
                    ts(V, rd[0:64, rr * 128:(rr + 1) * 128], O[64:128, rr, :], esink_b[64:128, h:h + 1], None, ADD, None,
                       [BP[ob], B_par], [Brd])
                rcp(rd[0:64, :], rd[0:64, :], [Brd], [Brd])
                tt(V, oT[0:64, 2 * g:2 * g + 2, qs], O[0:64, :, :], rd[0:64, :].rearrange("p (r t) -> p r t", r=2), MUL,
                   [BP[ob], Brd], [B_o])
            else:
                if d == 1:
                    cp(V, acc[:, :, qs], O, [BP[ob]], [B_acc])
                else:
                    tt(V, acc[:, :, qs], acc[:, :, qs], O, ADD, [BP[ob], B_acc], [B_acc])

        for g in range(2):
            vi_p = load_v(1, 0, -1, 0)
            for b in range(16):
                vcur = load_v(1, 0, b, 0)
                unit(kTA, 128, 0, 1, 0, b, g, vi_p, vcur, mA0 if b == 0 else mA, True)
                vi_p = vcur
        for g in range(2):
            for d in (1, 4, 16):
                nb = 16 // d
                for r in range(d):
                    vi_p = load_v(d, r, -1, 128)
                    for b in range(nb):
                        vcur = load_v(d, r, b, 128)
                        unit(kTC, 2048, 2, d, r, b, g, vi_p, vcur, mC0 if b == 0 else mC, False)
                        vi_p = vcur
            for rr in range(2):
                for c0 in range(0, 2048, 256):
                    cs = slice(c0, c0 + 256)
                    rcp(rd[0:64, :], acc[64:128, rr, cs], [B_acc], [Brd])
                    tt(V, oT[0:64, 4 + 2 * g + rr, cs], acc[0:64, rr, cs], rd[0:64, :], MUL, [B_acc, Brd], [B_o])


    def layer_S(l):
        xbS, dtall = state["xbS"], state["dtall"]
        AR.off = state["mark"]
        mk0 = AR.off
        ql = AR.f32(256).rearrange("p (j d) -> p j d", j=4)
        B_ql = Buf()
        for g in range(2):
            P.dma("sp", ql[g * 64:(g + 1) * 64, :, :],
                  sq_scr[g * 64:(g + 1) * 64, :].rearrange("p (j gg d) -> p j gg d", j=4, gg=2)[:, :, g, :],
                  r=[B_sq], w=[B_ql])
        Kbs = [AR.bf16(129 * 64).rearrange("p (k d) -> p k d", d=64), AR.bf16(129 * 64).rearrange("p (k d) -> p k d", d=64)]
        Vb = AR.bf16(129 * 64).rearrange("p (k d) -> p k d", d=64)
        prod = AR.bf16(65 * 64).rearrange("p (k d) -> p k d", d=64)
        sc = AR.f32(2 * 132).rearrange("p (r k) -> p r k", r=2)
        eb = AR.bf16(2 * 132).rearrange("p (r k) -> p r k", r=2)
        sm = AR.f32(64)
        numA = AR.f32(128).rearrange("p (r d) -> p r d", r=2)
        numC = AR.f32(3 * 128).rearrange("p (q r d) -> p q r d", q=3, r=2)
        numh = AR.f32(128).rearrange("p (r d) -> p r d", r=2)
        stC = AR.f32(3 * 4).rearrange("p (q r) -> p q r", q=3)
        oC = AR.f32(128).rearrange("p (r d) -> p r d", r=2)
        B_Ks, B_V, Bt = [Buf(), Buf()], Buf(), Buf()
        rdeps = {id(cak): [], id(cav): [], id(cck): [], id(ccv): [],
                 id(sak): B_cpy[(l, "ak")] + [B_new[(l, "ak")]], id(sav): B_cpy[(l, "av")] + [B_new[(l, "av")]],
                 id(sck): B_cpy[(l, "ck")] + [B_new[(l, "ck")]], id(scv): B_cpy[(l, "cv")] + [B_new[(l, "cv")]]}

        def load_k(pi, specs):
            for g in range(2):
                for t in range(4):
                    lanes = slice(g * 64 + t * 16, g * 64 + (t + 1) * 16)
                    for (j0, n, sk_, sv_, rf) in specs:
                        P.dma("pool", Kbs[pi % 2][lanes, j0:j0 + n, :], sk_[l, :, rf(t), g * 64:(g + 1) * 64],
                              r=rdeps[id(sk_)], w=[B_Ks[pi % 2]])

        def load_v(pi, specs):
            for g in range(2):
                for t in range(4):
                    lanes = slice(g * 64 + t * 16, g * 64 + (t + 1) * 16)
                    for (j0, n, sk_, sv_, rf) in specs:
                        P.dma("pool", Vb[lanes, j0:j0 + n, :], sv_[l, :, rf(t), g * 64:(g + 1) * 64],
                              r=rdeps[id(sv_)], w=[B_V])

        def scores(pi, nk, qj, nm, den):
            Kb, B_K = Kbs[pi % 2], B_Ks[pi % 2]
            halves = ((0, 65), (65, nk))
            for r in range(2):
                for (k0, k1) in halves:
                    tt(V, prod[:, 0:k1 - k0, :], Kb[:, k0:k1, :], bc(ql[:, qj + r, :].unsqueeze(1), [128, k1 - k0, 64]), MUL,
                       [B_K, B_ql, Bt], [Bt])
                    red(V, sc[:, r, k0:k1], prod[:, 0:k1 - k0, :], ADD, [Bt], [Bt])
                red(V, nm[:, r:r + 1], sc[:, r, 0:nk], MAXOP, [Bt], [Bt])
            ts(V, nm, nm, -0.125, None, MUL, None, [Bt], [Bt])
            P.op(V, lambda e: e.memset(den, 0.0), [Bt], [Bt])
            for r in range(2):
                act(eb[:, r, 0:nk], sc[:, r, 0:nk], AF.Exp, [Bt], [Bt], bias=nm[:, r:r + 1], scale=0.125,
                    accum=den[:, r:r + 1])

        def pv(nk, num):
            halves = ((0, 65), (65, nk))
            for r in range(2):
                for hi, (k0, k1) in enumerate(halves):
                    tt(V, prod[:, 0:k1 - k0, :], Vb[:, k0:k1, :], bc(eb[:, r, k0:k1].unsqueeze(2), [128, k1 - k0, 64]), MUL,
                       [B_V, Bt], [Bt])
                    red(V, (num if hi == 0 else numh)[:, r, :], prod[:, 0:k1 - k0, :].rearrange("p k d -> p d k"), ADD,
                        [Bt], [Bt])
                tt(V, num[:, r, :], num[:, r, :], numh[:, r, :], ADD, [Bt], [Bt])

        specA = [(0, 124, cak, cav, lambda t: slice(t + 1, t + 125)),
                 (124, 4, sak, sav, lambda t: slice(t + 121, t + 125))]
        pats = [
            [(0, 125, cck, ccv, lambda t: slice(t + 1920, t + 2045)), (125, 4, sck, scv, lambda t: slice(t + 2041, t + 2045))],
            [(0, 128, cck, ccv, lambda t: slice(1536 + t, 1536 + t + 4 * 127 + 1, 4)), (128, 1, sck, scv, lambda t: slice(2044 + t, 2045 + t))],
            [(0, 128, cck, ccv, lambda t: slice(t, t + 16 * 127 + 1, 16)), (128, 1, sck, scv, lambda t: slice(2044 + t, 2045 + t))],
        ]
        allp = [(specA, 128, 0)] + [(p_, 129, 2) for p_ in pats]
        nmA, denA, t2 = sm[:, 0:2], sm[:, 2:4], sm[:, 4:6]
        outs = [(numA, nmA, denA)] + [(numC[:, q_], stC[:, q_, 0:2], stC[:, q_, 2:4]) for q_ in range(3)]
        load_k(0, allp[0][0])
        load_v(0, allp[0][0])
        for pi, (specs, nk, qj) in enumerate(allp):
            if pi + 1 < 4:
                load_k(pi + 1, allp[pi + 1][0])
            num, nm, den = outs[pi]
            scores(pi, nk, qj, nm, den)
            pv(nk, num)
            if pi + 1 < 4:
                load_v(pi + 1, allp[pi + 1][0])
        for r in range(2):
            act(t2[:, r:r + 1], nmA[:, r:r + 1], AF.Exp, [Bt, B_par], [Bt], bias=sinkl[:, r:r + 1])
        tt(V, denA, denA, t2, ADD, [Bt], [Bt])
        rcp(denA, denA, [Bt], [Bt])
        tt(V, numA, numA, bc(denA.unsqueeze(2), [128, 2, 64]), MUL, [Bt], [Bt])
        for g in range(2):
            P.dma("sp", smix[:, g * 128:(g + 1) * 128], numA[g * 64:(g + 1) * 64, :, :], r=[Bt], w=[B_smix])
        mn, w3 = sm[:, 8:10], sm[:, 10:16].rearrange("p (q r) -> p q r", q=3)
        tt(V, mn, stC[:, 0, 0:2], stC[:, 1, 0:2], ALU.min, [Bt], [Bt])
        tt(V, mn, mn, stC[:, 2, 0:2], ALU.min, [Bt], [Bt])
        for q_ in range(3):
            for r in range(2):
                act(w3[:, q_, r:r + 1], stC[:, q_, r:r + 1], AF.Exp, [Bt], [Bt], bias=mn[:, r:r + 1], scale=-1.0)
        dn = sm[:, 16:18]
        tt(V, stC[:, :, 2:4], stC[:, :, 2:4], w3, MUL, [Bt], [Bt])
        tt(V, dn, stC[:, 0, 2:4], stC[:, 1, 2:4], ADD, [Bt], [Bt])
        tt(V, dn, dn, stC[:, 2, 2:4], ADD, [Bt], [Bt])
        rcp(dn, dn, [Bt], [Bt])
        for q_ in range(3):
            tt(V, numC[:, q_], numC[:, q_], bc(w3[:, q_, :].unsqueeze(2), [128, 2, 64]), MUL, [Bt], [Bt])
        tt(V, oC, numC[:, 0], numC[:, 1], ADD, [Bt], [Bt])
        tt(V, oC, oC, numC[:, 2], ADD, [Bt], [Bt])
        tt(V, oC, oC, bc(dn.unsqueeze(2), [128, 2, 64]), MUL, [Bt], [Bt])
        for g in range(2):
            P.dma("sp", smix[:, 256 + g * 128:256 + (g + 1) * 128], oC[g * 64:(g + 1) * 64, :, :], r=[Bt], w=[B_smix])

        P.barrier()
        AR.off = mk0
        pre = AR.f32(1024)
        xcS = AR.bf16(8 * 112).rearrange("p (c s b) -> p c s b", c=8, s=7)
        acc = AR.f32(512).rearrange("p (c k) -> p c k", c=8)
        xaS = AR.bf16(512).rearrange("p (c k) -> p c k", c=8)
        xaT = AR.f32(1024)
        Bc = Buf()
        P.dma("sp", pre[0:48, :], sconv[l], w=[Bc])
        for c in range(8):
            tr(psf(4 + c // 4)[:, (c % 4) * 48:(c % 4 + 1) * 48], pre[0:48, c * 128:(c + 1) * 128], identF[0:48, 0:48],
               [Bc, B_cst], [BP[4 + c // 4]])
        for bk in range(2):
            cp(V, xcS[:, 4 * bk:4 * bk + 4, 0:3, :],
               psf(4 + bk)[:, 0:192].rearrange("p (c b j) -> p c j b", c=4, j=3), [BP[4 + bk]], [Bc])
        cp(V, xcS[:, :, 3:7, :].rearrange("p c s b -> p c (s b)"), xbS, [state["B_xbc"]], [Bc])
        xf = xcS.rearrange("p c s b -> p c (s b)")
        for c in range(8):
            ts(V, acc[:, c, :], xf[:, c, 0:64], convw[:, c, 0:1], None, MUL, None, [Bc, B_par], [Bc])
            for j in range(1, 4):
                stt(V, acc[:, c, :], xf[:, c, 16 * j:16 * j + 64], convw[:, c, j:j + 1], acc[:, c, :], MUL, ADD,
                    [Bc, B_par], [Bc])
            act(xaS[:, c, :], acc[:, c, :], AF.Silu, [Bc, B_par], [Bc], bias=convb[:, c:c + 1])
        for c in range(8):
            tr(psb(6)[0:64, c * 128:(c + 1) * 128], xaS[:, c, :], identB, [Bc, B_cst], [BP[6]])
        cp(V, xaT[0:64, :], psb(6)[0:64, 0:1024], [BP[6]], [Bc])
        P.dma("sp", sx_scr[:, :], xaT[0:64, :], r=[Bc], w=[B_sx])
        P.barrier()
        AR.off = mk0
        xl = AR.f32(256).rearrange("p (t d) -> p t d", t=4)
        bl = AR.f32(512).rearrange("p (t n) -> p t n", t=4)
        cl = AR.f32(512).rearrange("p (t n) -> p t n", t=4)
        yl = AR.f32(256).rearrange("p (t d) -> p t d", t=4)
        dl = AR.f32(4)
        s2 = AR.f32(16)
        xd = AR.f32(32)
        hS = AR.f32(4096).rearrange("p (d n) -> p d n", d=32)
        tmp = AR.f32(4096).rearrange("p (d n) -> p d n", d=32)
        Bl, Bh, Bs = Buf(), Buf(), Buf()
        for t in range(4):
            P.dma("sp", xl[:, t, :], bass.AP(sx_scr.tensor, t * 16 * 1024, [[64, 8], [1024, 16], [1, 64]]), r=[B_sx], w=[Bl])
            for g in range(2):
                P.dma("sp", bl[g * 64:(g + 1) * 64, t, :],
                      bass.AP(sx_scr.tensor, t * 16 * 1024 + 512 + g * 128, [[0, 4], [1024, 16], [1, 128]]), r=[B_sx], w=[Bl])
                P.dma("sp", cl[g * 64:(g + 1) * 64, t, :],
                      bass.AP(sx_scr.tensor, t * 16 * 1024 + 768 + g * 128, [[0, 4], [1024, 16], [1, 128]]), r=[B_sx], w=[Bl])
        for t in range(4):
            P.dma("sp", dl[:, t:t + 1], bass.AP(sdt_scr.tensor, t * 128, [[1, 8], [8, 16], [1, 1]]), r=[B_sdt], w=[Bl],
                  allow_slow_non_contiguous=True)
        a4, dA4 = s2[:, 0:4], s2[:, 4:8]
        ts(V, a4, dl, anegl, None, MUL, None, [Bl, B_par], [Bs])
        act(dA4, a4, AF.Exp, [Bs], [Bs])
        for ph in range(2):
            st_in = bass.AP(sssm.tensor, l * SB * 8 * 8192 + ph * 4096, [[8192, 8], [65536, 16], [1, 4096]])
            st_out = bass.AP(sssm_o.tensor, l * SB * 8 * 8192 + ph * 4096, [[8192, 8], [65536, 16], [1, 4096]])
            P.dma("sp", hS.rearrange("p d n -> p (d n)"), st_in, w=[Bh])
            for t in range(4):
                ts(V, xd, xl[:, t, ph * 32:(ph + 1) * 32], dl[:, t:t + 1], None, MUL, None, [Bl], [Bs])
                tt(V, tmp, bc(xd.unsqueeze(2), [128, 32, 128]), bc(bl[:, t, :].unsqueeze(1), [128, 32, 128]), MUL,
                   [Bs, Bl], [Bs])
                stt(V, hS, hS, dA4[:, t:t + 1], tmp, MUL, ADD, [Bs, Bh], [Bh])
                tt(V, tmp, hS, bc(cl[:, t, :].unsqueeze(1), [128, 32, 128]), MUL, [Bh, Bl, Bs], [Bs])
                red(V, yl[:, t, ph * 32:(ph + 1) * 32], tmp, ADD, [Bs], [Bs])
            P.dma("sp", st_out, hS.rearrange("p d n -> p (d n)"), r=[Bh], w=OUTW)
        for t in range(4):
            P.dma("sp", bass.AP(sy_scr.tensor, t * 16 * 512, [[64, 8], [512, 16], [1, 64]]), yl[:, t, :], r=[Bs], w=[B_sy])

    def layer_P4s(l):
        oT, B_o = state["oT"], state["B_o"]
        AR.off = state["mark3"]
        sf = AR.f32(512)
        sbf = AR.bf16(512)
        Bt = Buf()
        for hh in range(2):
            P.dma("sp", sf[hh * 64:(hh + 1) * 64, :], smix[:, :], r=[B_smix], w=[Bt])
        cp(V, sbf, sf, [Bt], [Bt])
        for hi in range(8):
            tr(psb(7)[0:64, hi * 128:(hi + 1) * 128], sbf[:, hi * 64:(hi + 1) * 64], identB, [Bt, B_cst], [BP[7]])
        cp(V, oT[0:64, :, 2048:2176], psb(7)[0:64, 0:1024].rearrange("p (h t) -> p h t", h=8), [BP[7]], [B_o])

    def layer_P5(l, tiles):
        mo_all, B_mo, oT, B_o = state["mo_all"], state["B_mo"], state["oT"], state["B_o"]
        AR.off = state["mark3"]
        WoA = AR.bf16(8 * 1024).rearrange("p (h n) -> p h n", h=8)
        WoM = AR.bf16(4 * 1024).rearrange("p (c n) -> p c n", c=4)
        moT = AR.bf16(512)
        B_wo, Bt = Buf(), Buf()
        P.dma("pool", WoA[0:64, 0:4, :], w_out[l, 0:256, :].rearrange("(h p) n -> p h n", p=64), w=[B_wo])
        P.dma("pool", WoA[0:64, 4:8, :], w_out[l, 768:1024, :].rearrange("(h p) n -> p h n", p=64), w=[B_wo])
        for c in range(4):
            P.dma("pool", WoM[:, c, :], w_out[l, 256 + c * 128:256 + (c + 1) * 128, :], w=[B_wo])
        for t in tiles:
            for j in range(4):
                tr(psb(6)[:, j * 128:(j + 1) * 128], mo_all[:, t, j * 128:(j + 1) * 128], identB, [B_mo, B_cst], [BP[6]])
            cp(A_, moT, psb(6)[:, 0:512], [BP[6]], [Bt])
            hx, bx = (H[:, t, :], BH[t]) if t < 16 else (HS, BH[16])
            for half in range(2):
                hs_ = slice(half * 512, (half + 1) * 512)
                for hi in range(8):
                    mm(psf(half), oT[0:64, hi, t * 128:(t + 1) * 128], WoA[0:64, hi, hs_], hi == 0, False,
                       [B_o, B_wo], [BP[half]])
                for c in range(4):
                    mm(psf(half), moT[:, c * 128:(c + 1) * 128], WoM[:, c, hs_], False, c == 3, [Bt, B_wo], [BP[half]])
                tt(V, hx[:, hs_], hx[:, hs_], psf(half), ADD, [BP[half], bx], [bx])

    def layer_P6(l, tiles):
        AR.reset()

        uT2 = AR.bf16(17 * 1024).rearrange("p (t c k) -> p t c k", t=17, c=8)
        ubf = AR.bf16(1024)
        junk = AR.bf16(1024)
        sm = AR.f32(32)
        NRING = 3
        ring = [(AR.bf16(8 * 256).rearrange("p (c f) -> p c f", c=8), AR.bf16(8 * 256).rearrange("p (c f) -> p c f", c=8),
                 AR.bf16(2 * 1024).rearrange("p (c n) -> p c n", c=2), Buf()) for _ in range(NRING)]
        sg = [AR.f32(512), AR.f32(512)]
        aT = [AR.bf16(1024).rearrange("p (c k) -> p c k", c=2), AR.bf16(1024).rearrange("p (c k) -> p c k", c=2)]
        Bsg = [Buf(), Buf()]
        BaT = [Buf(), Buf()]
        B_u, Bt = Buf(), Buf()
        for t in tiles:
            hx, bx = (H[:, t, :], BH[t]) if t < 16 else (HS, BH[16])
            rmsnorm_T(hx, bx, g2b, ubf, uT2[:, t].rearrange("p c k -> p (c k)"), sm, junk, [Bt])
        P.op(V, lambda e: e.memset(sm[:, 0:1], 0.0), [Bt], [B_u])
        blocks = [[t for t in tiles if t // 4 == b] for b in range(5)]
        blocks = [b for b in blocks if b]
        ng = DFF // 256
        k = 0
        for gi in range(ng):
            Wg, Wu, Wd, Bw = ring[gi % NRING]
            f0 = gi * 256
            P.dma("pool", Wg, w_gate[l, :, f0:f0 + 256].rearrange("(c p) f -> p c f", p=128), w=[Bw])
            P.dma("pool", Wu, w_up[l, :, f0:f0 + 256].rearrange("(c p) f -> p c f", p=128), w=[Bw])
            P.dma("pool", Wd, w_down[l, f0:f0 + 256, :].rearrange("(c p) n -> p c n", p=128), w=[Bw])
            for blk in blocks:
                nt = len(blk)
                ntok = 128 * nt
                t0 = blk[0]
                i2 = k % 2
                k += 1
                for fc in range(2):
                    for (W_, bk) in ((Wg, fc), (Wu, 2 + fc)):
                        for c in range(8):
                            mm(psf(bk)[:, 0:ntok].rearrange("p (t k) -> p t k", t=nt), W_[:, c, fc * 128:(fc + 1) * 128],
                               uT2[:, t0:t0 + nt, c, :], c == 0, c == 7, [Bw, B_u], [BP[bk]])
                    act(sg[i2][:, 0:ntok], psf(fc)[:, 0:ntok], AF.Silu, [BP[fc]], [Bsg[i2]])
                    tt(V, aT[i2][:, fc, 0:ntok], sg[i2][:, 0:ntok], psf(2 + fc)[:, 0:ntok], MUL, [Bsg[i2], BP[2 + fc]], [BaT[i2]])
                for ti, t in enumerate(blk):
                    hx, bx = (H[:, t, :], BH[t]) if t < 16 else (HS, BH[16])
                    for half in range(2):
                        bk = 4 + 2 * (ti % 2) + half
                        hs_ = slice(half * 512, (half + 1) * 512)
                        for fc in range(2):
                            mm(psf(bk), aT[i2][:, fc, ti * 128:(ti + 1) * 128], Wd[:, fc, hs_], fc == 0, fc == 1,
                               [BaT[i2], Bw], [BP[bk]])
                        tt(V, hx[:, hs_], hx[:, hs_], psf(bk), ADD, [BP[bk], bx], [bx])

    def write_y():
        for t in range(NT):
            P.dma("sp", yp[t * 128:(t + 1) * 128, :], H[:, t, :], r=[BH[t]], w=OUTW)
        P.dma("sp", ys[:, :], HS[0:64, :], r=[BH[16]], w=OUTW)

    PT = list(range(NT))
    for l in range(DEPTH if STAGE >= 2 else 1):
        layer_P1(l)
        P.barrier()
        layer_S(l)
        P.barrier()
        layer_P23(l)
        P.barrier()
        layer_P4(l)
        P.barrier()
        layer_P4s(l)
        P.barrier()
        layer_P5(l, PT + [16])
        P.barrier()
        layer_P6(l, PT + [16])
        P.barrier()
    write_y()
    P.finish(es)
    return nc, P, es


def _perm_cols():
    idx = []
    for base in (0, 640):
        for r in range(2):
            for g in range(2):
                h = 2 * g + r
                idx += list(range(base + h * 64, base + (h + 1) * 64))
    return idx


def _w_in_perm():
    ref = dict(aq=0, ak=256, av=384, cq=512, ck=768, cv=896, z=1024, xbc=1536, dt=2560)
    idx = []
    for nm in ("aq", "cq"):
        for r in range(2):
            for g in range(2):
                h = 2 * g + r
                idx += list(range(ref[nm] + h * 64, ref[nm] + (h + 1) * 64))
    idx += list(range(ref["ak"], ref["ak"] + 128))
    idx += list(range(ref["ck"], ref["ck"] + 128))
    idx += list(range(ref["av"], ref["av"] + 128))
    idx += list(range(ref["cv"], ref["cv"] + 128))
    idx += list(range(ref["z"], ref["z"] + 512))
    idx += list(range(ref["dt"], ref["dt"] + 8))
    idx += list(range(ref["xbc"], ref["xbc"] + 1024))
    assert len(idx) == NIN
    return np.array(idx)


def _consts(c):
    cf = np.zeros((128, 1344), np.float32)
    i = np.arange(128)
    cf[:, 0:128] = np.eye(128, dtype=np.float32)
    cf[:, 128:256] = (i[:, None] <= i[None, :])
    cf[:, 256:384] = (i[:, None] > i[None, :])
    cf[:, 384:512] = 1.0
    half = 8
    inv = (np.float32(500000.0) ** (-(np.arange(half, dtype=np.float32) * np.float32(2.0) / np.float32(16)))).astype(np.float32)
    pos = np.zeros((128, 17), np.float32)
    for t in range(16):
        pos[:, t] = c * TOK + t * 128 + i
    pos[:, 16] = 16384 + (i % 64) // 16
    ang = pos[:, :, None] * inv[None, None, :]
    cf[:, 512:648] = np.cos(ang).astype(np.float32).reshape(128, 136)
    cf[:, 648:784] = np.sin(ang).astype(np.float32).reshape(128, 136)
    co, si = np.cos(ang).astype(np.float32), np.sin(ang).astype(np.float32)
    cf[:, 800:1072] = np.concatenate([co, si], axis=2).reshape(128, 272)
    cf[:, 1072:1344] = np.concatenate([si, co], axis=2).reshape(128, 272)
    cf[:, 784] = 1.0 if c > 0 else 0.0
    cf[:, 785:793] = (np.arange(8)[None, :] < c).astype(np.float32)
    cb = np.zeros((128, 2304), np.float32)
    cb[:, 0:128] = np.eye(128)
    cur = (i[:, None] <= i[None, :]).astype(np.float32)
    prevA = (i[:, None] >= i[None, :] + 1).astype(np.float32)
    prevC = (i[:, None] >= i[None, :]).astype(np.float32)
    pvf = 1.0 if c > 0 else 0.0
    cb[:, 128:640] = np.concatenate([prevA * pvf, prevA * pvf, cur, cur], axis=1)
    cb[:, 640:1152] = np.concatenate([prevA, prevA, cur, cur], axis=1)
    cb[:, 1152:1664] = np.concatenate([prevC * pvf, prevC * pvf, cur, cur], axis=1)
    cb[:, 1664:2176] = np.concatenate([prevC, prevC, cur, cur], axis=1)
    cb[:, 2176:2304] = cur
    return cf, cb.astype(ml_dtypes.bfloat16)


def _params(inp):
    par = np.zeros((DEPTH, 128, 3456), np.float32)
    p = np.arange(128)
    for l in range(DEPTH):
        a = par[l]
        a[:, 0:1024] = inp["norm1"][l][None, :]
        a[:, 1024:2048] = inp["norm2"][l][None, :]
        a[:, 2048:2560] = np.concatenate([np.tile(inp["a_qn"][l], 4), np.tile(inp["c_qn"][l], 4)])[None, :]
        a[:, 2560:2816] = np.concatenate([np.tile(inp["a_kn"][l], 2), np.tile(inp["c_kn"][l], 2)])[None, :]
        a[:, 2816:3328] = inp["ssm_norm"][l][None, :]
        a[:, 3328:3332] = inp["a_sinks"][l][None, :]
        a[:, 3332:3340] = inp["dt_bias"][l][None, :]
        a[:, 3340:3348] = inp["a_log"][l][None, :]
        a[:, 3348:3356] = inp["d_skip"][l][None, :]
        cw = inp["conv_w"][l].reshape(4, 8, 128)
        a[:, 3356:3388] = cw.transpose(2, 1, 0).reshape(128, 32)
        a[:, 3388:3396] = inp["conv_b"][l].reshape(8, 128).T
        a[:, 3400:3402] = inp["a_sinks"][l].reshape(2, 2)[p // 64]
        a[:, 3402] = inp["a_log"][l][p // 16]
        a[:, 3403] = inp["d_skip"][l][p // 16]
        a[:, 3404] = inp["dt_bias"][l][p // 16]
    return par


_CACHE = {}


def kernel(**inp):
    inp = {k: np.asarray(v) for k, v in inp.items()}
    if "nc" not in _CACHE:
        _CACHE["nc"] = build()
    nc, P, es = _CACHE["nc"]
    perm = _w_in_perm()
    w_in_p = np.ascontiguousarray(inp["w_in"][:, :, perm])
    par = _params(inp)
    in_maps = []
    for c in range(NCORES):
        cf, cb = _consts(c)
        bs = slice(c * SB, (c + 1) * SB)
        m = {
            "xp": np.ascontiguousarray(inp["x_prompt"][0, c * TOK:(c + 1) * TOK]),
            "xs": np.ascontiguousarray(inp["x_sample"][bs].transpose(1, 0, 2).reshape(64, D)),
            "cak": np.ascontiguousarray(inp["cache_a_k"][:, bs].reshape(DEPTH, SB, 128, 128)),
            "cav": np.ascontiguousarray(inp["cache_a_v"][:, bs].reshape(DEPTH, SB, 128, 128)),
            "cck": np.ascontiguousarray(inp["cache_c_k"][:, bs].reshape(DEPTH, SB, 2048, 128)),
            "ccv": np.ascontiguousarray(inp["cache_c_v"][:, bs].reshape(DEPTH, SB, 2048, 128)),
            "sssm": np.ascontiguousarray(inp["state_ssm"][:, bs].reshape(DEPTH, SB, 8, 8192)),
            "sconv": np.ascontiguousarray(inp["state_conv"][:, bs].reshape(DEPTH, SB * 3, 1024)),
            "w_in": w_in_p, "w_out": inp["w_out"], "w_gate": inp["w_gate"], "w_up": inp["w_up"],
            "w_down": inp["w_down"], "par": par, "cst_f": cf, "cst_b": cb,
        }
        in_maps.append(m)
    res = run_bass_kernel_spmd(nc, in_maps, core_ids=list(range(NCORES)))
    R = res.results
    _CACHE["last"] = R
    y_prompt = np.concatenate([R[c]["yp"] for c in range(NCORES)], axis=0)[None]
    y_sample = np.concatenate([R[c]["ys"].reshape(4, SB, D).transpose(1, 0, 2) for c in range(NCORES)], axis=0)
    L = NCORES - 1
    p_a_k = R[L]["pak"].reshape(DEPTH, 1, 128, 2, 64)
    p_a_v = R[L]["pav"].reshape(DEPTH, 1, 128, 2, 64)
    p_c_k = R[L]["pck"].reshape(DEPTH, 1, 2048, 2, 64)
    p_c_v = R[L]["pcv"].reshape(DEPTH, 1, 2048, 2, 64)
    p_ssm = R[L]["pssm"].reshape(DEPTH, 1, 8, 64, 128)
    p_conv = R[L]["pconv"].reshape(DEPTH, 1, 3, 1024)

    def cat(name, shape):
        return np.concatenate([R[c][name].reshape((DEPTH, SB) + shape) for c in range(NCORES)], axis=1)

    s_a_k = cat("sak", (128, 2, 64))
    s_a_v = cat("sav", (128, 2, 64))
    s_c_k = cat("sck", (2048, 2, 64))
    s_c_v = cat("scv", (2048, 2, 64))
    s_ssm = cat("sssm_o", (8, 64, 128))
    s_conv = cat("sconv_o", (3, 1024))
    return (y_prompt, y_sample, p_a_k, p_a_v, p_c_k, p_c_v, p_ssm, p_conv,
            s_a_k, s_a_v, s_c_k, s_c_v, s_ssm, s_conv)
```

```python
import numpy as np
import ml_dtypes
from contextlib import ExitStack
import concourse.bass as bass
import concourse.mybir as mybir
from concourse.bass_utils import run_bass_kernel_spmd

F32 = mybir.dt.float32
BF16 = mybir.dt.bfloat16
AF = mybir.ActivationFunctionType
ALU = mybir.AluOpType
AX = mybir.AxisListType

NCORES = 8
D = 1024
DEPTH = 2
TOK = 2048
NT = 16
SB = 16
NIN = 2568
DFF = 2816
EPS = 1e-6
C_QA, C_QC, C_KA, C_KC, C_VA, C_VC, C_Z, C_DT, C_X = 0, 256, 512, 640, 768, 896, 1024, 1536, 1544
NR1 = 2048 + 1024 + 64 + 12
STAGE = 2


class Buf:
    __slots__ = ("w", "r")

    def __init__(self):
        self.w = None
        self.r = []


class Prog:
    ENG = ("pe", "act", "dve", "pool", "sp")
    NDSEM = {"sp": 12, "pool": 24, "act": 20}

    def __init__(self, nc):
        self.nc = nc
        self.ops = []
        self.last = {e: None for e in self.ENG}
        self.pend = {e: {} for e in self.ENG}
        self.dmas = []

    def _emit(self, eng, fn, r, w, kind):
        deps = dict(self.pend[eng])
        self.pend[eng] = {}
        for b in r:
            if b.w is not None:
                deps[b.w] = "raw"
        for b in w:
            if b.w is not None:
                deps[b.w] = "raw"
            for j in b.r:
                deps.setdefault(j, "war")
        i = len(self.ops)
        self.ops.append(dict(eng=eng, fn=fn, deps=deps, kind=kind))
        for b in r:
            if kind == "c":
                b.r = [j for j in b.r if not (self.ops[j]["kind"] == "c" and self.ops[j]["eng"] == eng)]
            b.r.append(i)
        for b in w:
            b.w = i
            b.r = []
        if kind in ("dma", "cc"):
            self.dmas.append(i)
        else:
            self.last[eng] = i
        return i

    def op(self, eng, fn, r=(), w=()):
        return self._emit(eng, fn, r, w, "c")

    def dma(self, q, out, in_, r=(), w=(), **kw):
        return self._emit(q, lambda e: e.dma_start(out=out, in_=in_, **kw), r, w, "dma")

    def cc(self, fn, r=(), w=()):
        return self._emit("pool", fn, r, w, "cc")

    def barrier(self):
        allp = {}
        for e in self.ENG:
            if self.last[e] is not None:
                allp[self.last[e]] = "raw"
        for j in self.dmas:
            allp[j] = "raw"
        self.dmas = []
        for e in self.ENG:
            self.pend[e].update(allp)

    def finish(self, es):
        nc = self.nc
        ops = self.ops
        self.barrier()
        fin = dict(self.pend["sp"])
        needed = set()

        def real_dep(i, j, kind):
            oi, oj = ops[i], ops[j]
            if oj["kind"] in ("dma", "cc"):
                return True
            if oi["eng"] == oj["eng"] and oi["kind"] != "dma":
                if oi["eng"] == "pe":
                    return False
                return kind == "raw"
            return True

        for i, o in enumerate(ops):
            for j, k in o["deps"].items():
                if ops[j]["kind"] == "c" and real_dep(i, j, k):
                    needed.add(j)
        for j in fin:
            if ops[j]["kind"] == "c":
                needed.add(j)
        seq = {}
        cnt = {e: 0 for e in self.ENG}
        dcount = {q: 0 for q in self.NDSEM}
        dtok = {}
        for i, o in enumerate(ops):
            if o["kind"] == "dma":
                q = o["eng"]
                k = dcount[q]
                dcount[q] += 1
                R = self.NDSEM[q]
                dtok[i] = (q, k % R, 16 * (k // R + 1))
            elif o["kind"] == "cc":
                dtok[i] = ("cc", i, 1)
            elif i in needed:
                cnt[o["eng"]] += 1
                seq[i] = cnt[o["eng"]]
        ccsem = {i: es.enter_context(nc.semaphore("cc%d" % i)) for i, o in enumerate(ops) if o["kind"] == "cc"}
        csem = {e: es.enter_context(nc.semaphore("c_" + e)) for e in ("pe", "act", "dve", "pool")}
        dsem = {q: [es.enter_context(nc.semaphore("d_%s%d" % (q, k))) for k in range(R)]
                for q, R in self.NDSEM.items()}
        streams = {e: [] for e in self.ENG}
        waited = {e: {} for e in self.ENG}

        def add_wait(e, lst, key, sem, val):
            if waited[e].get(key, 0) >= val:
                return
            waited[e][key] = val
            lst.append((sem, val))

        for i, o in enumerate(ops):
            e = o["eng"]
            waits = []
            for j, k in sorted(o["deps"].items()):
                if not real_dep(i, j, k):
                    continue
                if ops[j]["kind"] == "dma":
                    q, s, tgt = dtok[j]
                    add_wait(e, waits, ("d", q, s), dsem[q][s], tgt)
                elif ops[j]["kind"] == "cc":
                    add_wait(e, waits, ("cc", j), ccsem[j], 1)
                else:
                    f = ops[j]["eng"]
                    add_wait(e, waits, ("c", f), csem[f], seq[j])
            inc = None
            if o["kind"] == "dma":
                q, s, tgt = dtok[i]
                if tgt > 16:
                    add_wait(e, waits, ("d", q, s), dsem[q][s], tgt - 16)
                inc = (dsem[q][s], 16)
            elif o["kind"] == "cc":
                inc = (ccsem[i], 1)
            elif i in needed:
                inc = (csem[e], 1)
            streams[e].append((waits, o["fn"], inc))
        finw = []
        for j in sorted(fin):
            if ops[j]["kind"] == "dma":
                q, s, tgt = dtok[j]
                add_wait("sp", finw, ("d", q, s), dsem[q][s], tgt)
            elif ops[j]["kind"] == "cc":
                add_wait("sp", finw, ("cc", j), ccsem[j], 1)
            else:
                f = ops[j]["eng"]
                add_wait("sp", finw, ("c", f), csem[f], seq[j])
        for q, R in self.NDSEM.items():
            for s in range(R):
                k = dcount[q]
                n = (k - s + R - 1) // R if k > s else 0
                if n > 0:
                    add_wait("sp", finw, ("d", q, s), dsem[q][s], 16 * n)
        self.stats = dict(n_ops=len(ops), needed=len(needed), cnt=cnt, dcount=dcount)

        def run(eng_obj, lst, tail=()):
            for waits, fn, inc in lst:
                for sem, val in waits:
                    eng_obj.wait_ge(sem, val)
                ins = fn(eng_obj)
                if inc is not None:
                    ins.then_inc(inc[0], inc[1])
            for sem, val in tail:
                eng_obj.wait_ge(sem, val)

        with nc.Block() as block:
            @block.tensor
            def _(e):
                run(e, streams["pe"])

            @block.scalar
            def _(e):
                run(e, streams["act"])

            @block.vector
            def _(e):
                run(e, streams["dve"])

            @block.gpsimd
            def _(e):
                run(e, streams["pool"])

            @block.sync
            def _(e):
                run(e, streams["sp"], finw)


class Arena:
    def __init__(self, handle, nf32):
        self.h = handle
        self.n = nf32
        self.off = 0

    def reset(self):
        self.off = 0

    def f32(self, n):
        a = self.h[:, self.off:self.off + n]
        self.off += n
        assert self.off <= self.n, (self.off, self.n)
        return a

    def bf16(self, n):
        m = (n + 1) // 2
        a = self.h[:, self.off:self.off + m].bitcast(BF16)
        self.off += m
        assert self.off <= self.n, (self.off, self.n)
        return a[:, 0:n]


def build():
    nc = bass.Bass("TRN2", target_bir_lowering=False)
    P = Prog(nc)
    es = ExitStack()

    def din(name, shape, dt=F32):
        return nc.dram_tensor(name, list(shape), dt, kind="ExternalInput").ap()

    def dout(name, shape, dt=F32):
        return nc.dram_tensor(name, list(shape), dt, kind="ExternalOutput").ap()

    def dscr(name, shape, dt=F32):
        return nc.dram_tensor(name, list(shape), dt).ap()

    def sb(name, shape, dt=F32):
        return es.enter_context(nc.sbuf_tensor(name, list(shape), dt))

    xp = din("xp", [TOK, D])
    xs = din("xs", [64, D])
    cak = din("cak", [DEPTH, SB, 128, 128])
    cav = din("cav", [DEPTH, SB, 128, 128])
    cck = din("cck", [DEPTH, SB, 2048, 128])
    ccv = din("ccv", [DEPTH, SB, 2048, 128])
    sssm = din("sssm", [DEPTH, SB, 8, 8192])
    sconv = din("sconv", [DEPTH, SB * 3, 1024])
    w_in = din("w_in", [DEPTH, D, NIN])
    w_out = din("w_out", [DEPTH, D, D])
    w_gate = din("w_gate", [DEPTH, D, DFF])
    w_up = din("w_up", [DEPTH, D, DFF])
    w_down = din("w_down", [DEPTH, DFF, D])
    par_in = din("par", [DEPTH, 128, 3456])
    cst_f = din("cst_f", [128, 1344])
    cst_b = din("cst_b", [128, 2304], BF16)

    yp = dout("yp", [TOK, D])
    ys = dout("ys", [64, D])
    pak = dout("pak", [DEPTH, 128, 128])
    pav = dout("pav", [DEPTH, 128, 128])
    pck = dout("pck", [DEPTH, TOK, 128])
    pcv = dout("pcv", [DEPTH, TOK, 128])
    pssm = dout("pssm", [DEPTH, 512, 128])
    pconv = dout("pconv", [DEPTH, 3, 1024])
    sak = dout("sak", [DEPTH, SB, 128, 128])
    sav = dout("sav", [DEPTH, SB, 128, 128])
    sck = dout("sck", [DEPTH, SB, 2048, 128])
    scv = dout("scv", [DEPTH, SB, 2048, 128])
    sssm_o = dout("sssm_o", [DEPTH, SB, 8, 8192])
    sconv_o = dout("sconv_o", [DEPTH, SB * 3, 1024])

    bn1 = dscr("bn1", [NR1, 256], BF16)
    g1 = dscr("g1", [NCORES * NR1, 256], BF16)
    bn2 = dscr("bn2", [129, 512])
    g2 = dscr("g2", [NCORES * 129, 512])
    zs_scr = dscr("zs_scr", [17, 128, 512], BF16)
    qt_scr = dscr("qt_scr", [128, 4 * 2176], BF16)
    kta_scr = dscr("kta_scr", [128, 2048], BF16)
    sq_scr = dscr("sq_scr", [128, 512])
    sx_scr = dscr("sx_scr", [64, 1024])
    sdt_scr = dscr("sdt_scr", [64, 8])
    sy_scr = dscr("sy_scr", [64, 512])
    smix = dscr("smix", [64, 512])
    B_sq, B_sx, B_sdt, B_sy, B_smix = Buf(), Buf(), Buf(), Buf(), Buf()
    vprev = dscr("vprev", [2048, 256], BF16)
    B_vprev = Buf()
    B_kta = Buf()
    B_bn1, B_g1, B_bn2, B_g2, B_zs, B_qt = Buf(), Buf(), Buf(), Buf(), Buf(), Buf()
    RG = [list(range(NCORES))]

    Hh = sb("H", [128, NT * D])
    H = Hh[:, :].rearrange("p (t d) -> p t d", t=NT)
    BH = [Buf() for _ in range(NT + 1)]
    HSh = sb("HS", [128, D])
    HS = HSh[:, :]
    CFh = sb("CF", [128, 1344])
    CF = CFh[:, :]
    CBh = sb("CB", [128, 2304], BF16)
    CB = CBh[:, :]
    B_cst = Buf()
    identF, triU, triS, onesF = CF[:, 0:128], CF[:, 128:256], CF[:, 256:384], CF[:, 384:512]
    cosT = CF[:, 512:648].rearrange("p (t k) -> p t k", t=17)
    sinT = CF[:, 648:784].rearrange("p (t k) -> p t k", t=17)
    cs1T = CF[:, 800:1072].rearrange("p (t k) -> p t k", t=17)
    cs2T = CF[:, 1072:1344].rearrange("p (t k) -> p t k", t=17)
    pv = CF[:, 784:785]
    rmask = CF[:, 785:793]
    identB = CB[:, 0:128]
    mA0, mA, mC0, mC = CB[:, 128:640], CB[:, 640:1152], CB[:, 1152:1664], CB[:, 1664:2176]
    causB = CB[:, 2176:2304]
    PARh = sb("PAR", [128, 3456])
    PAR = PARh[:, :]
    B_par = Buf()
    g1b, g2b = PAR[:, 0:1024], PAR[:, 1024:2048]
    qkg = PAR[:, 2048:2816].rearrange("p (h d) -> p h d", h=12)
    qg = PAR[:, 2048:2560].rearrange("p (h d) -> p h d", h=8)
    kg = PAR[:, 2560:2816].rearrange("p (h d) -> p h d", h=4)
    ssmn = PAR[:, 2816:3328]
    sinks_b = PAR[:, 3328:3332]
    dtb_b = PAR[:, 3332:3340]
    aneg_b = PAR[:, 3340:3348]
    dsk_b = PAR[:, 3348:3356]
    convw = PAR[:, 3356:3388].rearrange("p (c j) -> p c j", c=8)
    convb = PAR[:, 3388:3396]
    esink_b = PAR[:, 3396:3400]
    sinkl = PAR[:, 3400:3402]
    anegl = PAR[:, 3402:3403]
    dskl = PAR[:, 3403:3404]
    dtbl = PAR[:, 3404:3405]
    ARN = 29000
    ARh = sb("ARENA", [128, ARN])
    AR = Arena(ARh, ARN)

    PS = [es.enter_context(nc.psum_tensor("ps%d" % i, [128, 512], F32)) for i in range(8)]
    BP = [Buf() for _ in range(8)]

    def psf(i):
        return PS[i][:, :]

    def psb(i):
        return PS[i][:, :].bitcast(BF16)

    def mm(out, lhsT, rhs, st, sp_, r, w):
        P.op("pe", lambda e: e.matmul(out, lhsT, rhs, start=st, stop=sp_), r, w)

    def tr(out, in_, idn, r, w):
        P.op("pe", lambda e: e.transpose(out, in_, idn), r, w)

    def act(out, in_, fn, r, w, bias=0.0, scale=1.0, accum=None):
        if accum is None:
            P.op("act", lambda e: e.activation(out, in_, fn, bias=bias, scale=scale), r, w)
        else:
            P.op("act", lambda e: e.activation(out, in_, fn, bias=bias, scale=scale, accum_out=accum), r, w)

    def tt(eng, out, in0, in1, op, r, w):
        P.op(eng, lambda e: e.tensor_tensor(out, in0, in1, op), r, w)

    def ts(eng, out, in0, s1, s2, op0, op1, r, w):
        if op1 is None:
            P.op(eng, lambda e: e.tensor_scalar(out, in0, s1, None, op0), r, w)
        else:
            P.op(eng, lambda e: e.tensor_scalar(out, in0, s1, s2, op0, op1), r, w)

    def stt(eng, out, in0, sc, in1, op0, op1, r, w):
        P.op(eng, lambda e: e.scalar_tensor_tensor(out, in0, sc, in1, op0, op1), r, w)

    def red(eng, out, in_, op, r, w):
        P.op(eng, lambda e: e.tensor_reduce(out, in_, AX.X, op), r, w)

    def cp(eng, out, in_, r, w):
        if eng == "act":
            P.op("act", lambda e: e.activation(out, in_, AF.Copy), r, w)
        else:
            P.op(eng, lambda e: e.tensor_copy(out, in_), r, w)

    def rcp(out, in_, r, w):
        P.op("dve", lambda e: e.reciprocal(out, in_), r, w)

    def bc(ap, shape):
        return ap.to_broadcast(list(shape))

    P.dma("sp", CF, cst_f[:, :], w=[B_cst])
    P.dma("sp", CB, cst_b[:, :], w=[B_cst])
    class _Fresh(list):
        def __iter__(self):
            return iter([Buf()])

    B_out = Buf()
    OUTW = _Fresh()
    B_cpy = {}
    B_new = {}
    def issue_copies(l, q):
        for (nm, src, dst, n) in (("ak", cak, sak, 128), ("av", cav, sav, 128), ("ck", cck, sck, 2048), ("cv", ccv, scv, 2048)):
            B_cpy[(l, nm)] = []
            B_new[(l, nm)] = Buf()
            step = SB if n == 128 else 4
            for b0 in range(0, SB, step):
                bb = Buf()
                B_cpy[(l, nm)].append(bb)
                P.dma(q, dst[l, b0:b0 + step, 0:n - 4, :].rearrange("b (r f) c -> b r (f c)", f=4),
                      src[l, b0:b0 + step, 4:n, :].rearrange("b (r f) c -> b r (f c)", f=4), w=[bb])

    issue_copies(0, "act")
    issue_copies(1, "act")

    def load_params(l):
        P.dma("sp", PAR, par_in[l], w=[B_par])
        act(aneg_b, aneg_b, AF.Exp, [B_par], [B_par])
        ts("dve", aneg_b, aneg_b, -1.0, None, ALU.mult, None, [B_par], [B_par])
        act(anegl, anegl, AF.Exp, [B_par], [B_par])
        ts("dve", anegl, anegl, -1.0, None, ALU.mult, None, [B_par], [B_par])
        act(esink_b, sinks_b, AF.Exp, [B_par], [B_par])

    V, A_, G = "dve", "act", "pool"
    MUL, ADD, SUB, POW, MAXOP = ALU.mult, ALU.add, ALU.subtract, ALU.pow, ALU.max
    bn1_kc = bn1[2048:3072, :].rearrange("(p j) c -> p (j c)", j=8)
    bn1_ka = bn1[3072:3136, :].rearrange("r (a c) -> (r a) c", a=2)
    bn1_tail = bn1[3136:3148, :].rearrange("r c -> (r c)").rearrange("(p k) -> p k", p=128)
    qt_v = qt_scr.rearrange("p (j t) -> p j t", j=4)

    def normrope(X, nh, gain, t, tmp, sm, bufs):
        sq = tmp[:, 0:nh * 64].rearrange("p (h d) -> p h d", h=nh)
        tt(V, sq, X, X, MUL, bufs, bufs)
        ss = sm[:, 0:nh]
        red(V, ss, sq, ADD, bufs, bufs)
        act(ss, ss, AF.Sqrt, bufs, bufs, bias=EPS, scale=1.0 / 64)
        rcp(ss, ss, bufs, bufs)
        tt(V, X, X, bc(ss.unsqueeze(2), [128, nh, 64]), MUL, bufs, bufs)
        tt(V, X, X, gain, MUL, bufs + [B_par], bufs)
        tA = tmp[:, 0:nh * 16].rearrange("p (h d) -> p h d", h=nh)
        tB = tmp[:, nh * 16:nh * 32].rearrange("p (h d) -> p h d", h=nh)
        rb = bufs + [B_cst]
        tt(V, tA, X[:, :, 0:16], bc(cs1T[:, t, :].unsqueeze(1), [128, nh, 16]), MUL, rb, bufs)
        tt(V, tB, X[:, :, 0:16], bc(cs2T[:, t, :].unsqueeze(1), [128, nh, 16]), MUL, rb, bufs)
        tt(V, X[:, :, 0:8], tA[:, :, 0:8], tA[:, :, 8:16], SUB, bufs, bufs)
        tt(V, X[:, :, 8:16], tB[:, :, 0:8], tB[:, :, 8:16], ADD, bufs, bufs)

    def rmsnorm_T(xin, bx, gb, ubf, uT, sm, junk, bufs):
        ss = sm[:, 16:17]
        P.op(V, lambda e: e.memset(ss, 0.0), [], bufs)
        act(junk, xin, AF.Square, [bx] + bufs, bufs, accum=ss)
        act(ss, ss, AF.Sqrt, bufs, bufs, bias=EPS, scale=1.0 / D)
        rcp(ss, ss, bufs, bufs)
        stt(V, ubf, xin, ss, gb, MUL, MUL, [bx, B_par] + bufs, bufs)
        for c in range(8):
            tr(psb(6)[:, c * 128:(c + 1) * 128], ubf[:, c * 128:(c + 1) * 128], identB, bufs + [B_cst], [BP[6]])
        cp(A_, uT, psb(6)[:, 0:1024], [BP[6]], bufs)

    state = {}

    def layer_P1(l):
        load_params(l)
        AR.reset()
        xbS = AR.bf16(8 * 64).rearrange("p (c t) -> p c t", c=8)
        dtall = AR.f32(17 * 8).rearrange("p (t h) -> p t h", t=17)
        mo_all = AR.bf16(17 * 512).rearrange("p (t c) -> p t c", t=17)
        markA = AR.off
        xbcT = AR.bf16(8 * 2052).rearrange("p (c t) -> p c t", c=8)
        B_xbc, B_dt, B_mo = Buf(), Buf(), Buf()
        state.update(xbcT=xbcT, xbS=xbS, dtall=dtall, B_xbc=B_xbc, B_dt=B_dt, mark=AR.off, markA=markA,
                     mo_all=mo_all, B_mo=B_mo)
        Wb = AR.bf16(8 * NIN).rearrange("p (c n) -> p c n", c=8)
        B_W = Buf()
        for c in range(8):
            P.dma("pool", Wb[:, c, :], w_in[l, c * 128:(c + 1) * 128, :], w=[B_W])
        usets = []
        for k in range(2):
            usets.append(dict(ubf=AR.bf16(1024), uT=AR.bf16(1024), sm=AR.f32(32), B=Buf()))
        qkf = AR.f32(768)
        vf = AR.f32(256)
        tmpq = AR.f32(768)
        smq = AR.f32(32)
        smd = AR.f32(8)
        qkT = AR.bf16(768)
        zst = AR.bf16(512)
        vbf = AR.bf16(256)
        Bq, Bq2, Bkv, Bk2, Bz, Bd, Bvb = Buf(), Buf(), Buf(), Buf(), Buf(), Buf(), Buf()
        smd_t = AR.f32(24)
        Btl = Buf()
        qkbs = [AR.bf16(768), AR.bf16(768)]
        Bqn = [Buf(), Buf()]

        def tile_io(t):
            smp = (t == 16)
            us = usets[t % 2]
            if not smp:
                return smp, us, H[:, t, :], BH[t]
            return smp, us, HS, BH[16]

        def stA(t):
            smp, us, xin, bx = tile_io(t)
            if l == 0:
                if not smp:
                    P.dma("sp", xin, xp[t * 128:(t + 1) * 128, :], w=[bx])
                else:
                    P.dma("sp", HS[0:64, :], xs[:, :], w=[bx])
                    P.dma("sp", HS[64:128, :], xs[:, :], w=[bx])
            rmsnorm_T(xin, bx, g1b, us["ubf"], us["uT"], us["sm"], us["ubf"], [us["B"]])

        def stB(t):
            smp, us, xin, bx = tile_io(t)
            Bu, uT = us["B"], us["uT"]
            for (bk, c0, n) in ((0, 0, 512), (1, 512, 512), (2, 1024, 512), (3, 1536, 8)):
                for c in range(8):
                    mm(psf(bk)[:, 0:n], uT[:, c * 128:(c + 1) * 128], Wb[:, c, c0:c0 + n], c == 0, c == 7,
                       [Bu, B_W], [BP[bk]])
            for cc in range(8):
                for c in range(8):
                    mm(psf(4 + cc // 4)[:, (cc % 4) * 128:(cc % 4 + 1) * 128],
                       Wb[:, c, C_X + cc * 128:C_X + (cc + 1) * 128], uT[:, c * 128:(c + 1) * 128],
                       c == 0, c == 7, [Bu, B_W], [BP[4 + cc // 4]])

        def stC1(t):
            smp, us, xin, bx = tile_io(t)
            Bu, uT = us["B"], us["uT"]
            k2 = t % 2
            cp(A_, qkf[:, 0:512], psf(0), [BP[0]], [Bq])
            cp(A_, qkf[:, 512:768], psf(1)[:, 0:256], [BP[1]], [Bq])
            cp(A_, vf, psf(1)[:, 256:512], [BP[1]], [Bkv])
            normrope(qkf.rearrange("p (h d) -> p h d", h=12), 12, qkg, t, tmpq, smq, [Bq])
            cp(V, qkbs[k2], qkf, [Bq], [Bqn[k2]])
            if smp:
                P.dma("sp", sq_scr[:, :], qkf[:, 0:512], r=[Bq], w=[B_sq])
            if not smp:
                P.dma("sp", pck[l, t * 128:(t + 1) * 128, :], qkf[:, 640:768], r=[Bq], w=OUTW)
                P.dma("sp", pcv[l, t * 128:(t + 1) * 128, :], vf[:, 128:256], r=[Bkv], w=OUTW)
                if t == 15:
                    P.dma("sp", pak[l], qkf[:, 512:640], r=[Bq], w=OUTW)
                    P.dma("sp", pav[l], vf[:, 0:128], r=[Bkv], w=OUTW)
                cp(G, vbf, vf, [Bkv], [Bvb])
                P.dma("sp", bn1[t * 128:(t + 1) * 128, :], vbf, r=[Bvb], w=[B_bn1])
            else:
                for tk in range(4):
                    rows = slice(tk * 16, (tk + 1) * 16)
                    P.dma("sp", sak[l, :, 124 + tk, :], qkf[rows, 512:640], r=[Bq], w=[B_new[(l, "ak")]])
                    P.dma("sp", sck[l, :, 2044 + tk, :], qkf[rows, 640:768], r=[Bq], w=[B_new[(l, "ck")]])
                    P.dma("sp", sav[l, :, 124 + tk, :], vf[rows, 0:128], r=[Bkv], w=[B_new[(l, "av")]])
                    P.dma("sp", scv[l, :, 2044 + tk, :], vf[rows, 128:256], r=[Bkv], w=[B_new[(l, "cv")]])
            act(zst, psf(2), AF.Silu, [BP[2]], [Bz])
            P.dma("sp", zs_scr[t], zst, r=[Bz], w=[B_zs])
            tt(V, smd, psf(3)[:, 0:8], dtb_b, ADD, [BP[3], B_par], [Bd])
            act(smd, smd, AF.Exp, [Bd], [Bd])
            act(dtall[:, t, :], smd, AF.Ln, [Bd], [B_dt], bias=1.0)
            if smp:
                P.dma("sp", sdt_scr[:, :], dtall[0:64, t, :], r=[B_dt], w=[B_sdt])
            for bk in range(2):
                src = psf(4 + bk).rearrange("p (c t) -> p c t", c=4)
                if not smp:
                    cp(A_, xbcT[:, 4 * bk:4 * bk + 4, 3 + t * 128:3 + (t + 1) * 128], src, [BP[4 + bk]], [B_xbc])
                else:
                    cp(A_, xbS[:, 4 * bk:4 * bk + 4, :], src[:, :, 0:64], [BP[4 + bk]], [B_xbc])
            if t == 15:
                P.dma("sp", bn1_tail, xbcT[:, :, 3 + 2045:3 + 2048], r=[B_xbc], w=[B_bn1])
                tl = smd_t.rearrange("p (j c) -> p j c", j=3)
                cp(V, tl, xbcT[:, :, 3 + 2045:3 + 2048].rearrange("p c j -> p j c"), [B_xbc, Btl], [Btl])
                for j in range(3):
                    P.dma("sp", pconv[l, j].rearrange("(c p) -> p c", p=128), tl[:, j, :], r=[Btl], w=OUTW,
                          allow_slow_non_contiguous=True)
            if smp:
                xtm = (tmpq[:, 0:512], qkf[:, 0:512])
                for bk in range(2):
                    for c in range(8):
                        mm(psf(4 + bk), uT[:, c * 128:(c + 1) * 128],
                           Wb[:, c, C_X + bk * 512:C_X + (bk + 1) * 512], c == 0, c == 7, [Bu, B_W], [BP[4 + bk]])
                    cp(A_, xtm[bk], psf(4 + bk), [BP[4 + bk]], [Bq])
                sco = sconv_o[l].rearrange("(b j) c -> b j c", j=3)
                for tk in range(1, 4):
                    for bk in range(2):
                        P.dma("sp", sco[:, tk - 1, bk * 512:(bk + 1) * 512], xtm[bk][tk * 16:(tk + 1) * 16, :],
                              r=[Bq], w=OUTW)

        def stC2(t):
            smp = (t == 16)
            k2 = t % 2
            for j in range(6):
                tr(psb(7)[:, j * 128:(j + 1) * 128], qkbs[k2][:, j * 128:(j + 1) * 128], identB, [Bqn[k2], B_cst], [BP[7]])
            cp(A_, qkT, psb(7)[:, 0:768], [BP[7]], [Bq2])
            P.dma("sp", qt_v[:, :, t * 128:(t + 1) * 128], qkT[:, 0:512].rearrange("p (j t) -> p j t", j=4), r=[Bq2], w=[B_qt])
            if not smp:
                P.dma("sp", bn1_kc[:, t * 128:(t + 1) * 128], qkT[:, 640:768], r=[Bq2], w=[B_bn1])
                P.dma("sp", kta_scr[:, t * 128:(t + 1) * 128], qkT[:, 512:640], r=[Bq2], w=[B_kta])
                if t == 15:
                    P.dma("sp", bn1_ka, qkT[:, 512:640], r=[Bq2], w=[B_bn1])

        stA(0)
        for t in range(17):
            stB(t)
            if t + 1 < 17:
                stA(t + 1)
            stC1(t)
            if t >= 1:
                stC2(t - 1)
        stC2(16)
        P.cc(lambda e: e.collective_compute("AllGather", ALU.bypass, replica_groups=RG,
                                            ins=[bn1[:, :]], outs=[g1[:, :]]), r=[B_bn1], w=[B_g1])


    def dma_fn(q, fn, r=(), w=()):
        return P._emit(q, fn, r, w, "dma")

    pcache = {}

    def prev_base(e):
        k = id(e)
        if k not in pcache:
            pcache[k] = ((e.partition_id() + (NCORES - 1)) % NCORES) * (NR1 * 256)
        return pcache[k]

    def prev_rows(e, off, n):
        return bass.AP(g1.tensor, prev_base(e) + off * 256, [[256, n], [1, 256]])

    def layer_P23(l):
        xbcT, dtall, B_xbc, B_dt = state["xbcT"], state["dtall"], state["B_xbc"], state["B_dt"]
        AR.off = state["mark"]
        xact = AR.bf16(8 * 2048).rearrange("p (c t) -> p c t", c=8)
        hT = AR.f32(512)
        hTb = AR.bf16(512)
        Ltot = AR.f32(8)
        mo_all, B_mo = state["mo_all"], state["B_mo"]
        B_xact, B_h = Buf(), Buf()
        mk = AR.off
        dma_fn("sp", lambda e: e.dma_start(
            out=xbcT[:, :, 0:3],
            in_=prev_rows(e, 3136, 12).rearrange("r c -> (r c)").rearrange("(p k) -> p k", p=128)),
            r=[B_g1], w=[B_xbc])
        ts(V, xbcT[:, :, 0:3], xbcT[:, :, 0:3], pv, None, MUL, None, [B_xbc, B_cst], [B_xbc])
        accs = [AR.f32(2048), AR.f32(2048)]
        Bacc = [Buf(), Buf()]
        for c in range(8):
            eng = V
            acc, ba = accs[c % 2], Bacc[c % 2]
            ts(eng, acc, xbcT[:, c, 0:2048], convw[:, c, 0:1], None, MUL, None, [B_xbc, B_par], [ba])
            for j in range(1, 4):
                stt(eng, acc, xbcT[:, c, j:j + 2048], convw[:, c, j:j + 1], acc, MUL, ADD, [B_xbc, B_par, ba], [ba])
            act(xact[:, c, :], acc, AF.Silu, [ba, B_par], [B_xact], bias=convb[:, c:c + 1])
        AR.off = mk
        xbms = [AR.bf16(768), AR.bf16(768)]
        Bxb = [Buf(), Buf()]
        xw = AR.bf16(512)
        sm = AR.f32(64)
        Bt = Buf()
        Ba, Bxw = Buf(), Buf()
        a8, dte8, dec8, w8 = sm[:, 0:8], sm[:, 8:16], sm[:, 16:24], sm[:, 24:32]
        P.op(V, lambda e: e.memset(hT, 0.0), [], [B_h])
        P.op(V, lambda e: e.memset(Ltot, 0.0), [], [B_h])

        def xs_bm_tm(t):
            xbm, bb = xbms[t % 2], Bxb[t % 2]
            for j in range(6):
                tr(psb(6)[:, j * 128:(j + 1) * 128], xact[:, j, t * 128:(t + 1) * 128], identB, [B_xact, B_cst], [BP[6]])
            cp(A_, xbm, psb(6)[:, 0:768], [BP[6]], [bb])
            return xbm, bb

        for t in range(NT):
            xbm, bb = xs_bm_tm(t)
            tt(V, a8, dtall[:, t, :], aneg_b, MUL, [B_dt, B_par], [Ba])
            mm(psf(3)[:, 0:8], triS, a8, True, True, [Ba, B_cst], [BP[3]])
            mm(psf(3)[:, 8:16], onesF, a8, True, True, [Ba, B_cst], [BP[3]])
            act(dte8, psf(3)[:, 0:8], AF.Exp, [BP[3]], [Ba])
            act(dec8, psf(3)[:, 8:16], AF.Exp, [BP[3]], [Ba])
            act(sm[:, 48:56], psf(3)[:, 8:16], AF.Copy, [BP[3]], [Ba])
            tt(V, Ltot, Ltot, sm[:, 48:56], ADD, [Ba, B_h], [B_h])
            tt(V, w8, dtall[:, t, :], dte8, MUL, [B_dt, Ba], [Ba])
            tt(V, xw.rearrange("p (h d) -> p h d", h=8), xbm[:, 0:512].rearrange("p (h d) -> p h d", h=8),
               bc(w8.unsqueeze(2), [128, 8, 64]), MUL, [Ba, bb], [Bxw])
            for g in range(2):
                mm(psf(2)[:, g * 256:(g + 1) * 256], xbm[:, 512 + g * 128:512 + (g + 1) * 128],
                   xw[:, g * 256:(g + 1) * 256], True, True, [bb, Bxw], [BP[2]])
            tt(V, hT.rearrange("p (h d) -> p h d", h=8), hT.rearrange("p (h d) -> p h d", h=8),
               bc(dec8.unsqueeze(2), [128, 8, 64]), MUL, [Ba, B_h], [B_h])
            tt(V, hT, hT, psf(2), ADD, [BP[2], B_h], [B_h])
        P.dma("sp", bn2[0:128, :], hT, r=[B_h], w=[B_bn2])
        P.dma("sp", bn2[128:129, 0:8], Ltot[0:1, :], r=[B_h], w=[B_bn2])
        P.cc(lambda e: e.collective_compute("AllGather", ALU.bypass, replica_groups=RG,
                                            ins=[bn2[:, :]], outs=[g2[:, :]]), r=[B_bn2], w=[B_g2])
        Sall = AR.f32(8 * 512).rearrange("p (r c) -> p r c", r=8)
        B_S = Buf()
        P.dma("sp", Sall, g2.rearrange("(r k) c -> k r c", k=129)[0:128], r=[B_g2], w=[B_S])
        AX2 = Arena(ARh, state["mark"])
        AX2.off = state["markA"]
        Lall = AX2.f32(64).rearrange("p (r h) -> p r h", r=8)
        Dm = AX2.f32(64).rearrange("p (r h) -> p r h", r=8)
        P.dma("sp", Lall, bass.AP(g2.tensor, 128 * 512, [[0, 128], [129 * 512, 8], [1, 8]]), r=[B_g2, B_xbc], w=[B_S])
        tt(V, Lall, Lall, bc(rmask.unsqueeze(2), [128, 8, 8]), MUL, [B_S, B_cst], [B_S])
        act(Dm, Lall, AF.Exp, [B_S], [B_S])
        P.op(V, lambda e: e.memset(hT, 0.0), [B_bn2], [B_h])
        h3 = hT.rearrange("p (h d) -> p h d", h=8)
        for j in range(NCORES):
            tt(V, h3, h3, bc(Dm[:, j, :].unsqueeze(2), [128, 8, 64]), MUL, [B_S, B_h], [B_h])
            stt(V, hT, Sall[:, j, :], rmask[:, j:j + 1], hT, MUL, ADD, [B_S, B_cst, B_h], [B_h])
        cp(A_, hTb, hT, [B_h], [B_h])
        Brhs = AX2.f32(1024)
        eseg = AX2.bf16(1024)
        MT = AX2.bf16(1024)
        cbm = AX2.bf16(256)
        xdt = AX2.bf16(512)
        ytmp = AX2.f32(512)
        sk = AX2.f32(512)
        zsts = [AX2.bf16(512), AX2.bf16(512)]
        junk = AX2.bf16(512)
        sm3 = AX2.f32(16)
        eac8 = sm[:, 32:40]
        ss = sm3[:, 0:1]
        Bz = [Buf(), Buf()]
        BBr, Bes, Bcb, BMT, Bxd, By = Buf(), Buf(), Buf(), Buf(), Buf(), Buf()

        def gate_norm(t, yt, xs_tm, bxs, bufs):
            zst, bz = zsts[t % 2], Bz[t % 2]
            P.dma("sp", zst, zs_scr[t], r=[B_zs], w=[bz])
            tt(V, sk.rearrange("p (h d) -> p h d", h=8), xs_tm.rearrange("p (h d) -> p h d", h=8),
               bc(dsk_b.unsqueeze(2), [128, 8, 64]), MUL, bufs + bxs + [B_par], bufs)
            tt(V, yt, yt, sk, ADD, bufs, bufs)
            tt(V, yt, yt, zst, MUL, bufs + [bz], bufs)
            P.op(V, lambda e: e.memset(ss, 0.0), [], bufs)
            act(junk, yt, AF.Square, bufs, bufs, accum=ss)
            act(ss, ss, AF.Sqrt, bufs, bufs, bias=EPS, scale=1.0 / 512)
            rcp(ss, ss, bufs, bufs)
            stt(V, mo_all[:, t, :], yt, ss, ssmn, MUL, MUL, bufs + [B_par], [B_mo])

        Bt = By
        MT2 = [MT, AX2.bf16(1024)]
        xdt2 = [xdt, AX2.bf16(512)]
        xw2 = [xw, AX2.bf16(512)]
        smX = [AX2.f32(40), AX2.f32(40)]
        BMT2, Bxd2, Bxw2, Ba2 = [BMT, Buf()], [Bxd, Buf()], [Bxw, Buf()], [Buf(), Buf()]

        def stX(t):
            k2 = t % 2
            cols = slice(t * 128, (t + 1) * 128)
            sx = smX[k2]
            a8_, dte_, dec_, eac_ = sx[:, 0:8], sx[:, 8:16], sx[:, 16:24], sx[:, 24:32]
            ba = Ba2[k2]
            xbm, bb = xs_bm_tm(t)
            tt(V, a8_, dtall[:, t, :], aneg_b, MUL, [B_dt, B_par], [ba])
            mm(psf(3)[:, 0:8], triS, a8_, True, True, [ba, B_cst], [BP[3]])
            mm(psf(3)[:, 8:16], onesF, a8_, True, True, [ba, B_cst], [BP[3]])
            mm(psf(3)[:, 16:24], triU, a8_, True, True, [ba, B_cst], [BP[3]])
            tt(G, Brhs.rearrange("p (h l) -> p h l", h=8), bc(triU.unsqueeze(1), [128, 8, 128]),
               bc(a8_.unsqueeze(2), [128, 8, 128]), MUL, [ba, B_cst], [BBr])
            act(dte_, psf(3)[:, 0:8], AF.Exp, [BP[3]], [ba])
            act(dec_, psf(3)[:, 8:16], AF.Exp, [BP[3]], [ba])
            act(eac_, psf(3)[:, 16:24], AF.Exp, [BP[3]], [ba])
            x3 = xbm[:, 0:512].rearrange("p (h d) -> p h d", h=8)
            tt(V, xdt2[k2].rearrange("p (h d) -> p h d", h=8), x3, bc(dtall[:, t, :].unsqueeze(2), [128, 8, 64]), MUL,
               [bb, B_dt], [Bxd2[k2]])
            tt(V, xw2[k2].rearrange("p (h d) -> p h d", h=8), xdt2[k2].rearrange("p (h d) -> p h d", h=8),
               bc(dte_.unsqueeze(2), [128, 8, 64]), MUL, [Bxd2[k2], ba], [Bxw2[k2]])
            mm(psf(4), triS, Brhs[:, 0:512], True, True, [BBr, B_cst], [BP[4]])
            mm(psf(5), triS, Brhs[:, 512:1024], True, True, [BBr, B_cst], [BP[5]])
            act(eseg[:, 0:512], psf(4), AF.Exp, [BP[4]], [Bes])
            act(eseg[:, 512:1024], psf(5), AF.Exp, [BP[5]], [Bes])
            for g in range(2):
                mm(psf(1)[:, g * 128:(g + 1) * 128], xact[:, 4 + g, cols], xact[:, 6 + g, cols], True, True,
                   [B_xact], [BP[1]])
            tt(V, cbm.rearrange("p (g l) -> p g l", g=2), psf(1)[:, 0:256].rearrange("p (g l) -> p g l", g=2),
               bc(causB.unsqueeze(1), [128, 2, 128]), MUL, [BP[1], B_cst], [Bcb])
            tt(V, MT2[k2].rearrange("p (g h l) -> p g h l", g=2, h=4), eseg.rearrange("p (g h l) -> p g h l", g=2, h=4),
               bc(cbm.rearrange("p (g l) -> p g l", g=2).unsqueeze(2), [128, 2, 4, 128]), MUL, [Bes, Bcb], [BMT2[k2]])

        def stY(t):
            k2 = t % 2
            cols = slice(t * 128, (t + 1) * 128)
            sx = smX[k2]
            dec_, eac_ = sx[:, 16:24], sx[:, 24:32]
            ba = Ba2[k2]
            xbm, bb = xbms[k2], Bxb[k2]
            for h in range(8):
                mm(psf(0)[:, h * 64:(h + 1) * 64], MT2[k2][:, h * 128:(h + 1) * 128], xdt2[k2][:, h * 64:(h + 1) * 64],
                   True, True, [BMT2[k2], Bxd2[k2]], [BP[0]])
            for g in range(2):
                mm(psf(2)[:, g * 256:(g + 1) * 256], xact[:, 6 + g, cols], hTb[:, g * 256:(g + 1) * 256], True, True,
                   [B_xact, B_h], [BP[2]])
            for g in range(2):
                mm(psf(7)[:, g * 256:(g + 1) * 256], xbm[:, 512 + g * 128:512 + (g + 1) * 128],
                   xw2[k2][:, g * 256:(g + 1) * 256], True, True, [bb, Bxw2[k2]], [BP[7]])
            tt(V, h3, h3, bc(dec_.unsqueeze(2), [128, 8, 64]), MUL, [ba, B_h, BP[2]], [B_h])
            tt(V, hT, hT, psf(7), ADD, [BP[7], B_h], [B_h])
            cp(A_, hTb, hT, [B_h], [B_h])
            y3 = ytmp.rearrange("p (h d) -> p h d", h=8)
            tt(V, y3, psf(2).rearrange("p (h d) -> p h d", h=8), bc(eac_.unsqueeze(2), [128, 8, 64]), MUL,
               [BP[2], ba], [By])
            tt(V, ytmp, ytmp, psf(0), ADD, [BP[0], By], [By])
            gate_norm(t, ytmp, xbm[:, 0:512], [bb], [By])

        stX(0)
        for t in range(NT):
            if t + 1 < NT:
                stX(t + 1)
            stY(t)
        xsS = Brhs[:, 0:512]
        for hh in range(2):
            P.dma("sp", ytmp[hh * 64:(hh + 1) * 64, :], sy_scr[:, :], r=[B_sy, By], w=[By])
            P.dma("sp", xsS[hh * 64:(hh + 1) * 64, :], sx_scr[:, 0:512], r=[B_sx, BBr], w=[BBr])
        gate_norm(16, ytmp, xsS, [BBr], [By])
        for j in range(4):
            tr(psf(7)[:, j * 128:(j + 1) * 128], hT[:, j * 128:(j + 1) * 128], identF, [B_h, B_cst], [BP[7]])
        cp(V, ytmp, psf(7), [BP[7]], [Bt])
        P.dma("sp", pssm[l].rearrange("(j p) n -> p j n", p=128), ytmp.rearrange("p (j n) -> p j n", j=4),
              r=[Bt], w=OUTW)


    def layer_P4(l):
        mo_all = state["mo_all"]
        AR.off = state["markA"]
        oT = AR.bf16(8 * 2176).rearrange("p (h t) -> p h t", h=8)
        mark3 = AR.off
        qT = AR.bf16(4 * 2176).rearrange("p (j t) -> p j t", j=4)
        kTC = AR.bf16(4096)
        kTA = AR.bf16(2176)
        acc = AR.f32(2 * 2048).rearrange("p (r t) -> p r t", r=2)
        NV = 4
        vr = [AR.bf16(256).rearrange("p (g c) -> p g c", g=2) for _ in range(NV)]
        Bv = [Buf() for _ in range(NV)]
        Et = [AR.bf16(512), AR.bf16(512)]
        BE = [Buf(), Buf()]
        rd = AR.f32(256)
        Brd = Buf()
        B_q, B_k, B_acc, B_o = Buf(), Buf(), Buf(), Buf()
        state.update(oT=oT, B_o=B_o, mark3=mark3)
        P.dma("sp", qT[:, :, 0:2048], qt_v[:, :, 0:2048], r=[B_qt], w=[B_q])
        P.dma("sp", kTC[:, 2048:4096], bn1_kc, r=[B_bn1], w=[B_k])
        dma_fn("sp", lambda e: e.dma_start(out=kTC[:, 0:2048],
                                           in_=prev_rows(e, 2048, 1024).rearrange("(p j) c -> p (j c)", j=8)),
               r=[B_g1], w=[B_k])
        dma_fn("sp", lambda e: e.dma_start(out=kTA[:, 0:128],
                                           in_=prev_rows(e, 3072, 64).rearrange("r (a c) -> (r a) c", a=2)),
               r=[B_g1], w=[B_k])
        P.dma("sp", kTA[:, 128:2176], kta_scr[:, :], r=[B_kta], w=[B_k])
        for i in range(NV):
            P.op(G, lambda e, i=i: e.memset(vr[i][:, :, 64:128], 1.0), [], [Bv[i]])
        dma_fn("sp", lambda e: e.dma_start(out=vprev[:, :], in_=prev_rows(e, 0, 2048)), r=[B_g1], w=[B_vprev])
        cnt = {"v": 0, "e": 0}

        def load_v(d, r, b, voff):
            i = cnt["v"] % NV
            cnt["v"] += 1
            B = 128 * d
            if b >= 0:
                src = bn1[b * B + r:b * B + r + 127 * d + 1:d, voff:voff + 128]
                P.dma("sp", vr[i][:, :, 0:64], src.rearrange("p (g c) -> p g c", g=2), r=[B_bn1], w=[Bv[i]])
            else:
                st = 2048 - B + r
                src = vprev[st:st + 127 * d + 1:d, voff:voff + 128]
                P.dma("sp", vr[i][:, :, 0:64], src.rearrange("p (g c) -> p g c", g=2), r=[B_vprev], w=[Bv[i]])
            return i

        def unit(kT, koff, qj, d, r, b, g, vi_prev, vi_cur, mask, is_A):
            B = 128 * d
            q0 = b * B + r
            qs = slice(q0, q0 + 127 * d + 1, d)
            kc = slice(koff + q0, koff + q0 + 127 * d + 1, d)
            kp = slice(koff + q0 - B, koff + q0 - B + 127 * d + 1, d)
            gs = slice(g * 64, (g + 1) * 64)
            ei = cnt["e"] % 2
            cnt["e"] += 1
            E = Et[ei]
            for kb, ks in enumerate((kp, kc)):
                mm(psf(kb)[:, 0:256].rearrange("p (r t) -> p r t", r=2), kT[gs, ks], qT[gs, qj:qj + 2, qs], True, True,
                   [B_k, B_q], [BP[kb]])
                act(E[:, kb * 256:(kb + 1) * 256], psf(kb)[:, 0:256], AF.Exp, [BP[kb]], [BE[ei]], scale=0.125)
            tt(V, E, E, mask, MUL, [BE[ei], B_cst], [BE[ei]])
            ob = 2 + (cnt["e"] % 2)
            mm(psf(ob)[:, 0:256], vr[vi_prev][:, g, :], E[:, 0:256], True, False, [BE[ei], Bv[vi_prev]], [BP[ob]])
            mm(psf(ob)[:, 0:256], vr[vi_cur][:, g, :], E[:, 256:512], False, True, [BE[ei], Bv[vi_cur]], [BP[ob]])
            O = psf(ob)[:, 0:256].rearrange("p (r t) -> p r t", r=2)
            if is_A:
                for rr in range(2):
                    h = 2 * g + rr
                    ts(V, rd[0:64, rr * 128:(rr + 1) * 128], O[64:128, rr, :], esink_b[64:128, h:h + 1], None, ADD, None,
                       [BP[ob], B_par], [Brd])
                rcp(rd[0:64, :], rd[0:64, :], [Brd], [Brd])
                tt(V, oT[0:64, 2 * g:2 * g + 2, qs], O[0:64, :, :], rd[0:64, :].rearrange("p (r t) -> p r t", r=2), MUL,
                   [BP[ob], Brd], [B_o])
            else:
                if d == 1:
                    cp(V, acc[:, :, qs], O, [BP[ob]], [B_acc])
                else:
                    tt(V, acc[:, :, qs], acc[:, :, qs], O, ADD, [BP[ob], B_acc], [B_acc])

        for g in range(2):
            vi_p = load_v(1, 0, -1, 0)
            for b in range(16):
                vcur = load_v(1, 0, b, 0)
                unit(kTA, 128, 0, 1, 0, b, g, vi_p, vcur, mA0 if b == 0 else mA, True)
                vi_p = vcur
        for g in range(2):
            for d in (1, 4, 16):
                nb = 16 // d
                for r in range(d):
                    vi_p = load_v(d, r, -1, 128)
                    for b in range(nb):
                        vcur = load_v(d, r, b, 128)
                        unit(kTC, 2048, 2, d, r, b, g, vi_p, vcur, mC0 if b == 0 else mC, False)
                        vi_p = vcur
            for rr in range(2):
                for c0 in range(0, 2048, 256):
                    cs = slice(c0, c0 + 256)
                    rcp(rd[0:64, :], acc[64:128, rr, cs], [B_acc], [Brd])
                    tt(V, oT[0:64, 4 + 2 * g + rr, cs], acc[0:64, rr, cs], rd[0:64, :], MUL, [B_acc, Brd], [B_o])


    def layer_S(l):
        xbS, dtall = state["xbS"], state["dtall"]
        AR.off = state["mark"]
        mk0 = AR.off
        ql = AR.f32(256).rearrange("p (j d) -> p j d", j=4)
        B_ql = Buf()
        for g in range(2):
            P.dma("sp", ql[g * 64:(g + 1) * 64, :, :],
                  sq_scr[g * 64:(g + 1) * 64, :].rearrange("p (j gg d) -> p j gg d", j=4, gg=2)[:, :, g, :],
                  r=[B_sq], w=[B_ql])
        Kf = AR.f32(129 * 64).rearrange("p (k d) -> p k d", d=64)
        Vb = AR.bf16(129 * 64).rearrange("p (k d) -> p k d", d=64)
        prod = AR.bf16(65 * 64).rearrange("p (k d) -> p k d", d=64)
        sc = AR.f32(2 * 132).rearrange("p (r k) -> p r k", r=2)
        eb = AR.bf16(2 * 132).rearrange("p (r k) -> p r k", r=2)
        sm = AR.f32(64)
        numA = AR.f32(128).rearrange("p (r d) -> p r d", r=2)
        numC = AR.f32(3 * 128).rearrange("p (q r d) -> p q r d", q=3, r=2)
        numh = AR.f32(128).rearrange("p (r d) -> p r d", r=2)
        stC = AR.f32(3 * 4).rearrange("p (q r) -> p q r", q=3)
        oC = AR.f32(128).rearrange("p (r d) -> p r d", r=2)
        B_K1, B_V, Bt = Buf(), Buf(), Buf()
        rdeps = {id(cak): [], id(cav): [], id(cck): [], id(ccv): [],
                 id(sak): B_cpy[(l, "ak")] + [B_new[(l, "ak")]], id(sav): B_cpy[(l, "av")] + [B_new[(l, "av")]],
                 id(sck): B_cpy[(l, "ck")] + [B_new[(l, "ck")]], id(scv): B_cpy[(l, "cv")] + [B_new[(l, "cv")]]}

        def load_k(pi, specs):
            for g in range(2):
                for t in range(4):
                    lanes = slice(g * 64 + t * 16, g * 64 + (t + 1) * 16)
                    for (j0, n, sk_, sv_, rf) in specs:
                        P.dma("sp", Kf[lanes, j0:j0 + n, :], sk_[l, :, rf(t), g * 64:(g + 1) * 64],
                              r=rdeps[id(sk_)], w=[B_K1])

        def load_v(pi, specs):
            for g in range(2):
                for t in range(4):
                    lanes = slice(g * 64 + t * 16, g * 64 + (t + 1) * 16)
                    for (j0, n, sk_, sv_, rf) in specs:
                        P.dma("pool", Vb[lanes, j0:j0 + n, :], sv_[l, :, rf(t), g * 64:(g + 1) * 64],
                              r=rdeps[id(sv_)], w=[B_V])

        def scores(pi, nk, qj, nm, den):
            Kb, B_K = Kf, B_K1
            halves = ((0, 65), (65, nk))
            for r in range(2):
                for (k0, k1) in halves:
                    tt(V, prod[:, 0:k1 - k0, :], Kb[:, k0:k1, :], bc(ql[:, qj + r, :].unsqueeze(1), [128, k1 - k0, 64]), MUL,
                       [B_K, B_ql, Bt], [Bt])
                    red(V, sc[:, r, k0:k1], prod[:, 0:k1 - k0, :], ADD, [Bt], [Bt])
                red(V, nm[:, r:r + 1], sc[:, r, 0:nk], MAXOP, [Bt], [Bt])
            ts(V, nm, nm, -0.125, None, MUL, None, [Bt], [Bt])
            P.op(V, lambda e: e.memset(den, 0.0), [Bt], [Bt])
            for r in range(2):
                act(eb[:, r, 0:nk], sc[:, r, 0:nk], AF.Exp, [Bt], [Bt], bias=nm[:, r:r + 1], scale=0.125,
                    accum=den[:, r:r + 1])

        def pv(nk, num):
            halves = ((0, 65), (65, nk))
            for r in range(2):
                for hi, (k0, k1) in enumerate(halves):
                    tt(V, prod[:, 0:k1 - k0, :], Vb[:, k0:k1, :], bc(eb[:, r, k0:k1].unsqueeze(2), [128, k1 - k0, 64]), MUL,
                       [B_V, Bt], [Bt])
                    red(V, (num if hi == 0 else numh)[:, r, :], prod[:, 0:k1 - k0, :].rearrange("p k d -> p d k"), ADD,
                        [Bt], [Bt])
                tt(V, num[:, r, :], num[:, r, :], numh[:, r, :], ADD, [Bt], [Bt])

        specA = [(0, 124, cak, cav, lambda t: slice(t + 1, t + 125)),
                 (124, 4, sak, sav, lambda t: slice(t + 121, t + 125))]
        pats = [
            [(0, 125, cck, ccv, lambda t: slice(t + 1920, t + 2045)), (125, 4, sck, scv, lambda t: slice(t + 2041, t + 2045))],
            [(0, 128, cck, ccv, lambda t: slice(1536 + t, 1536 + t + 4 * 127 + 1, 4)), (128, 1, sck, scv, lambda t: slice(2044 + t, 2045 + t))],
            [(0, 128, cck, ccv, lambda t: slice(t, t + 16 * 127 + 1, 16)), (128, 1, sck, scv, lambda t: slice(2044 + t, 2045 + t))],
        ]
        allp = [(specA, 128, 0)] + [(p_, 129, 2) for p_ in pats]
        nmA, denA, t2 = sm[:, 0:2], sm[:, 2:4], sm[:, 4:6]
        outs = [(numA, nmA, denA)] + [(numC[:, q_], stC[:, q_, 0:2], stC[:, q_, 2:4]) for q_ in range(3)]
        load_k(0, allp[0][0])
        load_v(0, allp[0][0])
        for pi, (specs, nk, qj) in enumerate(allp):
            num, nm, den = outs[pi]
            scores(pi, nk, qj, nm, den)
            if pi + 1 < 4:
                load_k(pi + 1, allp[pi + 1][0])
            pv(nk, num)
            if pi + 1 < 4:
                load_v(pi + 1, allp[pi + 1][0])
        for r in range(2):
            act(t2[:, r:r + 1], nmA[:, r:r + 1], AF.Exp, [Bt, B_par], [Bt], bias=sinkl[:, r:r + 1])
        tt(V, denA, denA, t2, ADD, [Bt], [Bt])
        rcp(denA, denA, [Bt], [Bt])
        tt(V, numA, numA, bc(denA.unsqueeze(2), [128, 2, 64]), MUL, [Bt], [Bt])
        for g in range(2):
            P.dma("sp", smix[:, g * 128:(g + 1) * 128], numA[g * 64:(g + 1) * 64, :, :], r=[Bt], w=[B_smix])
        mn, w3 = sm[:, 8:10], sm[:, 10:16].rearrange("p (q r) -> p q r", q=3)
        tt(V, mn, stC[:, 0, 0:2], stC[:, 1, 0:2], ALU.min, [Bt], [Bt])
        tt(V, mn, mn, stC[:, 2, 0:2], ALU.min, [Bt], [Bt])
        for q_ in range(3):
            for r in range(2):
                act(w3[:, q_, r:r + 1], stC[:, q_, r:r + 1], AF.Exp, [Bt], [Bt], bias=mn[:, r:r + 1], scale=-1.0)
        dn = sm[:, 16:18]
        tt(V, stC[:, :, 2:4], stC[:, :, 2:4], w3, MUL, [Bt], [Bt])
        tt(V, dn, stC[:, 0, 2:4], stC[:, 1, 2:4], ADD, [Bt], [Bt])
        tt(V, dn, dn, stC[:, 2, 2:4], ADD, [Bt], [Bt])
        rcp(dn, dn, [Bt], [Bt])
        for q_ in range(3):
            tt(V, numC[:, q_], numC[:, q_], bc(w3[:, q_, :].unsqueeze(2), [128, 2, 64]), MUL, [Bt], [Bt])
        tt(V, oC, numC[:, 0], numC[:, 1], ADD, [Bt], [Bt])
        tt(V, oC, oC, numC[:, 2], ADD, [Bt], [Bt])
        tt(V, oC, oC, bc(dn.unsqueeze(2), [128, 2, 64]), MUL, [Bt], [Bt])
        for g in range(2):
            P.dma("sp", smix[:, 256 + g * 128:256 + (g + 1) * 128], oC[g * 64:(g + 1) * 64, :, :], r=[Bt], w=[B_smix])

        P.barrier()
        AR.off = mk0
        pre = AR.f32(1024)
        xcS = AR.bf16(8 * 112).rearrange("p (c s b) -> p c s b", c=8, s=7)
        acc = AR.f32(512).rearrange("p (c k) -> p c k", c=8)
        xaS = AR.bf16(512).rearrange("p (c k) -> p c k", c=8)
        xaT = AR.f32(1024)
        Bc = Buf()
        P.dma("sp", pre[0:48, :], sconv[l], w=[Bc])
        for c in range(8):
            tr(psf(4 + c // 4)[:, (c % 4) * 48:(c % 4 + 1) * 48], pre[0:48, c * 128:(c + 1) * 128], identF[0:48, 0:48],
               [Bc, B_cst], [BP[4 + c // 4]])
        for bk in range(2):
            cp(V, xcS[:, 4 * bk:4 * bk + 4, 0:3, :],
               psf(4 + bk)[:, 0:192].rearrange("p (c b j) -> p c j b", c=4, j=3), [BP[4 + bk]], [Bc])
        cp(V, xcS[:, :, 3:7, :].rearrange("p c s b -> p c (s b)"), xbS, [state["B_xbc"]], [Bc])
        xf = xcS.rearrange("p c s b -> p c (s b)")
        for c in range(8):
            ts(V, acc[:, c, :], xf[:, c, 0:64], convw[:, c, 0:1], None, MUL, None, [Bc, B_par], [Bc])
            for j in range(1, 4):
                stt(V, acc[:, c, :], xf[:, c, 16 * j:16 * j + 64], convw[:, c, j:j + 1], acc[:, c, :], MUL, ADD,
                    [Bc, B_par], [Bc])
            act(xaS[:, c, :], acc[:, c, :], AF.Silu, [Bc, B_par], [Bc], bias=convb[:, c:c + 1])
        for c in range(8):
            tr(psb(6)[0:64, c * 128:(c + 1) * 128], xaS[:, c, :], identB, [Bc, B_cst], [BP[6]])
        cp(V, xaT[0:64, :], psb(6)[0:64, 0:1024], [BP[6]], [Bc])
        P.dma("sp", sx_scr[:, :], xaT[0:64, :], r=[Bc], w=[B_sx])
        P.barrier()
        AR.off = mk0
        xl = AR.f32(256).rearrange("p (t d) -> p t d", t=4)
        bl = AR.f32(512).rearrange("p (t n) -> p t n", t=4)
        cl = AR.f32(512).rearrange("p (t n) -> p t n", t=4)
        yl = AR.f32(256).rearrange("p (t d) -> p t d", t=4)
        dl = AR.f32(4)
        s2 = AR.f32(16)
        xd = AR.f32(32)
        hS = AR.f32(4096).rearrange("p (d n) -> p d n", d=32)
        tmp = AR.f32(4096).rearrange("p (d n) -> p d n", d=32)
        Bl, Bh, Bs = Buf(), Buf(), Buf()
        for t in range(4):
            P.dma("sp", xl[:, t, :], bass.AP(sx_scr.tensor, t * 16 * 1024, [[64, 8], [1024, 16], [1, 64]]), r=[B_sx], w=[Bl])
            for g in range(2):
                P.dma("sp", bl[g * 64:(g + 1) * 64, t, :],
                      bass.AP(sx_scr.tensor, t * 16 * 1024 + 512 + g * 128, [[0, 4], [1024, 16], [1, 128]]), r=[B_sx], w=[Bl])
                P.dma("sp", cl[g * 64:(g + 1) * 64, t, :],
                      bass.AP(sx_scr.tensor, t * 16 * 1024 + 768 + g * 128, [[0, 4], [1024, 16], [1, 128]]), r=[B_sx], w=[Bl])
        for t in range(4):
            P.dma("sp", dl[:, t:t + 1], bass.AP(sdt_scr.tensor, t * 128, [[1, 8], [8, 16], [1, 1]]), r=[B_sdt], w=[Bl],
                  allow_slow_non_contiguous=True)
        a4, dA4 = s2[:, 0:4], s2[:, 4:8]
        ts(V, a4, dl, anegl, None, MUL, None, [Bl, B_par], [Bs])
        act(dA4, a4, AF.Exp, [Bs], [Bs])
        for ph in range(2):
            st_in = bass.AP(sssm.tensor, l * SB * 8 * 8192 + ph * 4096, [[8192, 8], [65536, 16], [1, 4096]])
            st_out = bass.AP(sssm_o.tensor, l * SB * 8 * 8192 + ph * 4096, [[8192, 8], [65536, 16], [1, 4096]])
            P.dma("sp", hS.rearrange("p d n -> p (d n)"), st_in, w=[Bh])
            for t in range(4):
                ts(V, xd, xl[:, t, ph * 32:(ph + 1) * 32], dl[:, t:t + 1], None, MUL, None, [Bl], [Bs])
                tt(V, tmp, bc(xd.unsqueeze(2), [128, 32, 128]), bc(bl[:, t, :].unsqueeze(1), [128, 32, 128]), MUL,
                   [Bs, Bl], [Bs])
                stt(V, hS, hS, dA4[:, t:t + 1], tmp, MUL, ADD, [Bs, Bh], [Bh])
                tt(V, tmp, hS, bc(cl[:, t, :].unsqueeze(1), [128, 32, 128]), MUL, [Bh, Bl, Bs], [Bs])
                red(V, yl[:, t, ph * 32:(ph + 1) * 32], tmp, ADD, [Bs], [Bs])
            P.dma("sp", st_out, hS.rearrange("p d n -> p (d n)"), r=[Bh], w=OUTW)
        for t in range(4):
            P.dma("sp", bass.AP(sy_scr.tensor, t * 16 * 512, [[64, 8], [512, 16], [1, 64]]), yl[:, t, :], r=[Bs], w=[B_sy])

    def layer_P4s(l):
        oT, B_o = state["oT"], state["B_o"]
        AR.off = state["mark3"]
        sf = AR.f32(512)
        sbf = AR.bf16(512)
        Bt = Buf()
        for hh in range(2):
            P.dma("sp", sf[hh * 64:(hh + 1) * 64, :], smix[:, :], r=[B_smix], w=[Bt])
        cp(V, sbf, sf, [Bt], [Bt])
        for hi in range(8):
            tr(psb(7)[0:64, hi * 128:(hi + 1) * 128], sbf[:, hi * 64:(hi + 1) * 64], identB, [Bt, B_cst], [BP[7]])
        cp(V, oT[0:64, :, 2048:2176], psb(7)[0:64, 0:1024].rearrange("p (h t) -> p h t", h=8), [BP[7]], [B_o])

    def layer_P5(l, tiles):
        mo_all, B_mo, oT, B_o = state["mo_all"], state["B_mo"], state["oT"], state["B_o"]
        AR.off = state["mark3"]
        WoA = AR.bf16(8 * 1024).rearrange("p (h n) -> p h n", h=8)
        WoM = AR.bf16(4 * 1024).rearrange("p (c n) -> p c n", c=4)
        moT = AR.bf16(512)
        B_wo, Bt = Buf(), Buf()
        P.dma("pool", WoA[0:64, 0:4, :], w_out[l, 0:256, :].rearrange("(h p) n -> p h n", p=64), w=[B_wo])
        P.dma("pool", WoA[0:64, 4:8, :], w_out[l, 768:1024, :].rearrange("(h p) n -> p h n", p=64), w=[B_wo])
        for c in range(4):
            P.dma("pool", WoM[:, c, :], w_out[l, 256 + c * 128:256 + (c + 1) * 128, :], w=[B_wo])
        for t in tiles:
            for j in range(4):
                tr(psb(6)[:, j * 128:(j + 1) * 128], mo_all[:, t, j * 128:(j + 1) * 128], identB, [B_mo, B_cst], [BP[6]])
            cp(A_, moT, psb(6)[:, 0:512], [BP[6]], [Bt])
            hx, bx = (H[:, t, :], BH[t]) if t < 16 else (HS, BH[16])
            for half in range(2):
                hs_ = slice(half * 512, (half + 1) * 512)
                for hi in range(8):
                    mm(psf(half), oT[0:64, hi, t * 128:(t + 1) * 128], WoA[0:64, hi, hs_], hi == 0, False,
                       [B_o, B_wo], [BP[half]])
                for c in range(4):
                    mm(psf(half), moT[:, c * 128:(c + 1) * 128], WoM[:, c, hs_], False, c == 3, [Bt, B_wo], [BP[half]])
                tt(V, hx[:, hs_], hx[:, hs_], psf(half), ADD, [BP[half], bx], [bx])

    def layer_P6(l, tiles):
        AR.reset()

        uT2 = AR.bf16(17 * 1024).rearrange("p (t c k) -> p t c k", t=17, c=8)
        ubf = AR.bf16(1024)
        junk = AR.bf16(1024)
        sm = AR.f32(32)
        NRING = 3
        ring = [(AR.bf16(8 * 256).rearrange("p (c f) -> p c f", c=8), AR.bf16(8 * 256).rearrange("p (c f) -> p c f", c=8),
                 AR.bf16(2 * 1024).rearrange("p (c n) -> p c n", c=2), Buf()) for _ in range(NRING)]
        sg = [AR.f32(512), AR.f32(512)]
        aT = [AR.bf16(1024).rearrange("p (c k) -> p c k", c=2), AR.bf16(1024).rearrange("p (c k) -> p c k", c=2)]
        Bsg = [Buf(), Buf()]
        BaT = [Buf(), Buf()]
        B_u, Bt = Buf(), Buf()
        for t in tiles:
            hx, bx = (H[:, t, :], BH[t]) if t < 16 else (HS, BH[16])
            rmsnorm_T(hx, bx, g2b, ubf, uT2[:, t].rearrange("p c k -> p (c k)"), sm, junk, [Bt])
        P.op(V, lambda e: e.memset(sm[:, 0:1], 0.0), [Bt], [B_u])
        blocks = [[t for t in tiles if t // 4 == b] for b in range(5)]
        blocks = [b for b in blocks if b]
        ng = DFF // 256
        k = 0
        for gi in range(ng):
            Wg, Wu, Wd, Bw = ring[gi % NRING]
            f0 = gi * 256
            P.dma("pool", Wg, w_gate[l, :, f0:f0 + 256].rearrange("(c p) f -> p c f", p=128), w=[Bw])
            P.dma("pool", Wu, w_up[l, :, f0:f0 + 256].rearrange("(c p) f -> p c f", p=128), w=[Bw])
            P.dma("pool", Wd, w_down[l, f0:f0 + 256, :].rearrange("(c p) n -> p c n", p=128), w=[Bw])
            for blk in blocks:
                nt = len(blk)
                ntok = 128 * nt
                t0 = blk[0]
                i2 = k % 2
                k += 1
                for fc in range(2):
                    for (W_, bk) in ((Wg, fc), (Wu, 2 + fc)):
                        for c in range(8):
                            mm(psf(bk)[:, 0:ntok].rearrange("p (t k) -> p t k", t=nt), W_[:, c, fc * 128:(fc + 1) * 128],
                               uT2[:, t0:t0 + nt, c, :], c == 0, c == 7, [Bw, B_u], [BP[bk]])
                    act(sg[i2][:, 0:ntok], psf(fc)[:, 0:ntok], AF.Silu, [BP[fc]], [Bsg[i2]])
                    tt(V, aT[i2][:, fc, 0:ntok], sg[i2][:, 0:ntok], psf(2 + fc)[:, 0:ntok], MUL, [Bsg[i2], BP[2 + fc]], [BaT[i2]])
                for ti, t in enumerate(blk):
                    hx, bx = (H[:, t, :], BH[t]) if t < 16 else (HS, BH[16])
                    for half in range(2):
                        bk = 4 + 2 * (ti % 2) + half
                        hs_ = slice(half * 512, (half + 1) * 512)
                        for fc in range(2):
                            mm(psf(bk), aT[i2][:, fc, ti * 128:(ti + 1) * 128], Wd[:, fc, hs_], fc == 0, fc == 1,
                               [BaT[i2], Bw], [BP[bk]])
                        tt(V, hx[:, hs_], hx[:, hs_], psf(bk), ADD, [BP[bk], bx], [bx])

    def write_y():
        for t in range(NT):
            P.dma("sp", yp[t * 128:(t + 1) * 128, :], H[:, t, :], r=[BH[t]], w=OUTW)
        P.dma("sp", ys[:, :], HS[0:64, :], r=[BH[16]], w=OUTW)

    PT = list(range(NT))
    for l in range(DEPTH if STAGE >= 2 else 1):
        layer_P1(l)
        P.barrier()
        layer_S(l)
        P.barrier()
        layer_P23(l)
        P.barrier()
        layer_P4(l)
        P.barrier()
        layer_P4s(l)
        P.barrier()
        layer_P5(l, PT + [16])
        P.barrier()
        layer_P6(l, PT + [16])
        P.barrier()
    write_y()
    P.finish(es)
    return nc, P, es


def _perm_cols():
    idx = []
    for base in (0, 640):
        for r in range(2):
            for g in range(2):
                h = 2 * g + r
                idx += list(range(base + h * 64, base + (h + 1) * 64))
    return idx


def _w_in_perm():
    ref = dict(aq=0, ak=256, av=384, cq=512, ck=768, cv=896, z=1024, xbc=1536, dt=2560)
    idx = []
    for nm in ("aq", "cq"):
        for r in range(2):
            for g in range(2):
                h = 2 * g + r
                idx += list(range(ref[nm] + h * 64, ref[nm] + (h + 1) * 64))
    idx += list(range(ref["ak"], ref["ak"] + 128))
    idx += list(range(ref["ck"], ref["ck"] + 128))
    idx += list(range(ref["av"], ref["av"] + 128))
    idx += list(range(ref["cv"], ref["cv"] + 128))
    idx += list(range(ref["z"], ref["z"] + 512))
    idx += list(range(ref["dt"], ref["dt"] + 8))
    idx += list(range(ref["xbc"], ref["xbc"] + 1024))
    assert len(idx) == NIN
    return np.array(idx)


def _consts(c):
    cf = np.zeros((128, 1344), np.float32)
    i = np.arange(128)
    cf[:, 0:128] = np.eye(128, dtype=np.float32)
    cf[:, 128:256] = (i[:, None] <= i[None, :])
    cf[:, 256:384] = (i[:, None] > i[None, :])
    cf[:, 384:512] = 1.0
    half = 8
    inv = (np.float32(500000.0) ** (-(np.arange(half, dtype=np.float32) * np.float32(2.0) / np.float32(16)))).astype(np.float32)
    pos = np.zeros((128, 17), np.float32)
    for t in range(16):
        pos[:, t] = c * TOK + t * 128 + i
    pos[:, 16] = 16384 + (i % 64) // 16
    ang = pos[:, :, None] * inv[None, None, :]
    cf[:, 512:648] = np.cos(ang).astype(np.float32).reshape(128, 136)
    cf[:, 648:784] = np.sin(ang).astype(np.float32).reshape(128, 136)
    co, si = np.cos(ang).astype(np.float32), np.sin(ang).astype(np.float32)
    cf[:, 800:1072] = np.concatenate([co, si], axis=2).reshape(128, 272)
    cf[:, 1072:1344] = np.concatenate([si, co], axis=2).reshape(128, 272)
    cf[:, 784] = 1.0 if c > 0 else 0.0
    cf[:, 785:793] = (np.arange(8)[None, :] < c).astype(np.float32)
    cb = np.zeros((128, 2304), np.float32)
    cb[:, 0:128] = np.eye(128)
    cur = (i[:, None] <= i[None, :]).astype(np.float32)
    prevA = (i[:, None] >= i[None, :] + 1).astype(np.float32)
    prevC = (i[:, None] >= i[None, :]).astype(np.float32)
    pvf = 1.0 if c > 0 else 0.0
    cb[:, 128:640] = np.concatenate([prevA * pvf, prevA * pvf, cur, cur], axis=1)
    cb[:, 640:1152] = np.concatenate([prevA, prevA, cur, cur], axis=1)
    cb[:, 1152:1664] = np.concatenate([prevC * pvf, prevC * pvf, cur, cur], axis=1)
    cb[:, 1664:2176] = np.concatenate([prevC, prevC, cur, cur], axis=1)
    cb[:, 2176:2304] = cur
    return cf, cb.astype(ml_dtypes.bfloat16)


def _params(inp):
    par = np.zeros((DEPTH, 128, 3456), np.float32)
    p = np.arange(128)
    for l in range(DEPTH):
        a = par[l]
        a[:, 0:1024] = inp["norm1"][l][None, :]
        a[:, 1024:2048] = inp["norm2"][l][None, :]
        a[:, 2048:2560] = np.concatenate([np.tile(inp["a_qn"][l], 4), np.tile(inp["c_qn"][l], 4)])[None, :]
        a[:, 2560:2816] = np.concatenate([np.tile(inp["a_kn"][l], 2), np.tile(inp["c_kn"][l], 2)])[None, :]
        a[:, 2816:3328] = inp["ssm_norm"][l][None, :]
        a[:, 3328:3332] = inp["a_sinks"][l][None, :]
        a[:, 3332:3340] = inp["dt_bias"][l][None, :]
        a[:, 3340:3348] = inp["a_log"][l][None, :]
        a[:, 3348:3356] = inp["d_skip"][l][None, :]
        cw = inp["conv_w"][l].reshape(4, 8, 128)
        a[:, 3356:3388] = cw.transpose(2, 1, 0).reshape(128, 32)
        a[:, 3388:3396] = inp["conv_b"][l].reshape(8, 128).T
        a[:, 3400:3402] = inp["a_sinks"][l].reshape(2, 2)[p // 64]
        a[:, 3402] = inp["a_log"][l][p // 16]
        a[:, 3403] = inp["d_skip"][l][p // 16]
        a[:, 3404] = inp["dt_bias"][l][p // 16]
    return par


_CACHE = {}


def kernel(**inp):
    inp = {k: np.asarray(v) for k, v in inp.items()}
    if "nc" not in _CACHE:
        _CACHE["nc"] = build()
    nc, P, es = _CACHE["nc"]
    perm = _w_in_perm()
    w_in_p = np.ascontiguousarray(inp["w_in"][:, :, perm])
    par = _params(inp)
    in_maps = []
    for c in range(NCORES):
        cf, cb = _consts(c)
        bs = slice(c * SB, (c + 1) * SB)
        m = {
            "xp": np.ascontiguousarray(inp["x_prompt"][0, c * TOK:(c + 1) * TOK]),
            "xs": np.ascontiguousarray(inp["x_sample"][bs].transpose(1, 0, 2).reshape(64, D)),
            "cak": np.ascontiguousarray(inp["cache_a_k"][:, bs].reshape(DEPTH, SB, 128, 128)),
            "cav": np.ascontiguousarray(inp["cache_a_v"][:, bs].reshape(DEPTH, SB, 128, 128)),
            "cck": np.ascontiguousarray(inp["cache_c_k"][:, bs].reshape(DEPTH, SB, 2048, 128)),
            "ccv": np.ascontiguousarray(inp["cache_c_v"][:, bs].reshape(DEPTH, SB, 2048, 128)),
            "sssm": np.ascontiguousarray(inp["state_ssm"][:, bs].reshape(DEPTH, SB, 8, 8192)),
            "sconv": np.ascontiguousarray(inp["state_conv"][:, bs].reshape(DEPTH, SB * 3, 1024)),
            "w_in": w_in_p, "w_out": inp["w_out"], "w_gate": inp["w_gate"], "w_up": inp["w_up"],
            "w_down": inp["w_down"], "par": par, "cst_f": cf, "cst_b": cb,
        }
        in_maps.append(m)
    res = run_bass_kernel_spmd(nc, in_maps, core_ids=list(range(NCORES)))
    R = res.results
    _CACHE["last"] = R
    y_prompt = np.concatenate([R[c]["yp"] for c in range(NCORES)], axis=0)[None]
    y_sample = np.concatenate([R[c]["ys"].reshape(4, SB, D).transpose(1, 0, 2) for c in range(NCORES)], axis=0)
    L = NCORES - 1
    p_a_k = R[L]["pak"].reshape(DEPTH, 1, 128, 2, 64)
    p_a_v = R[L]["pav"].reshape(DEPTH, 1, 128, 2, 64)
    p_c_k = R[L]["pck"].reshape(DEPTH, 1, 2048, 2, 64)
    p_c_v = R[L]["pcv"].reshape(DEPTH, 1, 2048, 2, 64)
    p_ssm = R[L]["pssm"].reshape(DEPTH, 1, 8, 64, 128)
    p_conv = R[L]["pconv"].reshape(DEPTH, 1, 3, 1024)

    def cat(name, shape):
        return np.concatenate([R[c][name].reshape((DEPTH, SB) + shape) for c in range(NCORES)], axis=1)

    s_a_k = cat("sak", (128, 2, 64))
    s_a_v = cat("sav", (128, 2, 64))
    s_c_k = cat("sck", (2048, 2, 64))
    s_c_v = cat("scv", (2048, 2, 64))
    s_ssm = cat("sssm_o", (8, 64, 128))
    s_conv = cat("sconv_o", (3, 1024))
    return (y_prompt, y_sample, p_a_k, p_a_v, p_c_k, p_c_v, p_ssm, p_conv,
            s_a_k, s_a_v, s_c_k, s_c_v, s_ssm, s_conv)
```
